# Optimizing a Trainium2 kernel written in Bass

```python
import math
import jax
import jax.numpy as jnp
from jax import lax
import numpy as np

D_MODEL = 1024
BATCH = 2
SEQ = 8192
DEPTH = 4

MIX_W = 256
N_BRANCH = 4
GM_CHUNK = 128
GM_GROUPS = 4
GM_GD = MIX_W // GM_GROUPS
ATT_HEADS = 4
ATT_HD = MIX_W // ATT_HEADS
DIL_PATTERNS = ((128, 1), (512, 4), (2048, 16))
ATT_BLOCK = 64
REL_BUCKETS = 32
REL_MAX_DIST = 1024
ML_HEADS = 4
ML_HD = MIX_W // ML_HEADS
ML_CHUNK = 64
ML_CONV = 3
POOL_WINDOWS = (2, 4, 8, 16)
POOL_GD = MIX_W // len(POOL_WINDOWS)
N_EXPERTS = 16
EXPERT_FF = 1024
EC_FACTOR = 2
DEEPNORM_ALPHA = (2 * DEPTH) ** 0.25
DEEPNORM_BETA = (8 * DEPTH) ** -0.25
LN_EPS = 1e-5
IN_SIZES = (MIX_W, MIX_W,
            3 * MIX_W,
            4 * MIX_W,
            2 * ML_HEADS, 2 * ML_HEADS,
            MIX_W,
            N_BRANCH * D_MODEL)
N_IN = 10 * MIX_W + 4 * ML_HEADS + N_BRANCH * D_MODEL

kernel_name = 'hybrid_gated_mixers_ec_moe_encoder'


def _standardize(x):
    xf = x.astype(jnp.float32)
    mu = jnp.mean(xf, axis=-1, keepdims=True)
    var = jnp.mean(jnp.square(xf - mu), axis=-1, keepdims=True)
    return (xf - mu) * lax.rsqrt(var + LN_EPS)


def layer_norm(x, g, b):
    return (_standardize(x) * g + b).astype(x.dtype)


def gmlp_spatial_gate(u, v, ln_g, ws, bs):
    B, S, _ = u.shape
    vn = (_standardize(v) * ln_g).astype(v.dtype)
    vc = vn.reshape(B, S // GM_CHUNK, GM_CHUNK, GM_GROUPS, GM_GD)
    mixed = jnp.einsum('gpq,bcqgd->bcpgd', ws, vc) + jnp.transpose(bs)[None, None, :, :, None]
    return u * mixed.reshape(B, S, MIX_W)


def t5_bucket(rel):
    half = REL_BUCKETS // 2
    max_exact = half // 2
    ret = jnp.where(rel > 0, half, 0)
    n = jnp.abs(rel)
    nf = jnp.maximum(n, 1).astype(jnp.float32)
    large = max_exact + (jnp.log(nf / max_exact) / math.log(REL_MAX_DIST / max_exact)
                         * (half - max_exact)).astype(jnp.int32)
    large = jnp.minimum(large, half - 1)
    return ret + jnp.where(n < max_exact, n, large)


def dilated_window_attention(q, k, v, rel_bias, window, dil):
    B, S, H, Dh = q.shape
    side = (window // 2) // dil
    L = S // dil
    nb = -(-L // ATT_BLOCK)
    Lp = nb * ATT_BLOCK

    def to_blocks(t):
        t = t.reshape(B, L, dil, H, Dh).transpose(0, 2, 3, 1, 4)
        t = jnp.pad(t, ((0, 0), (0, 0), (0, 0), (0, Lp - L), (0, 0)))
        return t.reshape(B, dil, H, nb, ATT_BLOCK, Dh)

    def band(t):
        tp = jnp.pad(t, ((0, 0), (0, 0), (0, 0), (1, 1), (0, 0), (0, 0)))
        return jnp.concatenate([tp[:, :, :, :-2], tp[:, :, :, 1:-1], tp[:, :, :, 2:]], axis=4)

    qb = to_blocks(q)
    kw = band(to_blocks(k))
    vw = band(to_blocks(v))
    rel_local = jnp.arange(3 * ATT_BLOCK)[None, :] - ATT_BLOCK - jnp.arange(ATT_BLOCK)[:, None]
    bias = jnp.transpose(rel_bias[t5_bucket(dil * rel_local)], (2, 0, 1))
    kpos = (jnp.arange(nb)[:, None] - 1) * ATT_BLOCK + jnp.arange(3 * ATT_BLOCK)[None, :]
    valid = (jnp.abs(rel_local) <= side)[None] & ((kpos >= 0) & (kpos < L))[:, None, :]
    logits = (jnp.einsum('brhnqd,brhnkd->brhnqk', qb, kw).astype(jnp.float32) * ATT_HD ** -0.5
              + bias[:, None].astype(jnp.float32))
    logits = jnp.where(valid, logits, -1e30)
    lse = jax.nn.logsumexp(logits, axis=-1)
    p = jnp.exp(logits - lse[..., None])
    o = jnp.einsum('brhnqk,brhnkd->brhnqd', p.astype(v.dtype), vw)
    o = o.reshape(B, dil, H, Lp, Dh)[:, :, :, :L].transpose(0, 3, 1, 2, 4).reshape(B, S, H, Dh)
    lse = lse.reshape(B, dil, H, Lp)[..., :L].transpose(0, 3, 1, 2).reshape(B, S, H)
    return o, lse


def dilated_mixture_attention(q, k, v, rel_bias):
    outs, lses = [], []
    for window, dil in DIL_PATTERNS:
        o, l = dilated_window_attention(q, k, v, rel_bias, window, dil)
        outs.append(o)
        lses.append(l)
    wts = jax.nn.softmax(jnp.stack(lses, axis=0), axis=0)
    return jnp.einsum('pbsh,pbshd->bshd', wts.astype(q.dtype), jnp.stack(outs, axis=0))


def depthwise_conv(x, w):
    K, C = w.shape
    pad = K // 2
    return lax.conv_general_dilated(x, w[:, None, :], window_strides=(1,),
                                    padding=[(pad, K - 1 - pad)],
                                    dimension_numbers=('NWC', 'WIO', 'NWC'),
                                    feature_group_count=C)


def mlstm_scan(q, k, v, li, lf):
    B, H, S, Dh = q.shape
    nc = S // ML_CHUNK

    def chunks(t):
        return jnp.moveaxis(t.reshape(B, H, nc, ML_CHUNK, *t.shape[3:]), 2, 0)

    tri = jnp.tril(jnp.ones((ML_CHUNK, ML_CHUNK), dtype=bool))

    def step(carry, inp):
        C, n, m = carry
        qc, kc, vc, lic, lfc = inp
        b = jnp.cumsum(lfc, axis=-1)
        D = jnp.where(tri, b[..., :, None] - b[..., None, :] + lic[..., None, :], -jnp.inf)
        m_inter = b + m[..., None]
        m_t = jnp.maximum(jnp.max(D, axis=-1), m_inter)
        inter_w = jnp.exp(m_inter - m_t)
        s = jnp.einsum('bhtd,bhsd->bhts', qc, kc) * jnp.exp(D - m_t[..., None])
        num = (jnp.einsum('bhts,bhsd->bhtd', s, vc)
               + inter_w[..., None] * jnp.einsum('bhvk,bhtk->bhtv', C, qc))
        den = jnp.sum(s, axis=-1) + inter_w * jnp.einsum('bhk,bhtk->bht', n, qc)
        h = num / jnp.maximum(jnp.abs(den), jnp.exp(-m_t))[..., None]
        g = b[..., -1]
        w_s = g[..., None] - b + lic
        m_new = jnp.maximum(g + m, jnp.max(w_s, axis=-1))
        decay = jnp.exp(g + m - m_new)
        ws = jnp.exp(w_s - m_new[..., None])
        C = decay[..., None, None] * C + jnp.einsum('bhs,bhsv,bhsk->bhvk', ws, vc, kc)
        n = decay[..., None] * n + jnp.einsum('bhs,bhsk->bhk', ws, kc)
        return (C, n, m_new), h

    init = (jnp.zeros((B, H, Dh, Dh), jnp.float32), jnp.zeros((B, H, Dh), jnp.float32),
            jnp.zeros((B, H), jnp.float32))
    _, hs = lax.scan(step, init, (chunks(q), chunks(k), chunks(v), chunks(li), chunks(lf)))
    return jnp.moveaxis(hs, 0, 2).reshape(B, H, S, Dh)


def bidirectional_mlstm(q, k, v, li, lf):
    flip = lambda t: jnp.flip(t, axis=2)
    fwd = mlstm_scan(q, k, v, li[0], lf[0])
    bwd = flip(mlstm_scan(flip(q), flip(k), flip(v), flip(li[1]), flip(lf[1])))
    return fwd + bwd


def head_norm(h, g):
    B, H, S, Dh = h.shape
    mu = jnp.mean(h, axis=-1, keepdims=True)
    var = jnp.mean(jnp.square(h - mu), axis=-1, keepdims=True)
    hn = (h - mu) * lax.rsqrt(var + LN_EPS)
    return hn.transpose(0, 2, 1, 3).reshape(B, S, H * Dh) * g


def pool_mixer(xd, pool_w, pool_scale):
    B, S, _ = xd.shape
    xg = xd.reshape(B, S, len(POOL_WINDOWS), POOL_GD).astype(jnp.float32)
    cs = jnp.pad(jnp.cumsum(xg, axis=1), ((0, 0), (1, 0), (0, 0), (0, 0)))
    pos = jnp.arange(S)
    outs = []
    for gi, win in enumerate(POOL_WINDOWS):
        lo = jnp.clip(pos - win // 2, 0, S)
        hi = jnp.clip(pos + win // 2, 0, S)
        cg = cs[:, :, gi]
        mean = (cg[:, hi] - cg[:, lo]) / (hi - lo).astype(jnp.float32)[None, :, None]
        outs.append(mean - xg[:, :, gi])
    pooled = jnp.stack(outs, axis=2).astype(xd.dtype)
    mixed = jnp.einsum('bsgi,gio->bsgo', pooled, pool_w)
    return mixed.reshape(B, S, MIX_W) * pool_scale


def mixer_sublayer(x, w_in, b_in, gm_ln_g, gm_ws, gm_bs, rel_bias, ml_conv, ml_fbias,
                   ml_norm_g, pool_w, pool_scale, w_branch, w_out):
    B, S, _ = x.shape
    h = jnp.einsum('bsd,dn->bsn', x, w_in) + b_in
    split_points = np.cumsum(IN_SIZES)[:-1].tolist()
    a_u, a_v, b_qkv, c_qkvo, c_ig, c_fg, d_x, gate_raw = jnp.split(h, split_points, axis=-1)

    y_a = gmlp_spatial_gate(jax.nn.gelu(a_u), jax.nn.gelu(a_v), gm_ln_g, gm_ws, gm_bs)

    qkv = b_qkv.reshape(B, S, 3, ATT_HEADS, ATT_HD)
    y_b = dilated_mixture_attention(qkv[:, :, 0], qkv[:, :, 1], qkv[:, :, 2], rel_bias)
    y_b = y_b.reshape(B, S, MIX_W)

    c_qk = jax.nn.silu(depthwise_conv(c_qkvo[..., :2 * MIX_W], ml_conv))
    heads = lambda t: t.reshape(B, S, ML_HEADS, ML_HD).transpose(0, 2, 1, 3).astype(jnp.float32)
    c_q = heads(c_qk[..., :MIX_W])
    c_k = heads(c_qk[..., MIX_W:]) * ML_HD ** -0.5
    c_v = heads(c_qkvo[..., 2 * MIX_W:3 * MIX_W])
    c_o = c_qkvo[..., 3 * MIX_W:]
    li = c_ig.reshape(B, S, 2, ML_HEADS).astype(jnp.float32).transpose(2, 0, 3, 1)
    lf = jax.nn.log_sigmoid(c_fg.reshape(B, S, 2, ML_HEADS).astype(jnp.float32)
                            + ml_fbias).transpose(2, 0, 3, 1)
    y_c = head_norm(bidirectional_mlstm(c_q, c_k, c_v, li, lf), ml_norm_g)
    y_c = (jax.nn.sigmoid(c_o.astype(jnp.float32)) * y_c).astype(x.dtype)

    y_d = pool_mixer(d_x, pool_w, pool_scale)

    ys = jnp.stack([y_a, y_b.astype(x.dtype), y_c, y_d.astype(x.dtype)], axis=2)
    proj = jnp.einsum('bsnc,ncd->bsnd', ys, w_branch)
    gates = jax.nn.sigmoid(gate_raw.reshape(B, S, N_BRANCH, D_MODEL))
    merged = jnp.einsum('bsnd,bsnd->bsd', gates, proj)
    return jnp.einsum('bsd,de->bse', merged, w_out)


def expert_choice_ffn(x, w_router, w1, w3, w2):
    B, T, D = x.shape
    cap = EC_FACTOR * T // N_EXPERTS
    aff = jax.nn.softmax(jnp.einsum('btd,de->bte', x, w_router).astype(jnp.float32), axis=-1)
    gate, idx = lax.top_k(jnp.swapaxes(aff, 1, 2), cap)
    xs = jax.vmap(lambda xb, ib: xb[ib])(x, idx)
    hid = jax.nn.silu(jnp.einsum('becd,edf->becf', xs, w1)) * jnp.einsum('becd,edf->becf', xs, w3)
    ye = jnp.einsum('becf,efd->becd', hid, w2) * gate[..., None].astype(x.dtype)
    scatter = lambda yb, ib: jnp.zeros((T, D), yb.dtype).at[ib.reshape(-1)].add(yb.reshape(-1, D))
    return jax.vmap(scatter)(ye, idx)


def setup_inputs(seed: int = 0) -> dict:
    key = jax.random.key(seed)
    ks = jax.random.split(key, 24)
    nrm = lambda k, shape, scale: jax.random.normal(k, shape, jnp.float32) * scale
    L = DEPTH
    return {
        'x': nrm(ks[0], (BATCH, SEQ, D_MODEL), 1.0),
        'w_in': nrm(ks[1], (L, D_MODEL, N_IN), D_MODEL ** -0.5),
        'b_in': nrm(ks[2], (L, N_IN), 0.02),
        'gm_ln_g': 1.0 + nrm(ks[3], (L, MIX_W), 0.02),
        'gm_ws': nrm(ks[4], (L, GM_GROUPS, GM_CHUNK, GM_CHUNK), GM_CHUNK ** -0.5),
        'gm_bs': 1.0 + nrm(ks[5], (L, GM_GROUPS, GM_CHUNK), 0.02),
        'rel_bias': nrm(ks[6], (REL_BUCKETS, ATT_HEADS), 0.3),
        'ml_conv': nrm(ks[7], (L, ML_CONV, 2 * MIX_W), ML_CONV ** -0.5),
        'ml_fbias': jnp.linspace(3.0, 6.0, ML_HEADS, dtype=jnp.float32) + nrm(ks[8], (L, 2, ML_HEADS), 0.1),
        'ml_norm_g': 1.0 + nrm(ks[9], (L, MIX_W), 0.02),
        'pool_w': nrm(ks[10], (L, len(POOL_WINDOWS), POOL_GD, POOL_GD), POOL_GD ** -0.5),
        'pool_scale': 1.0 + nrm(ks[11], (L, MIX_W), 0.02),
        'w_branch': nrm(ks[12], (L, N_BRANCH, MIX_W, D_MODEL), MIX_W ** -0.5),
        'w_out': nrm(ks[13], (L, D_MODEL, D_MODEL), D_MODEL ** -0.5 * DEEPNORM_BETA),
        'ln1_g': 1.0 + nrm(ks[14], (L, D_MODEL), 0.02),
        'ln1_b': nrm(ks[15], (L, D_MODEL), 0.02),
        'w_router': nrm(ks[16], (L, D_MODEL, N_EXPERTS), D_MODEL ** -0.5),
        'w_e1': nrm(ks[17], (L, N_EXPERTS, D_MODEL, EXPERT_FF), D_MODEL ** -0.5),
        'w_e3': nrm(ks[18], (L, N_EXPERTS, D_MODEL, EXPERT_FF), D_MODEL ** -0.5),
        'w_e2': nrm(ks[19], (L, N_EXPERTS, EXPERT_FF, D_MODEL), EXPERT_FF ** -0.5 * DEEPNORM_BETA),
        'ln2_g': 1.0 + nrm(ks[20], (L, D_MODEL), 0.02),
        'ln2_b': nrm(ks[21], (L, D_MODEL), 0.02),
    }


def reference(x, w_in, b_in, gm_ln_g, gm_ws, gm_bs, rel_bias, ml_conv, ml_fbias, ml_norm_g,
              pool_w, pool_scale, w_branch, w_out, ln1_g, ln1_b, w_router, w_e1, w_e3, w_e2,
              ln2_g, ln2_b):
    for l in range(DEPTH):
        mix = mixer_sublayer(x, w_in[l], b_in[l], gm_ln_g[l], gm_ws[l], gm_bs[l], rel_bias,
                             ml_conv[l], ml_fbias[l], ml_norm_g[l], pool_w[l], pool_scale[l],
                             w_branch[l], w_out[l])
        x = layer_norm(DEEPNORM_ALPHA * x + mix, ln1_g[l], ln1_b[l])
        ffn = expert_choice_ffn(x, w_router[l], w_e1[l], w_e3[l], w_e2[l])
        x = layer_norm(DEEPNORM_ALPHA * x + ffn, ln2_g[l], ln2_b[l])
    return x
```

```python
import math
import os
from contextlib import ExitStack

import numpy as np
import ml_dtypes

import concourse.bass as bass
import concourse.mybir as mybir
from concourse.bass_utils import run_bass_kernel_spmd

F32 = mybir.dt.float32
BF16 = mybir.dt.bfloat16
I32 = mybir.dt.int32
AF = mybir.ActivationFunctionType
ALU = mybir.AluOpType

D = 1024
S = 8192
T = 2048
DEPTH = 4
NCORE = 8
ALPHA = (2 * DEPTH) ** 0.25
EPS = 1e-5
NEG = -30000.0
WROWS = 13824
HALO = 1024
CW = S + 2 * HALO
DILS = (1, 4, 16)


class Sched:
    def __init__(self, nc, stack):
        self.nc = nc
        self.stack = stack
        self.eng = {'sp': nc.sync, 'act': nc.scalar, 'pe': nc.tensor, 'dve': nc.vector, 'pool': nc.gpsimd}
        self.esem = {}
        self.eseq = {}
        for e in self.eng:
            self.esem[e] = stack.enter_context(nc.semaphore("es_" + e))
            self.eseq[e] = 0
        self.dsem = {}
        self.lastw = {}
        self.readers = {}
        self.waited = {e: {} for e in self.eng}
        self.sig = []
        self.nsem = 0
        self.bgkeys = set()
        self.marker = stack.enter_context(nc.sbuf_tensor("cc_marker", [1, 4], F32))

    def _dsem(self, key):
        if key not in self.dsem:
            self.nsem += 1
            self.dsem[key] = [self.stack.enter_context(self.nc.semaphore("ds%d" % self.nsem)), 0]
        return self.dsem[key]

    def _wait(self, eng, sem, val):
        w = self.waited[eng]
        k = id(sem)
        if w.get(k, (None, 0))[1] >= val:
            return
        w[k] = (sem, val)
        self.eng[eng].wait_ge(sem, val)

    def wait_bg(self, eng, key):
        for sw in (False, True):
            if (key, sw) in self.dsem:
                sem, cnt = self.dsem[(key, sw)]
                self._wait(eng, sem, cnt)

    def add(self, eng, fn, r=(), w=(), dma=None, inc=16, bg=False, cc=False):
        if dma is not None:
            dma = (dma, eng == 'pool')
        if bg:
            self.bgkeys.add(dma)
        deps = set()
        for k in r:
            if k in self.lastw:
                deps.add(self.lastw[k])
        for k in w:
            if k in self.lastw:
                deps.add(self.lastw[k])
            rd = self.readers.get(k)
            if rd:
                deps.update(rd[0].values())
                deps.update(rd[1])
        need = {}
        for d in deps:
            sem, val, deng, isdma = self.sig[d][:4]
            if (not isdma) and deng == eng and eng == 'pe':
                continue
            if isdma and len(self.sig[d]) > 4:
                val = self.sig[d][4][1]
            k = id(sem)
            if k not in need or need[k][1] < val:
                need[k] = (sem, val)
        for sem, val in need.values():
            self._wait(eng, sem, val)
        ins = fn(self.eng[eng])
        idx = len(self.sig)
        if cc:
            ds = self._dsem(dma)
            self.bgkeys.add(dma)
            ds[1] += 1
            ins.then_inc(ds[0])
            if not bg:
                self._wait('pool', ds[0], ds[1])
                mk = self.eng['pool'].memset(self.marker[0:1, 0:1], 0.0)
                self.eseq['pool'] += 1
                mk.then_inc(self.esem['pool'], 1)
                self.sig.append((self.esem['pool'], self.eseq['pool'], 'pool', False))
            else:
                self.sig.append((ds[0], ds[1], eng, True))
        elif dma is None:
            self.eseq[eng] += 1
            sem = self.esem[eng]
            ins.then_inc(sem, 1)
            self.sig.append((sem, self.eseq[eng], eng, False))
        else:
            ds = self._dsem(dma)
            ds[1] += inc
            if inc == 1:
                ins.then_inc(ds[0])
            else:
                ins.then_inc(ds[0], inc)
            self.sig.append((ds[0], ds[1], eng, True, ds))
        for k in w:
            self.lastw[k] = idx
            self.readers[k] = [{}, []]
        for k in r:
            rd = self.readers.setdefault(k, [{}, []])
            if dma is None:
                rd[0][eng] = idx
            else:
                rd[1].append(idx)
        return idx

    def barrier(self):
        for e in self.eng:
            for e2 in self.eng:
                if self.eseq[e2] > 0:
                    self._wait(e, self.esem[e2], self.eseq[e2])
            for k, (sem, cnt) in self.dsem.items():
                if cnt > 0 and k not in self.bgkeys:
                    self._wait(e, sem, cnt)
        self.lastw.clear()
        self.readers.clear()


def build_program(depth=DEPTH, debug=None):
    nc = bass.Bass("TRN2", target_bir_lowering=False)
    L = depth

    def din(name, shape, dt=F32):
        return nc.dram_tensor(name, list(shape), dt, kind="ExternalInput").ap()

    xT_in = din("xT", [D, T])
    whf = din("whf", [L, D, 384])
    bhf = din("bhf", [L, 384])
    wht = din("wht", [L, D, 452])
    bht = din("bht", [L, 452])
    fbias = din("fbias", [L, 2])
    wsT = din("wsT", [L, 128, 128])
    bsv = din("bsv", [L, 128])
    lng = din("lng", [L, 64])
    bmat = din("bmat", [6, 128, 128])
    cwq = din("cwq", [L, 64, 3])
    cwk = din("cwk", [L, 64, 3])
    ngv = din("ngv", [L, 64])
    bands = din("bands", [5, 128, 128])
    pw = din("pw", [L, 64, 64])
    psc = din("psc", [L, 64])
    idxy = din("idxy", [128, 8], I32)
    wpack = din("wpack", [L, WROWS, D])
    bg = din("bg", [L, 4096])
    ln1g = din("ln1g", [L, D])
    ln1b = din("ln1b", [L, D])
    wr = din("wr", [L, D, 16])
    ln2g = din("ln2g", [L, D])
    ln2b = din("ln2b", [L, D])
    consts = din("consts", [8, 128, 128])
    out = nc.dram_tensor("out", [T, D], F32, kind="ExternalOutput").ap()
    dbg = None
    if debug is not None:
        dbg = nc.dram_tensor("dbg", list(debug[1]), debug[2], kind="ExternalOutput").ap()

    snd_x = nc.dram_tensor("snd_x", [D, T], BF16)
    rcv_x = [nc.dram_tensor("rcv_x%d" % i, [4 * 256, T], BF16) for i in range(4)]
    snd_y = nc.dram_tensor("snd_y", [4 * 256, T], BF16)
    rcv_y = nc.dram_tensor("rcv_y", [4 * 1024, T], BF16)
    snd_a = nc.dram_tensor("snd_a", [16, T], F32)
    rcv_a = nc.dram_tensor("rcv_a", [64, T], F32)
    xres = nc.dram_tensor("xres", [D, T], F32)
    CHR = 512
    NCH = WROWS // CHR
    snd_w = [nc.dram_tensor("snd_w%d" % l, [WROWS, D], BF16) for l in range(L)]
    rcv_w = [[nc.dram_tensor("rcv_w%d_%d" % (l, i), [4 * CHR, D], BF16) for i in range(NCH)] for l in range(L)]
    G4 = [[0, 1, 2, 3], [4, 5, 6, 7]]
    G8 = [list(range(8))]

    with ExitStack() as top:
        sc = Sched(nc, top)
        A = sc.add

        def rstd_op(out_ap, in_ap, rk, wk):
            A('act', lambda e: e.activation(out=out_ap, in_=in_ap, func=AF.Sqrt, bias=EPS), r=rk, w=wk)
            A('dve', lambda e: e.reciprocal(out=out_ap, in_=out_ap), r=wk, w=wk)

        uniq = [0]

        def sb(stack, name, shape, dt):
            uniq[0] += 1
            return stack.enter_context(nc.sbuf_tensor("%s_%d" % (name, uniq[0]), list(shape), dt))

        psb = [top.enter_context(nc.psum_tensor("psb%d" % i, [128, 512], F32)) for i in range(7)]
        pst = top.enter_context(nc.psum_tensor("pst", [128, 1024], BF16))
        pscount = [0]

        def nextps():
            i = pscount[0] % 7
            pscount[0] += 1
            return psb[i], ("ps", i)

        cst = sb(top, "cst", [128, 8, 128], F32)
        A('sp', lambda e: e.dma_start(out=cst[:], in_=consts.rearrange("k p f -> p k f")), w=["cst"], dma="cst")
        ident_f = cst[:, 0, :]
        tri = [cst[:, 1, :], cst[:, 2, :]]
        mneg = [cst[:, 3, :], cst[:, 4, :]]
        blk = cst[:, 5, :]
        ones_f = cst[:, 6, :]
        kb = cst[:, 7, 0:3]
        cstb = sb(top, "cstb", [128, 2, 128], BF16)
        A('dve', lambda e: e.tensor_copy(out=cstb[:, 0, :], in_=cst[:, 0, :]), r=["cst"], w=["cstb0"])
        A('dve', lambda e: e.tensor_copy(out=cstb[:, 1, :], in_=cst[:, 6, :]), r=["cst"], w=["cstb1"])
        ident_b = cstb[:, 0, :]
        ones_b = cstb[:, 1, :]
        CK = ["cst", "cstb0", "cstb1"]

        if debug is not None and debug[0] == "tm1":
            sc.barrier()
            A('sp', lambda e: e.dma_start(out=dbg, in_=cstb[:, 0, :]), dma="dbg")
            sc.barrier()
            return nc
        def bounce_weights(l):
            for i in range(8):
                A('pool', lambda e, i=i: e.dma_start(out=snd_w[l].ap()[i * 1728:(i + 1) * 1728, :], in_=wpack[l][i * 1728:(i + 1) * 1728, :]),
                  dma=("sndw", l), bg=True)

        def gather_weights(l):
            sc.wait_bg('pool', ("sndw", l))
            for i in range(NCH):
                A('pool', lambda e, i=i: e.collective_compute("AllGather", ALU.bypass, replica_groups=G4,
                                                              ins=[snd_w[l].ap()[i * CHR:(i + 1) * CHR, :].opt()],
                                                              outs=[rcv_w[l][i].ap().opt()]),
                  dma=("ccw", l), inc=1, bg=True, cc=True)

        def wrows(l, rank, row0, n=128):
            i = row0 // CHR
            assert (row0 % CHR) + n <= CHR
            sem, _ = sc.dsem[((("ccw", l)), True)]
            sc._wait('pool', sem, i + 1)
            o = rank * CHR + row0 % CHR
            return rcv_w[l][i].ap()[o:o + n, :]

        A('pool', lambda e: e.dma_start(out=snd_x.ap(), in_=xT_in), w=["snd_x"], dma="snd_x")

        if debug is not None and debug[0] == "tm2":
            sc.barrier()
            A('pool', lambda e: e.dma_start(out=dbg, in_=snd_x.ap()[0:128, 0:128]), dma="dbg")
            sc.barrier()
            return nc
        for l in range(L):
            xsrc = xT_in if l == 0 else xres.ap()
            last = (l == L - 1)
            for i in range(4):
                A('pool', lambda e, i=i: e.collective_compute("AllGather", ALU.bypass, replica_groups=G4,
                                                              ins=[snd_x.ap()[i * 256:(i + 1) * 256, :].opt()], outs=[rcv_x[i].ap().opt()]),
                  r=["snd_x"], w=[("rcv_x", i)], dma="ccx", inc=1, cc=True)

            if debug is not None and debug[0] == "t0":
                sc.barrier()
                A('pool', lambda e: e.dma_start(out=dbg, in_=rcv_x[0].ap()[0:128, 0:128]), dma="dbg")
                sc.barrier()
                return nc
            with ExitStack() as pb:
                whf_sb = sb(pb, "whf_sb", [128, 8, 384], BF16)
                wht_sb = sb(pb, "wht_sb", [128, 8, 452], BF16)
                bhf_sb = sb(pb, "bhf_sb", [128, 3], F32)
                bht_bc = sb(pb, "bht_bc", [128, 452], F32)
                fb_bc = sb(pb, "fb_bc", [128, 2], F32)
                wsT_sb = sb(pb, "wsT_sb", [128, 128], BF16)
                bs_bc = sb(pb, "bs_bc", [64, 128], F32)
                lng_sb = sb(pb, "lng_sb", [64, 1], F32)
                bands_sb = sb(pb, "bands_sb", [128, 5, 128], BF16)
                pw_sb = sb(pb, "pw_sb", [64, 64], BF16)
                psc_sb = sb(pb, "psc_sb", [64, 1], F32)
                B_sb = sb(pb, "B_sb", [128, 6, 128], BF16)
                cw_sb = sb(pb, "cw_sb", [64, 2, 3], F32)
                ng_bc = sb(pb, "ng_bc", [128, 64], F32)
                C0 = sb(pb, "C0", [128, S], BF16)
                C1 = sb(pb, "C1", [128, CW], BF16)
                C2 = sb(pb, "C2", [128, CW], BF16)
                vaug = sb(pb, "vaug", [128, 64, 65], BF16)
                co_sb = sb(pb, "co_sb", [128, 64, 64], BF16)
                co_f = sb(pb, "co_f", [128, 64, 64], F32)
                gates = sb(pb, "gates", [128, 64, 4], F32)
                dx_sb = sb(pb, "dx_sb", [128, 64, 64], BF16)

                A('pool', lambda e: e.dma_start(out=whf_sb[:], in_=whf[l].rearrange("(c p) n -> p c n", p=128)), w=["whf"], dma="setup")
                A('pool', lambda e: e.dma_start(out=wht_sb[:], in_=wht[l].rearrange("(c p) n -> p c n", p=128)), w=["wht"], dma="setup")
                with nc.allow_non_contiguous_dma(reason="tiny bias loads"):
                    A('sp', lambda e: e.dma_start(out=bhf_sb[:], in_=bhf[l].rearrange("(c p) -> p c", p=128)), w=["bhf"], dma="setup")
                    A('sp', lambda e: e.dma_start(out=lng_sb[:], in_=lng[l].rearrange("(p o) -> p o", o=1)), w=["lng"], dma="setup")
                    A('sp', lambda e: e.dma_start(out=psc_sb[:], in_=psc[l].rearrange("(p o) -> p o", o=1)), w=["psc"], dma="setup")
                A('sp', lambda e: e.dma_start(out=bht_bc[:], in_=bht[l].partition_broadcast(128)), w=["bht"], dma="setup")
                A('sp', lambda e: e.dma_start(out=fb_bc[:], in_=fbias[l].partition_broadcast(128)), w=["fb"], dma="setup")
                A('sp', lambda e: e.dma_start(out=bs_bc[:], in_=bsv[l].partition_broadcast(64)), w=["bs"], dma="setup")
                A('sp', lambda e: e.dma_start(out=ng_bc[:], in_=ngv[l].partition_broadcast(128)), w=["ng"], dma="setup")
                A('pool', lambda e: e.dma_start(out=wsT_sb[:], in_=wsT[l]), w=["wsT"], dma="setup")
                A('pool', lambda e: e.dma_start(out=bands_sb[:], in_=bands.rearrange("k p f -> p k f")), w=["bands"], dma="setup")
                A('pool', lambda e: e.dma_start(out=pw_sb[:], in_=pw[l]), w=["pw"], dma="setup")
                A('pool', lambda e: e.dma_start(out=B_sb[:], in_=bmat.rearrange("k p f -> p k f")), w=["B"], dma="setup")
                A('sp', lambda e: e.dma_start(out=cw_sb[:, 0, :], in_=cwq[l]), w=["cwq"], dma="setup")
                A('sp', lambda e: e.dma_start(out=cw_sb[:, 1, :], in_=cwk[l]), w=["cwk"], dma="setup")
                if debug is not None and debug[0] == "t1":
                    sc.barrier()
                    A('pool', lambda e: e.dma_start(out=dbg, in_=rcv_x[0].ap()[0:128, 0:128]), dma="dbg")
                    sc.barrier()
                    return nc
                A('pool', lambda e: e.memset(C1[:], 0.0), w=["C1"])
                A('pool', lambda e: e.memset(C2[:], 0.0), w=["C2"])
                A('pool', lambda e: e.memset(vaug[:], 1.0), w=["vaug"])
                if debug is not None and debug[0] == "s0":
                    sc.barrier()
                    A('pool', lambda e: e.dma_start(out=dbg, in_=rcv_x[0].ap()[0:128, 0:128]), dma="dbg")
                    sc.barrier()
                    return nc
                if l == 0:
                    bounce_weights(0)
                    gather_weights(0)
                if l + 1 < L:
                    bounce_weights(l + 1)
                    gather_weights(l + 1)
                sc.barrier()
                A('dve', lambda e: e.tensor_tensor(out=bht_bc[:, 385:388:2], in0=bht_bc[:, 385:388:2], in1=fb_bc[:], op=ALU.add),
                  r=["fb", "bht"], w=["bht"])
                sc.barrier()

                SKIPB = bool(os.environ.get("K_SKIPB"))
                if SKIPB:
                    for i4 in range(4):
                        A('pool', lambda e, i4=i4: e.dma_start(out=snd_y.ap()[i4 * 256:(i4 + 1) * 256, :], in_=xT_in[0:256, :]), dma="setup")
                    sc.barrier()
                if not SKIPB:
                    with ExitStack() as pj:
                        xblk = [sb(pj, "xblk%d" % i, [128, 8, 512], BF16) for i in range(2)]
                        ublk = [sb(pj, "ublk%d" % i, [64, 512], F32) for i in range(2)]
                        tm = [sb(pj, "tm%d" % i, [128, 452], F32) for i in range(4)]
                        vg = [sb(pj, "vg%d" % i, [128, 256], F32) for i in range(4)]
                        st6 = [sb(pj, "st6%d" % i, [128, 6], F32) for i in range(4)]
                        mv = [sb(pj, "mv%d" % i, [128, 2], F32) for i in range(4)]
                        rstd = [sb(pj, "rstd%d" % i, [128, 1], F32) for i in range(4)]
                        vhat = [sb(pj, "vhat%d" % i, [128, 64], BF16) for i in range(4)]
                        t1 = [sb(pj, "t1%d" % i, [64, 128], F32) for i in range(4)]
                        ya = [sb(pj, "ya%d" % i, [64, 128], BF16) for i in range(4)]
                        pooled = [sb(pj, "pooled%d" % i, [64, 128], BF16) for i in range(2)]
                        yd = [sb(pj, "yd%d" % i, [64, 128], BF16) for i in range(2)]

                        def pool_tile(i):
                            s2 = i % 2
                            ps, pk = nextps()
                            srcs = []
                            if i > 0:
                                srcs.append((i - 1, 0))
                            srcs.append((i, 3 if i == 0 else (4 if i == 63 else 1)))
                            if i < 63:
                                srcs.append((i + 1, 2))
                            for n_, (j_, kind) in enumerate(srcs):
                                A('pe', lambda e, j_=j_, kind=kind, n_=n_: e.matmul(ps[0:64, 0:128], dx_sb[:, j_, :], bands_sb[:, kind, :],
                                                                                      start=(n_ == 0), stop=(n_ == len(srcs) - 1)),
                                  r=[("dx", j_)], w=[pk])
                            A('dve', lambda e: e.tensor_copy(out=pooled[s2][:], in_=ps[0:64, 0:128]), r=[pk], w=[("pooled", s2)])
                            ps2, pk2 = nextps()
                            A('pe', lambda e: e.matmul(ps2[0:64, 0:128], pw_sb[:], pooled[s2][:], start=True, stop=True),
                              r=[("pooled", s2)], w=[pk2])
                            A('dve', lambda e: e.tensor_scalar_mul(out=yd[s2][:], in0=ps2[0:64, 0:128], scalar1=psc_sb[:, 0:1]),
                              r=[pk2], w=[("yd", s2)])
                            A('sp', lambda e: e.dma_start(out=snd_y.ap()[(i // 16) * 256 + 192:(i // 16) * 256 + 256, (i % 16) * 128:(i % 16 + 1) * 128], in_=yd[s2][:]),
                              r=[("yd", s2)], w=[("sndy", 3, i)], dma=("ydo", s2))

                        def stage1(i, s_, tt, xbk):
                            s2 = i % 4
                            ps, pk = nextps()
                            for k in range(8):
                                A('pe', lambda e, k=k: e.matmul(ps[:, 0:452], xblk[s_][:, k, tt * 128:(tt + 1) * 128], wht_sb[:, k, :],
                                                                  start=(k == 0), stop=(k == 7)),
                                  r=xbk, w=[pk])
                            A('dve', lambda e: e.tensor_tensor(out=tm[s2][:], in0=ps[:, 0:452], in1=bht_bc[:], op=ALU.add),
                              r=[pk], w=[("tm", s2)])
                            A('act', lambda e: e.activation(out=vg[s2][:], in_=tm[s2][:, 0:256], func=AF.Gelu), r=[("tm", s2)], w=[("vg", s2)])
                            A('pool', lambda e: e.tensor_copy(out=vaug[:, i, 0:64], in_=tm[s2][:, 256:320]), r=[("tm", s2)], w=[("vaug", i)])
                            A('pool', lambda e: e.tensor_copy(out=co_f[:, i, :], in_=tm[s2][:, 320:384]), r=[("tm", s2)], w=[("co", i)])
                            A('pool', lambda e: e.tensor_copy(out=gates[:, i, :], in_=tm[s2][:, 384:388]), r=[("tm", s2)], w=[("gates", i)])
                            A('pool', lambda e: e.tensor_copy(out=dx_sb[:, i, :], in_=tm[s2][:, 388:452]), r=[("tm", s2)], w=[("dx", i)])

                        def stage2(i):
                            s2 = i % 4
                            A('dve', lambda e: e.bn_stats(out=st6[s2][:], in_=vg[s2][:]), r=[("vg", s2)], w=[("st6", s2)])
                            A('dve', lambda e: e.bn_aggr(out=mv[s2][:], in_=st6[s2][:]), r=[("st6", s2)], w=[("mv", s2)])
                            rstd_op(rstd[s2][:], mv[s2][:, 1:2], [("mv", s2)], [("rstd", s2)])
                            A('dve', lambda e: e.tensor_scalar(out=vhat[s2][:], in0=vg[s2][:, 0:64], scalar1=mv[s2][:, 0:1],
                                                               scalar2=rstd[s2][:, 0:1], op0=ALU.subtract, op1=ALU.mult),
                              r=[("vg", s2), ("mv", s2), ("rstd", s2)], w=[("vhat", s2)])

                        def stage3(i):
                            s2 = i % 4
                            sb_ = (i // 4) % 2
                            tt = i % 4
                            psg, pkg = nextps()
                            A('pe', lambda e: e.matmul(psg[0:64, 0:128], vhat[s2][:], wsT_sb[:], start=True, stop=True),
                              r=[("vhat", s2)], w=[pkg])
                            A('dve', lambda e: e.scalar_tensor_tensor(out=t1[s2][:], in0=psg[0:64, 0:128], scalar=lng_sb[:, 0:1],
                                                                      in1=bs_bc[:], op0=ALU.mult, op1=ALU.add), r=[pkg], w=[("t1", s2)])
                            A('dve', lambda e: e.tensor_tensor(out=ya[s2][:], in0=t1[s2][:], in1=ublk[sb_][:, tt * 128:(tt + 1) * 128], op=ALU.mult),
                              r=[("t1", s2), ("ublk", sb_)], w=[("ya", s2)])
                            A('sp', lambda e: e.dma_start(out=snd_y.ap()[(i // 16) * 256:(i // 16) * 256 + 64, (i % 16) * 128:(i % 16 + 1) * 128], in_=ya[s2][:]),
                              r=[("ya", s2)], w=[("sndy", 0, i)], dma=("yao", s2 % 2))

                        for j in range(16):
                            s_ = j % 2
                            rk, off = j // 4, (j % 4) * 512
                            for i4 in range(4):
                                A('sp', lambda e, i4=i4: e.dma_start(out=xblk[s_][:, 2 * i4:2 * i4 + 2, :],
                                                                     in_=rcv_x[i4].ap()[rk * 256:(rk + 1) * 256, off:off + 512].rearrange("(c p) t -> p c t", p=128)),
                                  w=[("xblk", s_, i4)], dma=("xblk", s_))
                            xbk = [("xblk", s_, i4) for i4 in range(4)]
                            for ci in range(3):
                                ps, pk = nextps()
                                for k in range(8):
                                    A('pe', lambda e, k=k: e.matmul(ps[:, 0:512], whf_sb[:, k, ci * 128:(ci + 1) * 128], xblk[s_][:, k, :],
                                                                      start=(k == 0), stop=(k == 7)),
                                      r=xbk, w=[pk])
                                if ci == 0:
                                    A('act', lambda e: e.activation(out=ublk[s_][:], in_=ps[0:64, 0:512], func=AF.Gelu, bias=bhf_sb[0:64, 0:1]),
                                      r=[pk], w=[("ublk", s_)])
                                    A('dve', lambda e: e.tensor_scalar(out=C0[64:128, j * 512:(j + 1) * 512], in0=ps[64:128, 0:512],
                                                                       scalar1=bhf_sb[64:128, 0:1], scalar2=0.125, op0=ALU.add, op1=ALU.mult),
                                      r=[pk], w=[("C0", j)])
                                else:
                                    Cx = C1 if ci == 1 else C2
                                    A('dve', lambda e, Cx=Cx: e.tensor_scalar(out=Cx[:, HALO + j * 512:HALO + (j + 1) * 512], in0=ps[:, 0:512],
                                                                               scalar1=bhf_sb[:, ci:ci + 1], scalar2=None, op0=ALU.add),
                                      r=[pk], w=[("C%d" % ci, j)])
                            for tt in range(4):
                                i = 4 * j + tt
                                stage1(i, s_, tt, xbk)
                                if i >= 1:
                                    stage2(i - 1)
                                if i >= 2:
                                    stage3(i - 2)
                                if i >= 1:
                                    pool_tile(i - 1)
                        stage2(63)
                        stage3(62)
                        stage3(63)
                        pool_tile(63)
                        gz = sb(pj, "gz", [128, 64, 2], F32)
                        ge_ = sb(pj, "ge_", [128, 64, 2], F32)
                        sc.barrier()
                        A('act', lambda e: e.activation(out=co_sb[:], in_=co_f[:], func=AF.Sigmoid), w=["co_all"])
                        A('act', lambda e: e.activation(out=gz[:], in_=gates[:, :, 1:4:2], func=AF.Abs), w=["gz"])
                        A('act', lambda e: e.activation(out=ge_[:], in_=gz[:], func=AF.Exp, scale=-1.0), r=["gz"], w=["ge"])
                        A('act', lambda e: e.activation(out=ge_[:], in_=ge_[:], func=AF.Ln, bias=1.0), r=["ge"], w=["ge"])
                        A('dve', lambda e: e.tensor_scalar_min(out=gz[:], in0=gates[:, :, 1:4:2], scalar1=0.0), r=["gz"], w=["gz"])
                        A('dve', lambda e: e.tensor_tensor(out=gates[:, :, 1:4:2], in0=gz[:], in1=ge_[:], op=ALU.subtract), r=["gz", "ge"], w=["gates"])
                        sc.barrier()

                    if debug is not None and debug[0] == "s1":
                        A('pool', lambda e: e.dma_start(out=dbg, in_=snd_y.ap()), dma="dbg")
                        sc.barrier()
                        return nc
                    with ExitStack() as pa:
                        vtok = sb(pa, "vtok", [128, 69, 64], BF16)
                        acc = sb(pa, "acc", [64, 2, T], F32)
                        pT = [sb(pa, "pT%d" % i, [128, 256], BF16) for i in range(3)]
                        rec = sb(pa, "rec", [64, T], F32)
                        yb = sb(pa, "yb", [64, T], BF16)
                        ptc = [0]
                        for sbk in range(4):
                            base = T * sbk
                            tiles = []
                            for p_, dil in enumerate(DILS):
                                nq = T // dil // 128
                                for r_ in range(dil):
                                    for m_ in range(nq + 1):
                                        tiles.append((p_, r_, m_, HALO + base + r_ + dil * (128 * m_ - 64), dil))
                            tix = {}
                            for g0 in range(0, len(tiles), 16):
                                grp = tiles[g0:g0 + 16]
                                for n_, (p_, r_, m_, c0, dil) in enumerate(grp):
                                    tix[(p_, r_, m_)] = g0 + n_
                                    A('pe', lambda e, n_=n_, c0=c0, dil=dil: e.transpose(pst[:, n_ * 64:(n_ + 1) * 64],
                                                                                           C2[64:128, c0:c0 + 127 * dil + 1:dil], ident_b[64:128, 64:128]),
                                      w=["pst"])
                                A('dve', lambda e, g0=g0, grp=grp: e.tensor_copy(out=vtok[:, g0:g0 + len(grp), :],
                                                                                 in_=pst[:, 0:len(grp) * 64].rearrange("p (a b) -> p a b", b=64)),
                                  r=["pst"], w=[("vtok", g0)])
                            for p_, dil in enumerate(DILS):
                                nq = T // dil // 128
                                for r_ in range(dil):
                                    for a_ in range(nq):
                                        q0 = base + r_ + dil * 128 * a_
                                        ps, pk = nextps()
                                        pt = pT[ptc[0] % 3]
                                        ptk = ("pT", ptc[0] % 3)
                                        ptc[0] += 1
                                        for h_ in range(2):
                                            kp0 = base + r_ + dil * (128 * a_ - 64 + 128 * h_)
                                            kc0 = HALO + kp0
                                            A('pe', lambda e, h_=h_, kc0=kc0: e.matmul(ps[:, h_ * 128:(h_ + 1) * 128],
                                                                                        C1[64:128, kc0:kc0 + 127 * dil + 1:dil],
                                                                                        C0[64:128, q0:q0 + 127 * dil + 1:dil], start=True, stop=False),
                                              w=[pk])
                                            A('pe', lambda e, h_=h_: e.matmul(ps[:, h_ * 128:(h_ + 1) * 128], ident_b, B_sb[:, p_ * 2 + h_, :],
                                                                              start=False, stop=True), w=[pk])
                                            kbc = 1 if kp0 < 0 else (2 if kp0 + 127 * dil >= S else 0)
                                            A('act', lambda e, h_=h_, kbc=kbc: e.activation(out=pt[:, h_ * 128:(h_ + 1) * 128],
                                                                                             in_=ps[:, h_ * 128:(h_ + 1) * 128], func=AF.Exp,
                                                                                             bias=kb[:, kbc:kbc + 1]),
                                              r=[pk], w=[ptk])
                                        pn, pnk = nextps()
                                        for h_ in range(2):
                                            ti = tix[(p_, r_, a_ + h_)]
                                            A('pe', lambda e, h_=h_, ti=ti: e.matmul(pn[0:64, 0:128], vtok[:, ti, :], pt[:, h_ * 128:(h_ + 1) * 128],
                                                                                      start=(h_ == 0), stop=(h_ == 1)),
                                              r=[ptk, ("vtok", (ti // 16) * 16)], w=[pnk])
                                        for h_ in range(2):
                                            A('pe', lambda e, h_=h_: e.matmul(pn[0:64, 128:256], ones_b[:, 0:64], pt[:, h_ * 128:(h_ + 1) * 128],
                                                                              start=(h_ == 0), stop=(h_ == 1)), r=[ptk], w=[pnk])
                                        lo = r_ + dil * 128 * a_
                                        accv = acc[:, :, lo:lo + 127 * dil + 1:dil]
                                        pnv = pn[0:64, 0:256].rearrange("p (a b) -> p a b", a=2)
                                        if p_ == 0:
                                            A('dve', lambda e, accv=accv, pnv=pnv: e.tensor_copy(out=accv, in_=pnv), r=[pnk], w=["acc"])
                                        else:
                                            A('dve', lambda e, accv=accv, pnv=pnv: e.tensor_tensor(out=accv, in0=accv, in1=pnv, op=ALU.add),
                                              r=[pnk, "acc"], w=["acc"])
                            A('dve', lambda e: e.reciprocal(out=rec[:], in_=acc[:, 1, :]), r=["acc"], w=["rec"])
                            A('dve', lambda e: e.tensor_tensor(out=yb[:], in0=acc[:, 0, :], in1=rec[:], op=ALU.mult), r=["acc", "rec"], w=["yb"])
                            A('sp', lambda e: e.dma_start(out=snd_y.ap()[sbk * 256 + 64:sbk * 256 + 128, :], in_=yb[:]), r=["yb"], w=[("sndy", 1, sbk)], dma="ybo")
                        sc.barrier()

                    if debug is not None and debug[0] == "s2":
                        A('pool', lambda e: e.dma_start(out=dbg, in_=snd_y.ap()), dma="dbg")
                        sc.barrier()
                        return nc
                    with ExitStack() as pm:
                        cq = sb(pm, "cq", [64, S], BF16)
                        ck = sb(pm, "ck", [64, S], BF16)
                        ktok = sb(pm, "ktok", [128, 64, 64], BF16)
                        hf = sb(pm, "hf", [128, 64, 64], F32)
                        cvt = [sb(pm, "cvt%d" % i, [64, T], F32) for i in range(2)]
                        for pc in range(4):
                            for wi, (Cx, dst) in enumerate(((C1, cq), (C2, ck))):
                                c0 = HALO + pc * T
                                tv = cvt[wi]
                                tk = ("cvt", wi)
                                A('dve', lambda e, Cx=Cx, tv=tv: e.tensor_scalar_mul(out=tv[:], in0=Cx[0:64, c0:c0 + T], scalar1=cw_sb[:, wi, 1:2]),
                                  w=[tk])
                                A('dve', lambda e, Cx=Cx, tv=tv: e.scalar_tensor_tensor(out=tv[:], in0=Cx[0:64, c0 - 1:c0 - 1 + T], scalar=cw_sb[:, wi, 0:1],
                                                                                        in1=tv[:], op0=ALU.mult, op1=ALU.add), r=[tk], w=[tk])
                                A('dve', lambda e, Cx=Cx, tv=tv: e.scalar_tensor_tensor(out=tv[:], in0=Cx[0:64, c0 + 1:c0 + 1 + T], scalar=cw_sb[:, wi, 2:3],
                                                                                        in1=tv[:], op0=ALU.mult, op1=ALU.add), r=[tk], w=[tk])
                                if wi == 0:
                                    A('act', lambda e, tv=tv, dst=dst: e.activation(out=dst[:, pc * T:(pc + 1) * T], in_=tv[:], func=AF.Silu),
                                      r=[tk], w=[("cq", pc)])
                                else:
                                    A('act', lambda e, tv=tv: e.activation(out=tv[:], in_=tv[:], func=AF.Silu), r=[tk], w=[tk])
                                    A('pool', lambda e, tv=tv, dst=dst: e.tensor_scalar_mul(out=dst[:, pc * T:(pc + 1) * T], in0=tv[:], scalar1=0.125),
                                      r=[tk], w=[("ck", pc)])
                        sc.barrier()
                        for g0 in range(0, 64, 16):
                            for n_ in range(16):
                                c_ = g0 + n_
                                A('pe', lambda e, n_=n_, c_=c_: e.transpose(pst[:, n_ * 64:(n_ + 1) * 64], ck[0:64, c_ * 128:(c_ + 1) * 128],
                                                                            ident_b[0:64, 0:64]), w=["pst"])
                            A('dve', lambda e, g0=g0: e.tensor_copy(out=ktok[:, g0:g0 + 16, :], in_=pst[:, 0:1024].rearrange("p (a b) -> p a b", b=64)),
                              r=["pst"], w=["ktok"])
                        sc.barrier()

                        NS = 3
                        av_ = [sb(pm, "av%d" % i, [128, 4], F32) for i in range(NS)]
                        eG = [sb(pm, "eG%d" % i, [128, 1], F32) for i in range(NS)]
                        eD = [sb(pm, "eD%d" % i, [128, 128], F32) for i in range(NS)]
                        ST = [sb(pm, "ST%d" % i, [128, 128], BF16) for i in range(NS)]
                        kw = [sb(pm, "kw%d" % i, [128, 64], BF16) for i in range(NS)]
                        tI = [sb(pm, "tI%d" % i, [128, 65], F32) for i in range(NS)]
                        tot = [sb(pm, "tot%d" % i, [128, 65], F32) for i in range(NS)]
                        dd = [sb(pm, "dd%d" % i, [128, 2], F32) for i in range(NS)]
                        Cf = [sb(pm, "Cf%d" % i, [64, 65], F32) for i in range(2)]
                        Cb = [sb(pm, "Cb%d" % i, [64, 65], BF16) for i in range(2)]
                        hs = [sb(pm, "hs%d" % i, [128, 64], F32) for i in range(2)]
                        hst = [sb(pm, "hst%d" % i, [128, 6], F32) for i in range(2)]
                        hmv = [sb(pm, "hmv%d" % i, [128, 2], F32) for i in range(2)]
                        hr = [sb(pm, "hr%d" % i, [128, 1], F32) for i in range(2)]
                        hn = [sb(pm, "hn%d" % i, [128, 64], F32) for i in range(2)]
                        yct = [sb(pm, "yct%d" % i, [128, 64], BF16) for i in range(2)]
                        ycT = [sb(pm, "ycT%d" % i, [64, 128], BF16) for i in range(2)]
                        for d_ in range(2):
                            A('dve', lambda e, d_=d_: e.memset(Cf[d_][:], 0.0), w=[("Cf", d_)])
                            A('dve', lambda e, d_=d_: e.memset(Cb[d_][:], 0.0), w=[("Cb", d_)])
                        stp = [0]
                        for it in range(64):
                            for d_ in range(2):
                                c_ = it if d_ == 0 else 63 - it
                                s3 = stp[0] % NS
                                stp[0] += 1
                                lf_ = gates[:, c_, 2 * d_ + 1:2 * d_ + 2]
                                li_ = gates[:, c_, 2 * d_:2 * d_ + 1]
                                psA, pkA = nextps()
                                A('pe', lambda e, d_=d_, lf_=lf_: e.matmul(psA[:, 0:1], tri[d_], lf_, start=True, stop=True), w=[pkA])
                                A('pe', lambda e, lf_=lf_: e.matmul(psA[:, 1:2], ones_f, lf_, start=True, stop=True), w=[pkA])
                                psB, pkB = nextps()
                                A('pe', lambda e, d_=d_, lf_=lf_: e.matmul(psB[:, 0:128], lf_.to_broadcast([128, 128]), tri[d_], start=True, stop=False),
                                  w=[pkB])
                                A('pe', lambda e, d_=d_: e.matmul(psB[:, 0:128], ident_f, mneg[d_], start=False, stop=True), w=[pkB])
                                avs = av_[s3]
                                A('dve', lambda e, avs=avs, li_=li_: e.tensor_tensor(out=avs[:, 0:1], in0=li_, in1=psA[:, 0:1], op=ALU.subtract),
                                  r=[pkA], w=[("av", s3)])
                                A('dve', lambda e, avs=avs: e.tensor_tensor(out=avs[:, 1:2], in0=psA[:, 1:2], in1=avs[:, 0:1], op=ALU.add),
                                  r=[pkA, ("av", s3)], w=[("av", s3)])
                                A('act', lambda e, avs=avs: e.activation(out=eD[s3][:], in_=psB[:, 0:128], func=AF.Exp, bias=avs[:, 0:1]),
                                  r=[pkB, ("av", s3)], w=[("eD", s3)])
                                A('act', lambda e, avs=avs: e.activation(out=avs[:, 2:3], in_=psA[:, 0:1], func=AF.Exp), r=[pkA, ("av", s3)], w=[("av", s3)])
                                A('act', lambda e, avs=avs: e.activation(out=avs[:, 3:4], in_=avs[:, 1:2], func=AF.Exp), r=[("av", s3)], w=[("av", s3)])
                                A('act', lambda e: e.activation(out=eG[s3][:], in_=psA[:, 1:2], func=AF.Exp), r=[pkA], w=[("eG", s3)])
                                psS, pkS = nextps()
                                A('pe', lambda e, c_=c_: e.matmul(psS[:, 0:128], ck[:, c_ * 128:(c_ + 1) * 128], cq[:, c_ * 128:(c_ + 1) * 128],
                                                                   start=True, stop=True), w=[pkS])
                                A('dve', lambda e: e.tensor_tensor(out=ST[s3][:], in0=psS[:, 0:128], in1=eD[s3][:], op=ALU.mult),
                                  r=[pkS, ("eD", s3)], w=[("ST", s3)])
                                A('dve', lambda e, c_=c_, avs=avs: e.tensor_scalar_mul(out=kw[s3][:], in0=ktok[:, c_, :], scalar1=avs[:, 3:4]),
                                  r=[("av", s3)], w=[("kw", s3)])
                                psC, pkC = nextps()
                                A('pe', lambda e, c_=c_: e.matmul(psC[0:64, 0:65], kw[s3][:], vaug[:, c_, :], start=True, stop=True),
                                  r=[("kw", s3)], w=[pkC])
                                psO, pkO = nextps()
                                A('pe', lambda e, c_=c_: e.matmul(psO[:, 0:65], ST[s3][:], vaug[:, c_, :], start=True, stop=True),
                                  r=[("ST", s3)], w=[pkO])
                                A('pe', lambda e, c_=c_, d_=d_: e.matmul(psO[:, 65:130], cq[:, c_ * 128:(c_ + 1) * 128], Cb[d_][:], start=True, stop=True),
                                  r=[("Cb", d_)], w=[pkO])
                                A('dve', lambda e, avs=avs: e.tensor_scalar_mul(out=tI[s3][:], in0=psO[:, 65:130], scalar1=avs[:, 2:3]),
                                  r=[pkO, ("av", s3)], w=[("tI", s3)])
                                A('dve', lambda e: e.tensor_tensor(out=tot[s3][:], in0=psO[:, 0:65], in1=tI[s3][:], op=ALU.add),
                                  r=[pkO, ("tI", s3)], w=[("tot", s3)])
                                A('dve', lambda e, d_=d_: e.scalar_tensor_tensor(out=Cf[d_][:], in0=Cf[d_][:], scalar=eG[s3][0:64, 0:1], in1=psC[0:64, 0:65],
                                                                                 op0=ALU.mult, op1=ALU.add), r=[pkC, ("eG", s3), ("Cf", d_)], w=[("Cf", d_)])
                                A('pool', lambda e, d_=d_: e.tensor_copy(out=Cb[d_][:], in_=Cf[d_][:]), r=[("Cf", d_)], w=[("Cb", d_)])
                                A('dve', lambda e: e.scalar_tensor_tensor(out=dd[s3][:, 0:1], in0=tot[s3][:, 64:65], scalar=-1.0, in1=tot[s3][:, 64:65],
                                                                          op0=ALU.mult, op1=ALU.max), r=[("tot", s3)], w=[("dd", s3)])
                                A('dve', lambda e: e.tensor_scalar_max(out=dd[s3][:, 0:1], in0=dd[s3][:, 0:1], scalar1=1.0),
                                  r=[("dd", s3)], w=[("dd", s3)])
                                A('dve', lambda e: e.reciprocal(out=dd[s3][:, 1:2], in_=dd[s3][:, 0:1]), r=[("dd", s3)], w=[("dd", s3)])
                                first = (c_ < 32) == (d_ == 0)
                                if first:
                                    A('dve', lambda e, c_=c_: e.tensor_scalar_mul(out=hf[:, c_, :], in0=tot[s3][:, 0:64], scalar1=dd[s3][:, 1:2]),
                                      r=[("tot", s3), ("dd", s3)], w=[("hf", c_)])
                                else:
                                    h2 = c_ % 2
                                    A('dve', lambda e, c_=c_: e.scalar_tensor_tensor(out=hs[h2][:], in0=tot[s3][:, 0:64], scalar=dd[s3][:, 1:2],
                                                                                      in1=hf[:, c_, :], op0=ALU.mult, op1=ALU.add),
                                      r=[("tot", s3), ("dd", s3), ("hf", c_)], w=[("hs", h2)])
                                    A('dve', lambda e: e.bn_stats(out=hst[h2][:], in_=hs[h2][:]), r=[("hs", h2)], w=[("hst", h2)])
                                    A('dve', lambda e: e.bn_aggr(out=hmv[h2][:], in_=hst[h2][:]), r=[("hst", h2)], w=[("hmv", h2)])
                                    A('act', lambda e: e.activation(out=hr[h2][:], in_=hmv[h2][:, 1:2], func=AF.Ln, bias=EPS), r=[("hmv", h2)], w=[("hr", h2)])
                                    A('act', lambda e: e.activation(out=hr[h2][:], in_=hr[h2][:], func=AF.Exp, scale=-0.5), r=[("hr", h2)], w=[("hr", h2)])
                                    A('dve', lambda e: e.tensor_scalar(out=hn[h2][:], in0=hs[h2][:], scalar1=hmv[h2][:, 0:1], scalar2=hr[h2][:, 0:1],
                                                                       op0=ALU.subtract, op1=ALU.mult),
                                      r=[("hs", h2), ("hmv", h2), ("hr", h2)], w=[("hn", h2)])
                                    A('pool', lambda e: e.tensor_tensor(out=hn[h2][:], in0=hn[h2][:], in1=ng_bc[:], op=ALU.mult),
                                      r=[("hn", h2)], w=[("hn", h2)])
                                    A('pool', lambda e, c_=c_: e.tensor_tensor(out=yct[h2][:], in0=hn[h2][:], in1=co_sb[:, c_, :], op=ALU.mult),
                                      r=[("hn", h2)], w=[("yct", h2)])
                                    A('pe', lambda e: e.transpose(pst[0:64, 0:128], yct[h2][:], ident_b), r=[("yct", h2)], w=["pst"])
                                    A('dve', lambda e: e.tensor_copy(out=ycT[h2][:], in_=pst[0:64, 0:128]), r=["pst"], w=[("ycT", h2)])
                                    A('sp', lambda e, c_=c_: e.dma_start(out=snd_y.ap()[(c_ // 16) * 256 + 128:(c_ // 16) * 256 + 192, (c_ % 16) * 128:(c_ % 16 + 1) * 128], in_=ycT[h2][:]),
                                      r=[("ycT", h2)], w=[("sndy", 2, c_)], dma=("yco", h2))
                        sc.barrier()
            if debug is not None and debug[0] == "y":
                A('pool', lambda e: e.dma_start(out=dbg, in_=snd_y.ap()), dma="dbg")
                sc.barrier()
                return nc

            for i in range(4):
                A('pool', lambda e, i=i: e.collective_compute("AllGather", ALU.bypass, replica_groups=G4,
                                                              ins=[snd_y.ap()[i * 256:(i + 1) * 256, :].opt()],
                                                              outs=[rcv_y.ap()[i * 1024:(i + 1) * 1024, :].opt()]),
                  dma="ccy", inc=1, cc=True)
            sc.barrier()
            if debug is not None and debug[0] == "c1":
                sc.barrier()
                A('sp', lambda e: e.dma_start(out=dbg, in_=cstb[:, 0, :]), dma="dbg")
                sc.barrier()
                return nc

            with ExitStack() as pd:
                mgh = sb(pd, "mgh", [128, 8, T], BF16)
                with ExitStack() as p1:
                    yall = sb(p1, "yall", [128, 8, T], BF16)
                    xbo = sb(p1, "xbo", [128, 8, T], BF16)
                    wbr_sb = sb(p1, "wbr_sb", [128, 8, D], BF16)
                    idx_sb = sb(p1, "idx_sb", [128, 8], I32)
                    bg_sb = sb(p1, "bg_sb", [128, 4, 8], F32)
                    wgs = [sb(p1, "wgs%d" % i, [128, 8, 4, 128], BF16) for i in range(2)]
                    gs = [sb(p1, "gs%d" % i, [128, 512], F32) for i in range(2)]
                    macc = [sb(p1, "macc%d" % i, [128, 512], F32) for i in range(2)]
                    tmpm = [sb(p1, "tmpm%d" % i, [128, 512], F32) for i in range(2)]
                    A('sp', lambda e: e.dma_start(out=idx_sb[:], in_=idxy), dma="setup")
                    with nc.allow_non_contiguous_dma(reason="tiny bias load"):
                        for n_ in range(4):
                            A('sp', lambda e, n_=n_: e.dma_start(out=bg_sb[:, n_, :], in_=bg[l][n_ * D:(n_ + 1) * D].rearrange("(dc p) -> p dc", p=128)), dma="setup")
                    A('sp', lambda e: e.dma_start(out=xbo[:], in_=snd_x.ap().rearrange("(c p) t -> p c t", p=128)), dma="setup")
                    for q in range(8):
                        src_ = wrows(l, q // 2, 1280 + (q % 2) * 128)
                        A('pool', lambda e, q=q, src_=src_: e.dma_start(out=wbr_sb[:, q, :], in_=src_), dma="setup")
                    sc.barrier()
                    if debug is not None and debug[0] == "c2":
                        sc.barrier()
                        A('sp', lambda e: e.dma_start(out=dbg, in_=cstb[:, 0, :]), dma="dbg")
                        sc.barrier()
                        return nc
                    for q in range(8):
                        A('pool', lambda e, q=q: e.indirect_dma_start(out=yall[:, q, :], out_offset=None, in_=rcv_y.ap(),
                                                                      in_offset=bass.IndirectOffsetOnAxis(ap=idx_sb[:, q:q + 1], axis=0)),
                          dma=("yall", q))
                    sc.barrier()
                    if debug is not None and debug[0] == "c3":
                        sc.barrier()
                        A('sp', lambda e: e.dma_start(out=dbg, in_=cstb[:, 0, :]), dma="dbg")
                        sc.barrier()
                        return nc
                    cnt = [0]
                    for dc in range(8):
                        ws = dc % 2
                        wkeys = [("wgs", ws, kc) for kc in range(8)]
                        for kc in range(8):
                            src_ = wrows(l, kc // 2, (kc % 2) * 512, 512)
                            A('pool', lambda e, kc=kc, src_=src_: e.dma_start(
                                out=wgs[ws][:, kc, :, :],
                                in_=src_.rearrange("(p a) b -> p a b", a=4)[:, :, dc * 128:(dc + 1) * 128]),
                              w=[("wgs", ws, kc)], dma=("wgs", ws))
                        for tg in range(4):
                            ms = (dc * 4 + tg) % 2
                            for n_ in range(4):
                                c2 = cnt[0] % 2
                                cnt[0] += 1
                                psG, pkG = nextps()
                                for kc in range(8):
                                    A('pe', lambda e, kc=kc, n_=n_: e.matmul(psG[:, 0:512], wgs[ws][:, kc, n_, :], xbo[:, kc, tg * 512:(tg + 1) * 512],
                                                                              start=(kc == 0), stop=(kc == 7)), r=wkeys, w=[pkG])
                                A('act', lambda e, n_=n_: e.activation(out=gs[c2][:], in_=psG[:, 0:512], func=AF.Sigmoid, bias=bg_sb[:, n_, dc:dc + 1]),
                                  r=[pkG], w=[("gs", c2)])
                                psP, pkP = nextps()
                                pb_ = (n_ % 2) * 64
                                for r_ in range(4):
                                    q = 2 * r_ + n_ // 2
                                    A('pe', lambda e, q=q, r_=r_, pb_=pb_: e.matmul(psP[:, 0:512], wbr_sb[pb_:pb_ + 64, q, dc * 128:(dc + 1) * 128],
                                                                                     yall[pb_:pb_ + 64, q, tg * 512:(tg + 1) * 512],
                                                                                     start=(r_ == 0), stop=(r_ == 3)), w=[pkP])
                                if n_ == 0:
                                    A('dve', lambda e: e.tensor_tensor(out=macc[ms][:], in0=gs[c2][:], in1=psP[:, 0:512], op=ALU.mult),
                                      r=[("gs", c2), pkP], w=[("macc", ms)])
                                else:
                                    A('dve', lambda e: e.tensor_tensor(out=tmpm[c2][:], in0=gs[c2][:], in1=psP[:, 0:512], op=ALU.mult),
                                      r=[("gs", c2), pkP], w=[("tmpm", c2)])
                                    if n_ < 3:
                                        A('pool', lambda e: e.tensor_tensor(out=macc[ms][:], in0=macc[ms][:], in1=tmpm[c2][:], op=ALU.add),
                                          r=[("tmpm", c2), ("macc", ms)], w=[("macc", ms)])
                                    else:
                                        A('pool', lambda e: e.tensor_tensor(out=mgh[:, dc, tg * 512:(tg + 1) * 512], in0=macc[ms][:], in1=tmpm[c2][:], op=ALU.add),
                                          r=[("tmpm", c2), ("macc", ms)], w=[("mg", dc, tg)])
                    sc.barrier()
                if debug is not None and debug[0] == "mg":
                    for c in range(8):
                        A('pool', lambda e, c=c: e.dma_start(out=dbg[c * 128:(c + 1) * 128, :], in_=mgh[:, c, :]), dma="dbg")
                    sc.barrier()
                    return nc

                z = sb(pd, "z", [128, 8, T], F32)
                x1b = sb(pd, "x1b", [128, 8, T], BF16)
                lnp = sb(pd, "lnp", [128, 4, 8], F32)
                gm = sb(pd, "gm", [16, T], F32)
                with nc.allow_non_contiguous_dma(reason="tiny ln param loads"):
                    for n_, src in enumerate((ln1g, ln1b, ln2g, ln2b)):
                        A('sp', lambda e, n_=n_, src=src: e.dma_start(out=lnp[:, n_, :], in_=src[l].rearrange("(c p) -> p c", p=128)), dma="setup")

                def layer_norm_fm(stack, gi, tag, also_bf16):
                    sq = [sb(stack, "sq%s%d" % (tag, i), [128, 512], F32) for i in range(2)]
                    mean_ = sb(stack, "mean" + tag, [128, 512], F32)
                    msq = sb(stack, "msq" + tag, [128, 512], F32)
                    rs_ = sb(stack, "rs" + tag, [128, 512], F32)
                    for tg in range(4):
                        tsl = slice(tg * 512, (tg + 1) * 512)
                        ps1, pk1 = nextps()
                        ps2, pk2 = nextps()
                        for c in range(8):
                            A('pe', lambda e, c=c: e.matmul(ps1[:, 0:512], ones_f, z[:, c, tsl], start=(c == 0), stop=(c == 7)),
                              r=[("z", c, tg)], w=[pk1])
                        for c in range(8):
                            A('act', lambda e, c=c: e.activation(out=sq[c % 2][:], in_=z[:, c, tsl], func=AF.Square),
                              r=[("z", c, tg)], w=[("sq", c % 2)])
                            A('pe', lambda e, c=c: e.matmul(ps2[:, 0:512], ones_f, sq[c % 2][:], start=(c == 0), stop=(c == 7)),
                              r=[("sq", c % 2)], w=[pk2])
                        A('dve', lambda e: e.tensor_scalar_mul(out=mean_[:], in0=ps1[:, 0:512], scalar1=1.0 / D), r=[pk1], w=["mean"])
                        A('dve', lambda e: e.tensor_tensor(out=msq[:], in0=mean_[:], in1=mean_[:], op=ALU.mult), r=["mean"], w=["msq"])
                        A('dve', lambda e: e.scalar_tensor_tensor(out=rs_[:], in0=ps2[:, 0:512], scalar=1.0 / D, in1=msq[:],
                                                                  op0=ALU.mult, op1=ALU.subtract), r=[pk2, "msq"], w=["rs"])
                        rstd_op(rs_[:], rs_[:], ["rs"], ["rs"])
                        for c in range(8):
                            zk = ("z", c, tg)
                            A('dve', lambda e, c=c: e.tensor_tensor(out=z[:, c, tsl], in0=z[:, c, tsl], in1=mean_[:], op=ALU.subtract),
                              r=[zk, "mean"], w=[zk])
                            A('pool', lambda e, c=c: e.tensor_tensor(out=z[:, c, tsl], in0=z[:, c, tsl], in1=rs_[:], op=ALU.mult),
                              r=[zk, "rs"], w=[zk])
                            A('act', lambda e, c=c: e.activation(out=z[:, c, tsl], in_=z[:, c, tsl], func=AF.Identity,
                                                                 bias=lnp[:, gi + 1, c:c + 1], scale=lnp[:, gi, c:c + 1]), r=[zk], w=[zk])
                            if also_bf16:
                                A('pool', lambda e, c=c: e.tensor_copy(out=x1b[:, c, tsl], in_=z[:, c, tsl]), r=[zk], w=[("x1b", c, tg)])

                with ExitStack() as p2:
                    wo_sb = sb(p2, "wo_sb", [128, 8, D], BF16)
                    xr = [sb(p2, "xr%d" % i, [128, 512], F32) for i in range(3)]
                    for kc in range(8):
                        src_ = wrows(l, kc // 2, 1024 + (kc % 2) * 128)
                        A('pool', lambda e, kc=kc, src_=src_: e.dma_start(out=wo_sb[:, kc, :], in_=src_), dma="setup")
                    sc.barrier()
                    n3 = [0]
                    for ec in range(8):
                        for tg in range(4):
                            s3 = n3[0] % 3
                            n3[0] += 1
                            A('sp', lambda e: e.dma_start(out=xr[s3][:], in_=xsrc[ec * 128:(ec + 1) * 128, tg * 512:(tg + 1) * 512]),
                              w=[("xr", s3)], dma=("xr", s3))
                            psM, pkM = nextps()
                            for kc in range(8):
                                A('pe', lambda e, kc=kc: e.matmul(psM[:, 0:512], wo_sb[:, kc, ec * 128:(ec + 1) * 128], mgh[:, kc, tg * 512:(tg + 1) * 512],
                                                                    start=(kc == 0), stop=(kc == 7)), w=[pkM])
                            A('dve', lambda e: e.scalar_tensor_tensor(out=z[:, ec, tg * 512:(tg + 1) * 512], in0=xr[s3][:], scalar=ALPHA,
                                                                      in1=psM[:, 0:512], op0=ALU.mult, op1=ALU.add),
                              r=[("xr", s3), pkM], w=[("z", ec, tg)])
                    sc.barrier()
                    layer_norm_fm(p2, 0, "a", True)
                    sc.barrier()
                if debug is not None and debug[0] == "x1":
                    for c in range(8):
                        A('sp', lambda e, c=c: e.dma_start(out=dbg[c * 128:(c + 1) * 128, :], in_=z[:, c, :]), dma="dbg")
                    sc.barrier()
                    return nc

                with ExitStack() as p3:
                    wr_sb = sb(p3, "wr_sb", [128, 8, 16], F32)
                    ex = sb(p3, "ex", [16, T], F32)
                    affo = sb(p3, "affo", [16, T], F32)
                    Aall = sb(p3, "Aall", [128, 1024], F32)
                    junk = sb(p3, "junk", [128, 1024], F32)
                    bs_ = sb(p3, "bs_", [128, 8], F32)
                    A('sp', lambda e: e.dma_start(out=wr_sb[:], in_=wr[l].rearrange("(c p) n -> p c n", p=128)), dma="setup")
                    sc.barrier()
                    for tg in range(4):
                        tsl = slice(tg * 512, (tg + 1) * 512)
                        psR, pkR = nextps()
                        for kc in range(8):
                            A('pe', lambda e, kc=kc: e.matmul(psR[0:16, 0:512], wr_sb[:, kc, :], z[:, kc, tsl], start=(kc == 0), stop=(kc == 7)), w=[pkR])
                        A('act', lambda e: e.activation(out=ex[:, tsl], in_=psR[0:16, 0:512], func=AF.Exp), r=[pkR], w=[("ex", tg)])
                        psE, pkE = nextps()
                        A('pe', lambda e: e.matmul(psE[0:16, 0:512], ones_f[0:16, 0:16], ex[:, tsl], start=True, stop=True), r=[("ex", tg)], w=[pkE])
                        A('dve', lambda e: e.reciprocal(out=affo[:, tsl], in_=psE[0:16, 0:512]), r=[pkE], w=[("affo", tg)])
                        A('dve', lambda e: e.tensor_tensor(out=affo[:, tsl], in0=affo[:, tsl], in1=ex[:, tsl], op=ALU.mult),
                          r=[("affo", tg), ("ex", tg)], w=[("affo", tg)])
                    sc.barrier()
                    A('sp', lambda e: e.dma_start(out=snd_a.ap(), in_=affo[:]), dma="setup")
                    sc.barrier()
                    A('pool', lambda e: e.collective_compute("AllGather", ALU.bypass, replica_groups=G4,
                                                             ins=[snd_a.ap().opt()], outs=[rcv_a.ap().opt()]), dma="cca", inc=1, cc=True)
                    sc.barrier()
                    for r_ in range(4):
                        for h_ in range(2):
                            g_ = 2 * r_ + h_
                            A('sp', lambda e, r_=r_, h_=h_, g_=g_: e.dma_start(out=Aall[g_ * 16:(g_ + 1) * 16, :],
                                                                               in_=rcv_a.ap()[r_ * 16:(r_ + 1) * 16, h_ * 1024:(h_ + 1) * 1024]), dma="setup")
                    A('dve', lambda e: e.memset(bs_[:, 0:1], 0.0), w=["bs"])
                    A('dve', lambda e: e.memset(bs_[:, 1:2], 1.5), r=["bs"], w=["bs"])
                    sc.barrier()
                    lo, hi, mid, cn, ge, d1, d2 = [bs_[:, i:i + 1] for i in range(7)]
                    for it in range(30):
                        A('dve', lambda e: e.tensor_scalar(out=mid, in0=lo, scalar1=hi, scalar2=0.5, op0=ALU.add, op1=ALU.mult), r=["bs"], w=["bs"])
                        A('dve', lambda e: e.tensor_scalar(out=junk[:], in0=Aall[:], scalar1=mid, scalar2=0.0, op0=ALU.is_ge, op1=ALU.add,
                                                           accum_out=cn), r=["bs"], w=["bs", "junk"])
                        psT, pkT = nextps()
                        A('pe', lambda e: e.matmul(psT[:, 0:1], blk, cn, start=True, stop=True), r=["bs"], w=[pkT])
                        A('dve', lambda e: e.tensor_single_scalar(out=ge, in_=psT[:, 0:1], scalar=1023.5, op=ALU.is_ge), r=[pkT, "bs"], w=["bs"])
                        A('dve', lambda e: e.tensor_tensor(out=d1, in0=mid, in1=lo, op=ALU.subtract), r=["bs"], w=["bs"])
                        A('dve', lambda e: e.tensor_tensor(out=d2, in0=hi, in1=mid, op=ALU.subtract), r=["bs"], w=["bs"])
                        A('dve', lambda e: e.scalar_tensor_tensor(out=lo, in0=d1, scalar=ge, in1=lo, op0=ALU.mult, op1=ALU.add), r=["bs"], w=["bs"])
                        A('dve', lambda e: e.scalar_tensor_tensor(out=hi, in0=d2, scalar=ge, in1=mid, op0=ALU.mult, op1=ALU.add), r=["bs"], w=["bs"])
                    A('dve', lambda e: e.scalar_tensor_tensor(out=gm[:], in0=affo[:], scalar=bs_[0:16, 0:1], in1=affo[:], op0=ALU.is_ge, op1=ALU.mult),
                      r=["bs"], w=["gm"])
                    for c in range(8):
                        A('pool', lambda e, c=c: e.tensor_scalar_mul(out=z[:, c, :], in0=z[:, c, :], scalar1=ALPHA), w=[("zz", c)])
                    sc.barrier()
                if debug is not None and debug[0] == "gm":
                    A('sp', lambda e: e.dma_start(out=dbg, in_=gm[:]), dma="dbg")
                    sc.barrier()
                    return nc

                with ExitStack() as p4:
                    wE = [sb(p4, "wE%d" % i, [128, 8, D], BF16) for i in range(3)]
                    G_sb = [sb(p4, "G_sb%d" % i, [128, 512], F32) for i in range(4)]
                    stmp = [sb(p4, "stmp%d" % i, [128, 512], F32) for i in range(2)]
                    stm2 = [sb(p4, "stm2%d" % i, [128, 512], F32) for i in range(2)]
                    hn_ = [0]

                    def load_expert_w(ex_, which):
                        erk, ebase = ex_ // 4, 1536 + (ex_ % 4) * 3072
                        for wh in which:
                            for kc in range(8):
                                src_ = wrows(l, erk, ebase + wh * 1024 + kc * 128)
                                A('pool', lambda e, wh=wh, kc=kc, src_=src_: e.dma_start(out=wE[wh][:, kc, :], in_=src_),
                                  w=[("wE", wh, kc)], dma=("wE", wh))

                    load_expert_w(0, (0, 1, 2))
                    wk13 = [("wE", wh, kc) for wh in range(2) for kc in range(8)]
                    w2k = [("wE", 2, kc) for kc in range(8)]
                    for ex_ in range(16):
                        for tg in range(4):
                            psQ, pkQ = nextps()
                            A('pe', lambda e, tg=tg: e.matmul(psQ[:, 0:512], ident_f[0:16, ex_:ex_ + 1].to_broadcast([16, 128]), gm[:, tg * 512:(tg + 1) * 512],
                                                               start=True, stop=True), w=[pkQ])
                            A('act', lambda e, tg=tg: e.copy(out=G_sb[tg][:], in_=psQ[:, 0:512]), r=[pkQ], w=[("G", tg)])
                        for ft in range(8):
                            for tg in range(4):
                                tsl = slice(tg * 512, (tg + 1) * 512)
                                h2 = hn_[0] % 2
                                hn_[0] += 1
                                ps1, pk1 = nextps()
                                ps3, pk3 = nextps()
                                for kc in range(8):
                                    A('pe', lambda e, kc=kc: e.matmul(ps1[:, 0:512], wE[0][:, kc, ft * 128:(ft + 1) * 128], x1b[:, kc, tsl], start=(kc == 0), stop=(kc == 7)),
                                      r=wk13, w=[pk1])
                                for kc in range(8):
                                    A('pe', lambda e, kc=kc: e.matmul(ps3[:, 0:512], wE[1][:, kc, ft * 128:(ft + 1) * 128], x1b[:, kc, tsl], start=(kc == 0), stop=(kc == 7)),
                                      r=wk13, w=[pk3])
                                A('act', lambda e: e.activation(out=stmp[h2][:], in_=ps1[:, 0:512], func=AF.Silu), r=[pk1], w=[("stmp", h2)])
                                A('dve', lambda e: e.tensor_tensor(out=stm2[h2][:], in0=stmp[h2][:], in1=ps3[:, 0:512], op=ALU.mult),
                                  r=[("stmp", h2), pk3], w=[("stm2", h2)])
                                A('dve', lambda e, tg=tg: e.tensor_tensor(out=mgh[:, ft, tsl], in0=stm2[h2][:], in1=G_sb[tg][:], op=ALU.mult),
                                  r=[("stm2", h2), ("G", tg)], w=[("hid", ft, tg)])
                        if ex_ + 1 < 16:
                            load_expert_w(ex_ + 1, (0, 1))
                        for dc in range(8):
                            for tg in range(4):
                                tsl = slice(tg * 512, (tg + 1) * 512)
                                psY, pkY = nextps()
                                for fc in range(8):
                                    A('pe', lambda e, fc=fc: e.matmul(psY[:, 0:512], wE[2][:, fc, dc * 128:(dc + 1) * 128], mgh[:, fc, tsl],
                                                                        start=(fc == 0), stop=(fc == 7)), r=w2k + [("hid", fc, tg)], w=[pkY])
                                A('dve', lambda e: e.tensor_tensor(out=z[:, dc, tsl], in0=z[:, dc, tsl], in1=psY[:, 0:512], op=ALU.add),
                                  r=[pkY], w=[("zz", dc, tg)])
                        if ex_ + 1 < 16:
                            load_expert_w(ex_ + 1, (2,))
                    sc.barrier()
                with ExitStack() as p4b:
                    layer_norm_fm(p4b, 2, "b", False)
                    sc.barrier()

                if not last:
                    for c in range(8):
                        A('sp', lambda e, c=c: e.dma_start(out=xres.ap()[c * 128:(c + 1) * 128, :], in_=z[:, c, :]), dma="setup")
                        A('pool', lambda e, c=c: e.dma_start(out=snd_x.ap()[c * 128:(c + 1) * 128, :], in_=z[:, c, :]), dma="setup")
                    sc.barrier()
                else:
                    with ExitStack() as p5:
                        xo = [sb(p5, "xo%d" % i, [128, D], F32) for i in range(2)]
                        for tt in range(16):
                            o2 = tt % 2
                            for hh in range(2):
                                psX, pkX = nextps()
                                for c4 in range(4):
                                    c = hh * 4 + c4
                                    A('pe', lambda e, c=c, c4=c4: e.transpose(psX[:, c4 * 128:(c4 + 1) * 128], z[:, c, tt * 128:(tt + 1) * 128], ident_f), w=[pkX])
                                A('act' if hh == 0 else 'dve',
                                  (lambda e, hh=hh: e.copy(out=xo[o2][:, hh * 512:(hh + 1) * 512], in_=psX[:, 0:512])) if hh == 0 else
                                  (lambda e, hh=hh: e.tensor_copy(out=xo[o2][:, hh * 512:(hh + 1) * 512], in_=psX[:, 0:512])),
                                  r=[pkX], w=[("xo", o2, hh)])
                            A('sp', lambda e: e.dma_start(out=out[tt * 128:(tt + 1) * 128, :], in_=xo[o2][:]),
                              r=[("xo", o2, 0), ("xo", o2, 1)], w=[("out", tt)], dma=("xo", o2))
                        sc.barrier()
        sc.barrier()
    return nc


def _t5_bucket(rel):
    rel = np.asarray(rel, dtype=np.int64)
    half, max_exact = 16, 8
    ret = np.where(rel > 0, half, 0)
    n = np.abs(rel)
    nf = np.maximum(n, 1).astype(np.float32)
    large = max_exact + (np.log(nf / np.float32(max_exact)) / np.float32(math.log(1024 / max_exact))
                         * np.float32(half - max_exact)).astype(np.int32)
    large = np.minimum(large, half - 1)
    return ret + np.where(n < max_exact, n, large)


def _static_tables():
    p = np.arange(128)
    ident = np.eye(128, dtype=np.float32)
    tri_f = (p[:, None] <= p[None, :]).astype(np.float32)
    tri_b = (p[:, None] >= p[None, :]).astype(np.float32)
    mneg_f = np.where(p[:, None] <= p[None, :], 0.0, NEG).astype(np.float32)
    mneg_b = np.where(p[:, None] >= p[None, :], 0.0, NEG).astype(np.float32)
    blk = ((p[:, None] % 16) == (p[None, :] % 16)).astype(np.float32)
    ones = np.ones((128, 128), np.float32)
    kbias = np.zeros((128, 128), np.float32)
    kbias[:64, 1] = NEG
    kbias[64:, 2] = NEG
    return np.ascontiguousarray(np.stack([ident, tri_f, tri_b, mneg_f, mneg_b, blk, ones, kbias]))


def _band_tables(win):
    def wmat(qbase, pbase):
        Q = qbase + np.arange(128)[:, None]
        P = pbase + np.arange(128)[None, :]
        lo = np.clip(P - win // 2, 0, S)
        hi = np.clip(P + win // 2, 0, S)
        cntw = (hi - lo).astype(np.float32)
        m = ((Q >= lo) & (Q < hi)).astype(np.float32) / cntw
        return (m - (Q == P).astype(np.float32)).astype(np.float32)
    mid = 4096
    return np.ascontiguousarray(np.stack([wmat(mid - 128, mid), wmat(mid, mid), wmat(mid + 128, mid),
                                          wmat(0, 0), wmat(S - 128, S - 128)]))


def _prep_inputs(inp, depth):
    L = depth
    f = lambda a: np.ascontiguousarray(np.asarray(a, dtype=np.float32))
    x = np.asarray(inp['x'], np.float32)
    w_in = np.asarray(inp['w_in'], np.float32)[:L]
    b_in = np.asarray(inp['b_in'], np.float32)[:L]
    consts = _static_tables()
    maps = []
    kq = np.arange(128)
    for c in range(NCORE):
        b, m = c // 4, c % 4
        d = {}
        d["xT"] = f(x[b, T * m:T * (m + 1), :].T)
        sl = lambda o: np.arange(o + 64 * m, o + 64 * m + 64)
        cols_f = np.concatenate([sl(0), sl(512), sl(1280), sl(768), sl(1536), sl(1024)])
        gv = 256 + np.concatenate([np.arange(64 * m, 64 * m + 64)] + [np.arange(64 * g, 64 * g + 64) for g in range(4) if g != m])
        gcols = np.array([2304 + m, 2312 + m, 2304 + 4 + m, 2312 + 4 + m])
        cols_t = np.concatenate([gv, sl(1792), sl(2048), gcols, sl(2320)])
        d["whf"] = f(w_in[:, :, cols_f])
        d["bhf"] = f(b_in[:, cols_f])
        d["wht"] = f(w_in[:, :, cols_t])
        d["bht"] = f(b_in[:, cols_t])
        d["fbias"] = f(np.asarray(inp['ml_fbias'])[:L, :, m])
        d["wsT"] = f(np.transpose(np.asarray(inp['gm_ws'])[:L, m], (0, 2, 1)))
        d["bsv"] = f(np.asarray(inp['gm_bs'])[:L, m, :])
        d["lng"] = f(np.asarray(inp['gm_ln_g'])[:L, 64 * m:64 * m + 64])
        rb = np.asarray(inp['rel_bias'], np.float32)
        bm = np.zeros((6, 128, 128), np.float32)
        for p_, dil in enumerate(DILS):
            for h_ in range(2):
                j = (-64 + 128 * h_ + kq[:, None]) - kq[None, :]
                valid = np.abs(j) <= 64
                vals = rb[_t5_bucket(dil * j), m]
                bm[p_ * 2 + h_] = np.where(valid, vals, np.float32(NEG))
        d["bmat"] = f(bm)
        mc = np.asarray(inp['ml_conv'], np.float32)[:L]
        d["cwq"] = f(np.transpose(mc[:, :, 64 * m:64 * m + 64], (0, 2, 1)))
        d["cwk"] = f(np.transpose(mc[:, :, 256 + 64 * m:256 + 64 * m + 64], (0, 2, 1)))
        d["ngv"] = f(np.asarray(inp['ml_norm_g'])[:L, 64 * m:64 * m + 64])
        d["bands"] = _band_tables((2, 4, 8, 16)[m])
        d["pw"] = f(np.asarray(inp['pool_w'])[:L, m])
        d["psc"] = f(np.asarray(inp['pool_scale'])[:L, 64 * m:64 * m + 64])
        pp, qq = np.meshgrid(np.arange(128), np.arange(8), indexing="ij")
        d["idxy"] = np.ascontiguousarray((m * 1024 + (qq // 2) * 256 + (qq % 2) * 128 + pp).astype(np.int32))
        wp = np.empty((L, WROWS, D), np.float32)
        w_br = np.asarray(inp['w_branch'], np.float32)
        pidx = np.arange(128)
        for l in range(L):
            for jj in range(2):
                kc = 2 * m + jj
                wp[l, jj * 512:(jj + 1) * 512] = w_in[l, kc * 128:(kc + 1) * 128, 2576:].reshape(512, D)
                wp[l, 1024 + jj * 128:1024 + (jj + 1) * 128] = inp['w_out'][l][kc * 128:(kc + 1) * 128]
                q = kc
                wp[l, 1280 + jj * 128:1280 + (jj + 1) * 128] = w_br[l, 2 * (q % 2) + pidx // 64, (q // 2) * 64 + pidx % 64, :]
            for k in range(4):
                e = 4 * m + k
                o = 1536 + k * 3072
                wp[l, o:o + 1024] = inp['w_e1'][l][e]
                wp[l, o + 1024:o + 2048] = inp['w_e3'][l][e]
                wp[l, o + 2048:o + 3072] = inp['w_e2'][l][e]
        d["wpack"] = wp
        d["bg"] = f(b_in[:, 2576:])
        d["ln1g"] = f(np.asarray(inp['ln1_g'])[:L])
        d["ln1b"] = f(np.asarray(inp['ln1_b'])[:L])
        d["wr"] = f(np.asarray(inp['w_router'])[:L])
        d["ln2g"] = f(np.asarray(inp['ln2_g'])[:L])
        d["ln2b"] = f(np.asarray(inp['ln2_b'])[:L])
        d["consts"] = consts
        maps.append(d)
    return maps


_NC_CACHE = {}


def kernel(**inputs):
    maps = _prep_inputs(inputs, DEPTH)
    if "nc" not in _NC_CACHE:
        _NC_CACHE["nc"] = build_program(DEPTH)
    res = run_bass_kernel_spmd(_NC_CACHE["nc"], maps, core_ids=list(range(NCORE)))
    outp = np.empty((2, S, D), np.float32)
    for c in range(NCORE):
        b, m = c // 4, c % 4
        outp[b, T * m:T * (m + 1), :] = np.asarray(res.results[c]["out"], np.float32)
    return outp
```

```python
import math
import os
from contextlib import ExitStack

import numpy as np
import ml_dtypes

import concourse.bass as bass
import concourse.mybir as mybir
from concourse.bass_utils import run_bass_kernel_spmd

F32 = mybir.dt.float32
BF16 = mybir.dt.bfloat16
I32 = mybir.dt.int32
AF = mybir.ActivationFunctionType
ALU = mybir.AluOpType

D = 1024
S = 8192
T = 2048
DEPTH = 4
NCORE = 8
ALPHA = (2 * DEPTH) ** 0.25
EPS = 1e-5
NEG = -30000.0
WROWS = 13824
HALO = 1024
CW = S + 2 * HALO
DILS = (1, 4, 16)


class Sched:
    def __init__(self, nc, stack):
        self.nc = nc
        self.stack = stack
        self.eng = {'sp': nc.sync, 'act': nc.scalar, 'pe': nc.tensor, 'dve': nc.vector, 'pool': nc.gpsimd}
        self.esem = {}
        self.eseq = {}
        for e in self.eng:
            self.esem[e] = stack.enter_context(nc.semaphore("es_" + e))
            self.eseq[e] = 0
        self.dsem = {}
        self.lastw = {}
        self.readers = {}
        self.waited = {e: {} for e in self.eng}
        self.sig = []
        self.nsem = 0
        self.bgkeys = set()
        self.marker = stack.enter_context(nc.sbuf_tensor("cc_marker", [1, 4], F32))

    def _dsem(self, key):
        if key not in self.dsem:
            self.nsem += 1
            self.dsem[key] = [self.stack.enter_context(self.nc.semaphore("ds%d" % self.nsem)), 0]
        return self.dsem[key]

    def _wait(self, eng, sem, val):
        w = self.waited[eng]
        k = id(sem)
        if w.get(k, (None, 0))[1] >= val:
            return
        w[k] = (sem, val)
        self.eng[eng].wait_ge(sem, val)

    def wait_bg(self, eng, key):
        for sw in (False, True):
            if (key, sw) in self.dsem:
                sem, cnt = self.dsem[(key, sw)]
                self._wait(eng, sem, cnt)

    def add(self, eng, fn, r=(), w=(), dma=None, inc=16, bg=False, cc=False):
        if dma is not None:
            dma = (dma, eng == 'pool')
        if bg:
            self.bgkeys.add(dma)
        deps = set()
        for k in r:
            if k in self.lastw:
                deps.add(self.lastw[k])
        for k in w:
            if k in self.lastw:
                deps.add(self.lastw[k])
            rd = self.readers.get(k)
            if rd:
                deps.update(rd[0].values())
                deps.update(rd[1])
        need = {}
        for d in deps:
            sem, val, deng, isdma = self.sig[d][:4]
            if (not isdma) and deng == eng and eng == 'pe':
                continue
            if isdma and len(self.sig[d]) > 4:
                val = self.sig[d][4][1]
            k = id(sem)
            if k not in need or need[k][1] < val:
                need[k] = (sem, val)
        for sem, val in need.values():
            self._wait(eng, sem, val)
        ins = fn(self.eng[eng])
        idx = len(self.sig)
        if cc:
            ds = self._dsem(dma)
            self.bgkeys.add(dma)
            ds[1] += 1
            ins.then_inc(ds[0])
            if not bg:
                self._wait('pool', ds[0], ds[1])
                mk = self.eng['pool'].memset(self.marker[0:1, 0:1], 0.0)
                self.eseq['pool'] += 1
                mk.then_inc(self.esem['pool'], 1)
                self.sig.append((self.esem['pool'], self.eseq['pool'], 'pool', False))
            else:
                self.sig.append((ds[0], ds[1], eng, True))
        elif dma is None:
            self.eseq[eng] += 1
            sem = self.esem[eng]
            ins.then_inc(sem, 1)
            self.sig.append((sem, self.eseq[eng], eng, False))
        else:
            ds = self._dsem(dma)
            ds[1] += inc
            if inc == 1:
                ins.then_inc(ds[0])
            else:
                ins.then_inc(ds[0], inc)
            self.sig.append((ds[0], ds[1], eng, True, ds))
        for k in w:
            self.lastw[k] = idx
            self.readers[k] = [{}, []]
        for k in r:
            rd = self.readers.setdefault(k, [{}, []])
            if dma is None:
                rd[0][eng] = idx
            else:
                rd[1].append(idx)
        return idx

    def barrier(self):
        for e in self.eng:
            for e2 in self.eng:
                if self.eseq[e2] > 0:
                    self._wait(e, self.esem[e2], self.eseq[e2])
            for k, (sem, cnt) in self.dsem.items():
                if cnt > 0 and k not in self.bgkeys:
                    self._wait(e, sem, cnt)
        self.lastw.clear()
        self.readers.clear()


def build_program(depth=DEPTH, debug=None):
    nc = bass.Bass("TRN2", target_bir_lowering=False)
    L = depth

    def din(name, shape, dt=F32):
        return nc.dram_tensor(name, list(shape), dt, kind="ExternalInput").ap()

    xT_in = din("xT", [D, T])
    whf = din("whf", [L, D, 384])
    bhf = din("bhf", [L, 384])
    wht = din("wht", [L, D, 452])
    bht = din("bht", [L, 452])
    fbias = din("fbias", [L, 2])
    wsT = din("wsT", [L, 128, 128])
    bsv = din("bsv", [L, 128])
    lng = din("lng", [L, 64])
    bmat = din("bmat", [6, 128, 128])
    cwq = din("cwq", [L, 64, 3])
    cwk = din("cwk", [L, 64, 3])
    ngv = din("ngv", [L, 64])
    bands = din("bands", [5, 128, 128])
    pw = din("pw", [L, 64, 64])
    psc = din("psc", [L, 64])
    idxy = din("idxy", [128, 8], I32)
    wpack = din("wpack", [L, WROWS, D])
    bg = din("bg", [L, 4096])
    ln1g = din("ln1g", [L, D])
    ln1b = din("ln1b", [L, D])
    wr = din("wr", [L, D, 16])
    ln2g = din("ln2g", [L, D])
    ln2b = din("ln2b", [L, D])
    consts = din("consts", [8, 128, 128])
    out = nc.dram_tensor("out", [T, D], F32, kind="ExternalOutput").ap()
    dbg = None
    if debug is not None:
        dbg = nc.dram_tensor("dbg", list(debug[1]), debug[2], kind="ExternalOutput").ap()

    snd_x = nc.dram_tensor("snd_x", [D, T], BF16)
    rcv_x = [nc.dram_tensor("rcv_x%d" % i, [4 * 256, T], BF16) for i in range(4)]
    snd_y = nc.dram_tensor("snd_y", [4 * 256, T], BF16)
    rcv_y = nc.dram_tensor("rcv_y", [4 * 1024, T], BF16)
    snd_a = nc.dram_tensor("snd_a", [16, T], F32)
    rcv_a = nc.dram_tensor("rcv_a", [64, T], F32)
    xres = nc.dram_tensor("xres", [D, T], F32)
    CHR = 512
    NCH = WROWS // CHR
    snd_w = [nc.dram_tensor("snd_w%d" % l, [WROWS, D], BF16) for l in range(L)]
    rcv_w = [[nc.dram_tensor("rcv_w%d_%d" % (l, i), [4 * CHR, D], BF16) for i in range(NCH)] for l in range(L)]
    G4 = [[0, 1, 2, 3], [4, 5, 6, 7]]
    G8 = [list(range(8))]

    with ExitStack() as top:
        sc = Sched(nc, top)
        A = sc.add

        def rstd_op(out_ap, in_ap, rk, wk):
            A('act', lambda e: e.activation(out=out_ap, in_=in_ap, func=AF.Sqrt, bias=EPS), r=rk, w=wk)
            A('dve', lambda e: e.reciprocal(out=out_ap, in_=out_ap), r=wk, w=wk)

        uniq = [0]

        def sb(stack, name, shape, dt):
            uniq[0] += 1
            return stack.enter_context(nc.sbuf_tensor("%s_%d" % (name, uniq[0]), list(shape), dt))

        psb = [top.enter_context(nc.psum_tensor("psb%d" % i, [128, 512], F32)) for i in range(7)]
        pst = top.enter_context(nc.psum_tensor("pst", [128, 1024], BF16))
        pscount = [0]

        def nextps():
            i = pscount[0] % 7
            pscount[0] += 1
            return psb[i], ("ps", i)

        cst = sb(top, "cst", [128, 8, 128], F32)
        A('sp', lambda e: e.dma_start(out=cst[:], in_=consts.rearrange("k p f -> p k f")), w=["cst"], dma="cst")
        ident_f = cst[:, 0, :]
        tri = [cst[:, 1, :], cst[:, 2, :]]
        mneg = [cst[:, 3, :], cst[:, 4, :]]
        blk = cst[:, 5, :]
        ones_f = cst[:, 6, :]
        kb = cst[:, 7, 0:3]
        cstb = sb(top, "cstb", [128, 2, 128], BF16)
        A('dve', lambda e: e.tensor_copy(out=cstb[:, 0, :], in_=cst[:, 0, :]), r=["cst"], w=["cstb0"])
        A('dve', lambda e: e.tensor_copy(out=cstb[:, 1, :], in_=cst[:, 6, :]), r=["cst"], w=["cstb1"])
        ident_b = cstb[:, 0, :]
        ones_b = cstb[:, 1, :]
        CK = ["cst", "cstb0", "cstb1"]

        if debug is not None and debug[0] == "tm1":
            sc.barrier()
            A('sp', lambda e: e.dma_start(out=dbg, in_=cstb[:, 0, :]), dma="dbg")
            sc.barrier()
            return nc
        def bounce_weights(l):
            for i in range(8):
                A('pool', lambda e, i=i: e.dma_start(out=snd_w[l].ap()[i * 1728:(i + 1) * 1728, :], in_=wpack[l][i * 1728:(i + 1) * 1728, :]),
                  dma=("sndw", l), bg=True)

        def gather_weights(l):
            sc.wait_bg('pool', ("sndw", l))
            for i in range(NCH):
                A('pool', lambda e, i=i: e.collective_compute("AllGather", ALU.bypass, replica_groups=G4,
                                                              ins=[snd_w[l].ap()[i * CHR:(i + 1) * CHR, :].opt()],
                                                              outs=[rcv_w[l][i].ap().opt()]),
                  dma=("ccw", l), inc=1, bg=True, cc=True)

        def wrows(l, rank, row0, n=128):
            i = row0 // CHR
            assert (row0 % CHR) + n <= CHR
            sem, _ = sc.dsem[((("ccw", l)), True)]
            sc._wait('pool', sem, i + 1)
            o = rank * CHR + row0 % CHR
            return rcv_w[l][i].ap()[o:o + n, :]

        A('pool', lambda e: e.dma_start(out=snd_x.ap(), in_=xT_in), w=["snd_x"], dma="snd_x")

        if debug is not None and debug[0] == "tm2":
            sc.barrier()
            A('pool', lambda e: e.dma_start(out=dbg, in_=snd_x.ap()[0:128, 0:128]), dma="dbg")
            sc.barrier()
            return nc
        for l in range(L):
            xsrc = xT_in if l == 0 else xres.ap()
            last = (l == L - 1)
            for i in range(4):
                A('pool', lambda e, i=i: e.collective_compute("AllGather", ALU.bypass, replica_groups=G4,
                                                              ins=[snd_x.ap()[i * 256:(i + 1) * 256, :].opt()], outs=[rcv_x[i].ap().opt()]),
                  r=["snd_x"], w=[("rcv_x", i)], dma="ccx", inc=1, cc=True)

            if debug is not None and debug[0] == "t0":
                sc.barrier()
                A('pool', lambda e: e.dma_start(out=dbg, in_=rcv_x[0].ap()[0:128, 0:128]), dma="dbg")
                sc.barrier()
                return nc
            with ExitStack() as pb:
                whf_sb = sb(pb, "whf_sb", [128, 8, 384], BF16)
                wht_sb = sb(pb, "wht_sb", [128, 8, 452], BF16)
                bhf_sb = sb(pb, "bhf_sb", [128, 3], F32)
                bht_bc = sb(pb, "bht_bc", [128, 452], F32)
                fb_bc = sb(pb, "fb_bc", [128, 2], F32)
                wsT_sb = sb(pb, "wsT_sb", [128, 128], BF16)
                bs_bc = sb(pb, "bs_bc", [64, 128], F32)
                lng_sb = sb(pb, "lng_sb", [64, 1], F32)
                bands_sb = sb(pb, "bands_sb", [128, 5, 128], BF16)
                pw_sb = sb(pb, "pw_sb", [64, 64], BF16)
                psc_sb = sb(pb, "psc_sb", [64, 1], F32)
                B_sb = sb(pb, "B_sb", [128, 6, 128], BF16)
                cw_sb = sb(pb, "cw_sb", [64, 2, 3], F32)
                ng_bc = sb(pb, "ng_bc", [128, 64], F32)
                C0 = sb(pb, "C0", [128, S], BF16)
                C1 = sb(pb, "C1", [128, CW], BF16)
                C2 = sb(pb, "C2", [128, CW], BF16)
                vaug = sb(pb, "vaug", [128, 64, 65], BF16)
                co_sb = sb(pb, "co_sb", [128, 64, 64], BF16)
                co_f = sb(pb, "co_f", [128, 64, 64], F32)
                gates = sb(pb, "gates", [128, 64, 4], F32)
                dx_sb = sb(pb, "dx_sb", [128, 64, 64], BF16)

                A('pool', lambda e: e.dma_start(out=whf_sb[:], in_=whf[l].rearrange("(c p) n -> p c n", p=128)), w=["whf"], dma="setup")
                A('pool', lambda e: e.dma_start(out=wht_sb[:], in_=wht[l].rearrange("(c p) n -> p c n", p=128)), w=["wht"], dma="setup")
                with nc.allow_non_contiguous_dma(reason="tiny bias loads"):
                    A('sp', lambda e: e.dma_start(out=bhf_sb[:], in_=bhf[l].rearrange("(c p) -> p c", p=128)), w=["bhf"], dma="setup")
                    A('sp', lambda e: e.dma_start(out=lng_sb[:], in_=lng[l].rearrange("(p o) -> p o", o=1)), w=["lng"], dma="setup")
                    A('sp', lambda e: e.dma_start(out=psc_sb[:], in_=psc[l].rearrange("(p o) -> p o", o=1)), w=["psc"], dma="setup")
                A('sp', lambda e: e.dma_start(out=bht_bc[:], in_=bht[l].partition_broadcast(128)), w=["bht"], dma="setup")
                A('sp', lambda e: e.dma_start(out=fb_bc[:], in_=fbias[l].partition_broadcast(128)), w=["fb"], dma="setup")
                A('sp', lambda e: e.dma_start(out=bs_bc[:], in_=bsv[l].partition_broadcast(64)), w=["bs"], dma="setup")
                A('sp', lambda e: e.dma_start(out=ng_bc[:], in_=ngv[l].partition_broadcast(128)), w=["ng"], dma="setup")
                A('pool', lambda e: e.dma_start(out=wsT_sb[:], in_=wsT[l]), w=["wsT"], dma="setup")
                A('pool', lambda e: e.dma_start(out=bands_sb[:], in_=bands.rearrange("k p f -> p k f")), w=["bands"], dma="setup")
                A('pool', lambda e: e.dma_start(out=pw_sb[:], in_=pw[l]), w=["pw"], dma="setup")
                A('pool', lambda e: e.dma_start(out=B_sb[:], in_=bmat.rearrange("k p f -> p k f")), w=["B"], dma="setup")
                A('sp', lambda e: e.dma_start(out=cw_sb[:, 0, :], in_=cwq[l]), w=["cwq"], dma="setup")
                A('sp', lambda e: e.dma_start(out=cw_sb[:, 1, :], in_=cwk[l]), w=["cwk"], dma="setup")
                if debug is not None and debug[0] == "t1":
                    sc.barrier()
                    A('pool', lambda e: e.dma_start(out=dbg, in_=rcv_x[0].ap()[0:128, 0:128]), dma="dbg")
                    sc.barrier()
                    return nc
                A('pool', lambda e: e.memset(C1[:], 0.0), w=["C1"])
                A('pool', lambda e: e.memset(C2[:], 0.0), w=["C2"])
                A('pool', lambda e: e.memset(vaug[:], 1.0), w=["vaug"])
                if debug is not None and debug[0] == "s0":
                    sc.barrier()
                    A('pool', lambda e: e.dma_start(out=dbg, in_=rcv_x[0].ap()[0:128, 0:128]), dma="dbg")
                    sc.barrier()
                    return nc
                if l == 0:
                    bounce_weights(0)
                    gather_weights(0)
                if l + 1 < L:
                    bounce_weights(l + 1)
                    gather_weights(l + 1)
                sc.barrier()
                A('dve', lambda e: e.tensor_tensor(out=bht_bc[:, 385:388:2], in0=bht_bc[:, 385:388:2], in1=fb_bc[:], op=ALU.add),
                  r=["fb", "bht"], w=["bht"])
                sc.barrier()

                SKIPB = bool(os.environ.get("K_SKIPB"))
                if SKIPB:
                    for i4 in range(4):
                        A('pool', lambda e, i4=i4: e.dma_start(out=snd_y.ap()[i4 * 256:(i4 + 1) * 256, :], in_=xT_in[0:256, :]), dma="setup")
                    sc.barrier()
                if not SKIPB:
                    with ExitStack() as pj:
                        xblk = [sb(pj, "xblk%d" % i, [128, 8, 512], BF16) for i in range(2)]
                        ublk = [sb(pj, "ublk%d" % i, [64, 512], F32) for i in range(2)]
                        tm = [sb(pj, "tm%d" % i, [128, 452], F32) for i in range(8)]
                        vg = [sb(pj, "vg%d" % i, [128, 256], F32) for i in range(8)]
                        st6 = [sb(pj, "st6%d" % i, [128, 6], F32) for i in range(8)]
                        mv = [sb(pj, "mv%d" % i, [128, 2], F32) for i in range(8)]
                        rstd = [sb(pj, "rstd%d" % i, [128, 1], F32) for i in range(8)]
                        vhat = [sb(pj, "vhat%d" % i, [128, 64], BF16) for i in range(8)]
                        t1 = [sb(pj, "t1%d" % i, [64, 128], F32) for i in range(8)]
                        ya = [sb(pj, "ya%d" % i, [64, 128], BF16) for i in range(8)]
                        pooled = [sb(pj, "pooled%d" % i, [64, 128], BF16) for i in range(4)]
                        yd = [sb(pj, "yd%d" % i, [64, 128], BF16) for i in range(4)]

                        pool_ps = {}

                        def pool_a(i):
                            ps, pk = nextps()
                            pool_ps[i] = (ps, pk)
                            srcs = []
                            if i > 0:
                                srcs.append((i - 1, 0))
                            srcs.append((i, 3 if i == 0 else (4 if i == 63 else 1)))
                            if i < 63:
                                srcs.append((i + 1, 2))
                            for n_, (j_, kind) in enumerate(srcs):
                                A('pe', lambda e, j_=j_, kind=kind, n_=n_: e.matmul(ps[0:64, 0:128], dx_sb[:, j_, :], bands_sb[:, kind, :],
                                                                                      start=(n_ == 0), stop=(n_ == len(srcs) - 1)),
                                  r=[("dx", j_)], w=[pk])

                        def pool_b(i):
                            s2 = i % 4
                            ps, pk = pool_ps[i]
                            A('dve', lambda e: e.tensor_copy(out=pooled[s2][:], in_=ps[0:64, 0:128]), r=[pk], w=[("pooled", s2)])
                            ps2, pk2 = nextps()
                            pool_ps[i] = (ps2, pk2)
                            A('pe', lambda e: e.matmul(ps2[0:64, 0:128], pw_sb[:], pooled[s2][:], start=True, stop=True),
                              r=[("pooled", s2)], w=[pk2])

                        def pool_c(i):
                            s2 = i % 4
                            ps2, pk2 = pool_ps.pop(i)
                            A('dve', lambda e: e.tensor_scalar_mul(out=yd[s2][:], in0=ps2[0:64, 0:128], scalar1=psc_sb[:, 0:1]),
                              r=[pk2], w=[("yd", s2)])
                            A('sp', lambda e: e.dma_start(out=snd_y.ap()[(i // 16) * 256 + 192:(i // 16) * 256 + 256, (i % 16) * 128:(i % 16 + 1) * 128], in_=yd[s2][:]),
                              r=[("yd", s2)], w=[("sndy", 3, i)], dma=("ydo", s2 % 2))

                        def stage1(i, s_, tt, xbk):
                            s2 = i % 8
                            ps, pk = nextps()
                            for k in range(8):
                                A('pe', lambda e, k=k: e.matmul(ps[:, 0:452], xblk[s_][:, k, tt * 128:(tt + 1) * 128], wht_sb[:, k, :],
                                                                  start=(k == 0), stop=(k == 7)),
                                  r=xbk, w=[pk])
                            A('dve', lambda e: e.tensor_tensor(out=tm[s2][:], in0=ps[:, 0:452], in1=bht_bc[:], op=ALU.add),
                              r=[pk], w=[("tm", s2)])
                            A('act', lambda e: e.activation(out=vg[s2][:], in_=tm[s2][:, 0:256], func=AF.Gelu), r=[("tm", s2)], w=[("vg", s2)])
                            A('pool', lambda e: e.tensor_copy(out=vaug[:, i, 0:64], in_=tm[s2][:, 256:320]), r=[("tm", s2)], w=[("vaug", i)])
                            A('pool', lambda e: e.tensor_copy(out=co_f[:, i, :], in_=tm[s2][:, 320:384]), r=[("tm", s2)], w=[("co", i)])
                            A('pool', lambda e: e.tensor_copy(out=gates[:, i, :], in_=tm[s2][:, 384:388]), r=[("tm", s2)], w=[("gates", i)])
                            A('pool', lambda e: e.tensor_copy(out=dx_sb[:, i, :], in_=tm[s2][:, 388:452]), r=[("tm", s2)], w=[("dx", i)])

                        def stage2a(i):
                            s2 = i % 8
                            A('dve', lambda e: e.bn_stats(out=st6[s2][:], in_=vg[s2][:]), r=[("vg", s2)], w=[("st6", s2)])
                            A('dve', lambda e: e.bn_aggr(out=mv[s2][:], in_=st6[s2][:]), r=[("st6", s2)], w=[("mv", s2)])
                            A('act', lambda e: e.activation(out=rstd[s2][:], in_=mv[s2][:, 1:2], func=AF.Sqrt, bias=EPS), r=[("mv", s2)], w=[("rstd", s2)])

                        def stage2b(i):
                            s2 = i % 8
                            A('dve', lambda e: e.reciprocal(out=rstd[s2][:], in_=rstd[s2][:]), r=[("rstd", s2)], w=[("rstd", s2)])
                            A('dve', lambda e: e.tensor_scalar(out=vhat[s2][:], in0=vg[s2][:, 0:64], scalar1=mv[s2][:, 0:1],
                                                               scalar2=rstd[s2][:, 0:1], op0=ALU.subtract, op1=ALU.mult),
                              r=[("vg", s2), ("mv", s2), ("rstd", s2)], w=[("vhat", s2)])

                        gm_ps = {}

                        def stage3a(i):
                            s2 = i % 8
                            psg, pkg = nextps()
                            gm_ps[i] = (psg, pkg)
                            A('pe', lambda e: e.matmul(psg[0:64, 0:128], vhat[s2][:], wsT_sb[:], start=True, stop=True),
                              r=[("vhat", s2)], w=[pkg])

                        def stage3b(i):
                            s2 = i % 8
                            sb_ = (i // 4) % 2
                            tt = i % 4
                            psg, pkg = gm_ps.pop(i)
                            A('dve', lambda e: e.scalar_tensor_tensor(out=t1[s2][:], in0=psg[0:64, 0:128], scalar=lng_sb[:, 0:1],
                                                                      in1=bs_bc[:], op0=ALU.mult, op1=ALU.add), r=[pkg], w=[("t1", s2)])
                            A('dve', lambda e: e.tensor_tensor(out=ya[s2][:], in0=t1[s2][:], in1=ublk[sb_][:, tt * 128:(tt + 1) * 128], op=ALU.mult),
                              r=[("t1", s2), ("ublk", sb_)], w=[("ya", s2)])
                            A('sp', lambda e: e.dma_start(out=snd_y.ap()[(i // 16) * 256:(i // 16) * 256 + 64, (i % 16) * 128:(i % 16 + 1) * 128], in_=ya[s2][:]),
                              r=[("ya", s2)], w=[("sndy", 0, i)], dma=("yao", s2 % 2))

                        def later_stages(step):
                            for lag, fn in ((4, stage3b), (3, pool_c), (3, stage3a), (2, pool_b), (2, stage2b), (1, stage2a)):
                                idx = step - lag
                                if 0 <= idx < 64:
                                    fn(idx)
                            if 1 <= step <= 64:
                                pool_a(step - 1)

                        for j in range(16):
                            s_ = j % 2
                            rk, off = j // 4, (j % 4) * 512
                            for i4 in range(4):
                                A('sp', lambda e, i4=i4: e.dma_start(out=xblk[s_][:, 2 * i4:2 * i4 + 2, :],
                                                                     in_=rcv_x[i4].ap()[rk * 256:(rk + 1) * 256, off:off + 512].rearrange("(c p) t -> p c t", p=128)),
                                  w=[("xblk", s_, i4)], dma=("xblk", s_))
                            xbk = [("xblk", s_, i4) for i4 in range(4)]
                            for ci in range(3):
                                ps, pk = nextps()
                                for k in range(8):
                                    A('pe', lambda e, k=k: e.matmul(ps[:, 0:512], whf_sb[:, k, ci * 128:(ci + 1) * 128], xblk[s_][:, k, :],
                                                                      start=(k == 0), stop=(k == 7)),
                                      r=xbk, w=[pk])
                                if ci == 0:
                                    A('act', lambda e: e.activation(out=ublk[s_][:], in_=ps[0:64, 0:512], func=AF.Gelu, bias=bhf_sb[0:64, 0:1]),
                                      r=[pk], w=[("ublk", s_)])
                                    A('dve', lambda e: e.tensor_scalar(out=C0[64:128, j * 512:(j + 1) * 512], in0=ps[64:128, 0:512],
                                                                       scalar1=bhf_sb[64:128, 0:1], scalar2=0.125, op0=ALU.add, op1=ALU.mult),
                                      r=[pk], w=[("C0", j)])
                                else:
                                    Cx = C1 if ci == 1 else C2
                                    A('dve', lambda e, Cx=Cx: e.tensor_scalar(out=Cx[:, HALO + j * 512:HALO + (j + 1) * 512], in0=ps[:, 0:512],
                                                                               scalar1=bhf_sb[:, ci:ci + 1], scalar2=None, op0=ALU.add),
                                      r=[pk], w=[("C%d" % ci, j)])
                            for tt in range(4):
                                i = 4 * j + tt
                                stage1(i, s_, tt, xbk)
                                later_stages(i)
                        for step in range(64, 69):
                            later_stages(step)
                        gz = sb(pj, "gz", [128, 64, 2], F32)
                        ge_ = sb(pj, "ge_", [128, 64, 2], F32)
                        sc.barrier()
                        A('act', lambda e: e.activation(out=co_sb[:], in_=co_f[:], func=AF.Sigmoid), w=["co_all"])
                        A('act', lambda e: e.activation(out=gz[:], in_=gates[:, :, 1:4:2], func=AF.Abs), w=["gz"])
                        A('act', lambda e: e.activation(out=ge_[:], in_=gz[:], func=AF.Exp, scale=-1.0), r=["gz"], w=["ge"])
                        A('act', lambda e: e.activation(out=ge_[:], in_=ge_[:], func=AF.Ln, bias=1.0), r=["ge"], w=["ge"])
                        A('dve', lambda e: e.tensor_scalar_min(out=gz[:], in0=gates[:, :, 1:4:2], scalar1=0.0), r=["gz"], w=["gz"])
                        A('dve', lambda e: e.tensor_tensor(out=gates[:, :, 1:4:2], in0=gz[:], in1=ge_[:], op=ALU.subtract), r=["gz", "ge"], w=["gates"])
                        sc.barrier()

                    if debug is not None and debug[0] == "s1":
                        A('pool', lambda e: e.dma_start(out=dbg, in_=snd_y.ap()), dma="dbg")
                        sc.barrier()
                        return nc
                    with ExitStack() as pa:
                        vtok = sb(pa, "vtok", [128, 69, 64], BF16)
                        acc = sb(pa, "acc", [64, 2, T], F32)
                        pT = [sb(pa, "pT%d" % i, [128, 256], BF16) for i in range(3)]
                        rec = sb(pa, "rec", [64, T], F32)
                        yb = sb(pa, "yb", [64, T], BF16)
                        ptc = [0]
                        for sbk in range(4):
                            base = T * sbk
                            tiles = []
                            for p_, dil in enumerate(DILS):
                                nq = T // dil // 128
                                for r_ in range(dil):
                                    for m_ in range(nq + 1):
                                        tiles.append((p_, r_, m_, HALO + base + r_ + dil * (128 * m_ - 64), dil))
                            tix = {}
                            for g0 in range(0, len(tiles), 16):
                                grp = tiles[g0:g0 + 16]
                                for n_, (p_, r_, m_, c0, dil) in enumerate(grp):
                                    tix[(p_, r_, m_)] = g0 + n_
                                    A('pe', lambda e, n_=n_, c0=c0, dil=dil: e.transpose(pst[:, n_ * 64:(n_ + 1) * 64],
                                                                                           C2[64:128, c0:c0 + 127 * dil + 1:dil], ident_b[64:128, 64:128]),
                                      w=["pst"])
                                A('dve', lambda e, g0=g0, grp=grp: e.tensor_copy(out=vtok[:, g0:g0 + len(grp), :],
                                                                                 in_=pst[:, 0:len(grp) * 64].rearrange("p (a b) -> p a b", b=64)),
                                  r=["pst"], w=[("vtok", g0)])
                            for p_, dil in enumerate(DILS):
                                nq = T // dil // 128
                                for r_ in range(dil):
                                    for a_ in range(nq):
                                        q0 = base + r_ + dil * 128 * a_
                                        ps, pk = nextps()
                                        pt = pT[ptc[0] % 3]
                                        ptk = ("pT", ptc[0] % 3)
                                        ptc[0] += 1
                                        for h_ in range(2):
                                            kp0 = base + r_ + dil * (128 * a_ - 64 + 128 * h_)
                                            kc0 = HALO + kp0
                                            A('pe', lambda e, h_=h_, kc0=kc0: e.matmul(ps[:, h_ * 128:(h_ + 1) * 128],
                                                                                        C1[64:128, kc0:kc0 + 127 * dil + 1:dil],
                                                                                        C0[64:128, q0:q0 + 127 * dil + 1:dil], start=True, stop=False),
                                              w=[pk])
                                            A('pe', lambda e, h_=h_: e.matmul(ps[:, h_ * 128:(h_ + 1) * 128], ident_b, B_sb[:, p_ * 2 + h_, :],
                                                                              start=False, stop=True), w=[pk])
                                            kbc = 1 if kp0 < 0 else (2 if kp0 + 127 * dil >= S else 0)
                                            A('act', lambda e, h_=h_, kbc=kbc: e.activation(out=pt[:, h_ * 128:(h_ + 1) * 128],
                                                                                             in_=ps[:, h_ * 128:(h_ + 1) * 128], func=AF.Exp,
                                                                                             bias=kb[:, kbc:kbc + 1]),
                                              r=[pk], w=[ptk])
                                        pn, pnk = nextps()
                                        for h_ in range(2):
                                            ti = tix[(p_, r_, a_ + h_)]
                                            A('pe', lambda e, h_=h_, ti=ti: e.matmul(pn[0:64, 0:128], vtok[:, ti, :], pt[:, h_ * 128:(h_ + 1) * 128],
                                                                                      start=(h_ == 0), stop=(h_ == 1)),
                                              r=[ptk, ("vtok", (ti // 16) * 16)], w=[pnk])
                                        for h_ in range(2):
                                            A('pe', lambda e, h_=h_: e.matmul(pn[0:64, 128:256], ones_b[:, 0:64], pt[:, h_ * 128:(h_ + 1) * 128],
                                                                              start=(h_ == 0), stop=(h_ == 1)), r=[ptk], w=[pnk])
                                        lo = r_ + dil * 128 * a_
                                        accv = acc[:, :, lo:lo + 127 * dil + 1:dil]
                                        pnv = pn[0:64, 0:256].rearrange("p (a b) -> p a b", a=2)
                                        if p_ == 0:
                                            A('dve', lambda e, accv=accv, pnv=pnv: e.tensor_copy(out=accv, in_=pnv), r=[pnk], w=["acc"])
                                        else:
                                            A('dve', lambda e, accv=accv, pnv=pnv: e.tensor_tensor(out=accv, in0=accv, in1=pnv, op=ALU.add),
                                              r=[pnk, "acc"], w=["acc"])
                            A('dve', lambda e: e.reciprocal(out=rec[:], in_=acc[:, 1, :]), r=["acc"], w=["rec"])
                            A('dve', lambda e: e.tensor_tensor(out=yb[:], in0=acc[:, 0, :], in1=rec[:], op=ALU.mult), r=["acc", "rec"], w=["yb"])
                            A('sp', lambda e: e.dma_start(out=snd_y.ap()[sbk * 256 + 64:sbk * 256 + 128, :], in_=yb[:]), r=["yb"], w=[("sndy", 1, sbk)], dma="ybo")
                        sc.barrier()

                    if debug is not None and debug[0] == "s2":
                        A('pool', lambda e: e.dma_start(out=dbg, in_=snd_y.ap()), dma="dbg")
                        sc.barrier()
                        return nc
                    with ExitStack() as pm:
                        cq = sb(pm, "cq", [64, S], BF16)
                        ck = sb(pm, "ck", [64, S], BF16)
                        ktok = sb(pm, "ktok", [128, 64, 64], BF16)
                        hf = sb(pm, "hf", [128, 64, 64], F32)
                        cvt = [sb(pm, "cvt%d" % i, [64, T], F32) for i in range(2)]
                        for pc in range(4):
                            for wi, (Cx, dst) in enumerate(((C1, cq), (C2, ck))):
                                c0 = HALO + pc * T
                                tv = cvt[wi]
                                tk = ("cvt", wi)
                                A('dve', lambda e, Cx=Cx, tv=tv: e.tensor_scalar_mul(out=tv[:], in0=Cx[0:64, c0:c0 + T], scalar1=cw_sb[:, wi, 1:2]),
                                  w=[tk])
                                A('dve', lambda e, Cx=Cx, tv=tv: e.scalar_tensor_tensor(out=tv[:], in0=Cx[0:64, c0 - 1:c0 - 1 + T], scalar=cw_sb[:, wi, 0:1],
                                                                                        in1=tv[:], op0=ALU.mult, op1=ALU.add), r=[tk], w=[tk])
                                A('dve', lambda e, Cx=Cx, tv=tv: e.scalar_tensor_tensor(out=tv[:], in0=Cx[0:64, c0 + 1:c0 + 1 + T], scalar=cw_sb[:, wi, 2:3],
                                                                                        in1=tv[:], op0=ALU.mult, op1=ALU.add), r=[tk], w=[tk])
                                if wi == 0:
                                    A('act', lambda e, tv=tv, dst=dst: e.activation(out=dst[:, pc * T:(pc + 1) * T], in_=tv[:], func=AF.Silu),
                                      r=[tk], w=[("cq", pc)])
                                else:
                                    A('act', lambda e, tv=tv: e.activation(out=tv[:], in_=tv[:], func=AF.Silu), r=[tk], w=[tk])
                                    A('pool', lambda e, tv=tv, dst=dst: e.tensor_scalar_mul(out=dst[:, pc * T:(pc + 1) * T], in0=tv[:], scalar1=0.125),
                                      r=[tk], w=[("ck", pc)])
                        sc.barrier()
                        for g0 in range(0, 64, 16):
                            for n_ in range(16):
                                c_ = g0 + n_
                                A('pe', lambda e, n_=n_, c_=c_: e.transpose(pst[:, n_ * 64:(n_ + 1) * 64], ck[0:64, c_ * 128:(c_ + 1) * 128],
                                                                            ident_b[0:64, 0:64]), w=["pst"])
                            A('dve', lambda e, g0=g0: e.tensor_copy(out=ktok[:, g0:g0 + 16, :], in_=pst[:, 0:1024].rearrange("p (a b) -> p a b", b=64)),
                              r=["pst"], w=["ktok"])
                        sc.barrier()

                        NS = 3
                        av_ = [sb(pm, "av%d" % i, [128, 4], F32) for i in range(NS)]
                        eG = [sb(pm, "eG%d" % i, [128, 1], F32) for i in range(NS)]
                        eD = [sb(pm, "eD%d" % i, [128, 128], F32) for i in range(NS)]
                        ST = [sb(pm, "ST%d" % i, [128, 128], BF16) for i in range(NS)]
                        kw = [sb(pm, "kw%d" % i, [128, 64], BF16) for i in range(NS)]
                        tI = [sb(pm, "tI%d" % i, [128, 65], F32) for i in range(NS)]
                        tot = [sb(pm, "tot%d" % i, [128, 65], F32) for i in range(NS)]
                        dd = [sb(pm, "dd%d" % i, [128, 2], F32) for i in range(NS)]
                        Cf = [sb(pm, "Cf%d" % i, [64, 65], F32) for i in range(2)]
                        Cb = [sb(pm, "Cb%d" % i, [64, 65], BF16) for i in range(2)]
                        hs = [sb(pm, "hs%d" % i, [128, 64], F32) for i in range(2)]
                        hst = [sb(pm, "hst%d" % i, [128, 6], F32) for i in range(2)]
                        hmv = [sb(pm, "hmv%d" % i, [128, 2], F32) for i in range(2)]
                        hr = [sb(pm, "hr%d" % i, [128, 1], F32) for i in range(2)]
                        hn = [sb(pm, "hn%d" % i, [128, 64], F32) for i in range(2)]
                        yct = [sb(pm, "yct%d" % i, [128, 64], BF16) for i in range(2)]
                        ycT = [sb(pm, "ycT%d" % i, [64, 128], BF16) for i in range(2)]
                        for d_ in range(2):
                            A('dve', lambda e, d_=d_: e.memset(Cf[d_][:], 0.0), w=[("Cf", d_)])
                            A('dve', lambda e, d_=d_: e.memset(Cb[d_][:], 0.0), w=[("Cb", d_)])
                        stp = [0]
                        for it in range(64):
                            for d_ in range(2):
                                c_ = it if d_ == 0 else 63 - it
                                s3 = stp[0] % NS
                                stp[0] += 1
                                lf_ = gates[:, c_, 2 * d_ + 1:2 * d_ + 2]
                                li_ = gates[:, c_, 2 * d_:2 * d_ + 1]
                                psA, pkA = nextps()
                                A('pe', lambda e, d_=d_, lf_=lf_: e.matmul(psA[:, 0:1], tri[d_], lf_, start=True, stop=True), w=[pkA])
                                A('pe', lambda e, lf_=lf_: e.matmul(psA[:, 1:2], ones_f, lf_, start=True, stop=True), w=[pkA])
                                psB, pkB = nextps()
                                A('pe', lambda e, d_=d_, lf_=lf_: e.matmul(psB[:, 0:128], lf_.to_broadcast([128, 128]), tri[d_], start=True, stop=False),
                                  w=[pkB])
                                A('pe', lambda e, d_=d_: e.matmul(psB[:, 0:128], ident_f, mneg[d_], start=False, stop=True), w=[pkB])
                                avs = av_[s3]
                                A('dve', lambda e, avs=avs, li_=li_: e.tensor_tensor(out=avs[:, 0:1], in0=li_, in1=psA[:, 0:1], op=ALU.subtract),
                                  r=[pkA], w=[("av", s3)])
                                A('dve', lambda e, avs=avs: e.tensor_tensor(out=avs[:, 1:2], in0=psA[:, 1:2], in1=avs[:, 0:1], op=ALU.add),
                                  r=[pkA, ("av", s3)], w=[("av", s3)])
                                A('act', lambda e, avs=avs: e.activation(out=eD[s3][:], in_=psB[:, 0:128], func=AF.Exp, bias=avs[:, 0:1]),
                                  r=[pkB, ("av", s3)], w=[("eD", s3)])
                                A('act', lambda e, avs=avs: e.activation(out=avs[:, 2:3], in_=psA[:, 0:1], func=AF.Exp), r=[pkA, ("av", s3)], w=[("av", s3)])
                                A('act', lambda e, avs=avs: e.activation(out=avs[:, 3:4], in_=avs[:, 1:2], func=AF.Exp), r=[("av", s3)], w=[("av", s3)])
                                A('act', lambda e: e.activation(out=eG[s3][:], in_=psA[:, 1:2], func=AF.Exp), r=[pkA], w=[("eG", s3)])
                                psS, pkS = nextps()
                                A('pe', lambda e, c_=c_: e.matmul(psS[:, 0:128], ck[:, c_ * 128:(c_ + 1) * 128], cq[:, c_ * 128:(c_ + 1) * 128],
                                                                   start=True, stop=True), w=[pkS])
                                A('dve', lambda e: e.tensor_tensor(out=ST[s3][:], in0=psS[:, 0:128], in1=eD[s3][:], op=ALU.mult),
                                  r=[pkS, ("eD", s3)], w=[("ST", s3)])
                                A('dve', lambda e, c_=c_, avs=avs: e.tensor_scalar_mul(out=kw[s3][:], in0=ktok[:, c_, :], scalar1=avs[:, 3:4]),
                                  r=[("av", s3)], w=[("kw", s3)])
                                psC, pkC = nextps()
                                A('pe', lambda e, c_=c_: e.matmul(psC[0:64, 0:65], kw[s3][:], vaug[:, c_, :], start=True, stop=True),
                                  r=[("kw", s3)], w=[pkC])
                                psO, pkO = nextps()
                                A('pe', lambda e, c_=c_: e.matmul(psO[:, 0:65], ST[s3][:], vaug[:, c_, :], start=True, stop=True),
                                  r=[("ST", s3)], w=[pkO])
                                A('pe', lambda e, c_=c_, d_=d_: e.matmul(psO[:, 65:130], cq[:, c_ * 128:(c_ + 1) * 128], Cb[d_][:], start=True, stop=True),
                                  r=[("Cb", d_)], w=[pkO])
                                A('dve', lambda e, avs=avs: e.tensor_scalar_mul(out=tI[s3][:], in0=psO[:, 65:130], scalar1=avs[:, 2:3]),
                                  r=[pkO, ("av", s3)], w=[("tI", s3)])
                                A('dve', lambda e: e.tensor_tensor(out=tot[s3][:], in0=psO[:, 0:65], in1=tI[s3][:], op=ALU.add),
                                  r=[pkO, ("tI", s3)], w=[("tot", s3)])
                                A('dve', lambda e, d_=d_: e.scalar_tensor_tensor(out=Cf[d_][:], in0=Cf[d_][:], scalar=eG[s3][0:64, 0:1], in1=psC[0:64, 0:65],
                                                                                 op0=ALU.mult, op1=ALU.add), r=[pkC, ("eG", s3), ("Cf", d_)], w=[("Cf", d_)])
                                A('pool', lambda e, d_=d_: e.tensor_copy(out=Cb[d_][:], in_=Cf[d_][:]), r=[("Cf", d_)], w=[("Cb", d_)])
                                A('dve', lambda e: e.scalar_tensor_tensor(out=dd[s3][:, 0:1], in0=tot[s3][:, 64:65], scalar=-1.0, in1=tot[s3][:, 64:65],
                                                                          op0=ALU.mult, op1=ALU.max), r=[("tot", s3)], w=[("dd", s3)])
                                A('dve', lambda e: e.tensor_scalar_max(out=dd[s3][:, 0:1], in0=dd[s3][:, 0:1], scalar1=1.0),
                                  r=[("dd", s3)], w=[("dd", s3)])
                                A('dve', lambda e: e.reciprocal(out=dd[s3][:, 1:2], in_=dd[s3][:, 0:1]), r=[("dd", s3)], w=[("dd", s3)])
                                first = (c_ < 32) == (d_ == 0)
                                if first:
                                    A('dve', lambda e, c_=c_: e.tensor_scalar_mul(out=hf[:, c_, :], in0=tot[s3][:, 0:64], scalar1=dd[s3][:, 1:2]),
                                      r=[("tot", s3), ("dd", s3)], w=[("hf", c_)])
                                else:
                                    h2 = c_ % 2
                                    A('dve', lambda e, c_=c_: e.scalar_tensor_tensor(out=hs[h2][:], in0=tot[s3][:, 0:64], scalar=dd[s3][:, 1:2],
                                                                                      in1=hf[:, c_, :], op0=ALU.mult, op1=ALU.add),
                                      r=[("tot", s3), ("dd", s3), ("hf", c_)], w=[("hs", h2)])
                                    A('dve', lambda e: e.bn_stats(out=hst[h2][:], in_=hs[h2][:]), r=[("hs", h2)], w=[("hst", h2)])
                                    A('dve', lambda e: e.bn_aggr(out=hmv[h2][:], in_=hst[h2][:]), r=[("hst", h2)], w=[("hmv", h2)])
                                    A('act', lambda e: e.activation(out=hr[h2][:], in_=hmv[h2][:, 1:2], func=AF.Ln, bias=EPS), r=[("hmv", h2)], w=[("hr", h2)])
                                    A('act', lambda e: e.activation(out=hr[h2][:], in_=hr[h2][:], func=AF.Exp, scale=-0.5), r=[("hr", h2)], w=[("hr", h2)])
                                    A('dve', lambda e: e.tensor_scalar(out=hn[h2][:], in0=hs[h2][:], scalar1=hmv[h2][:, 0:1], scalar2=hr[h2][:, 0:1],
                                                                       op0=ALU.subtract, op1=ALU.mult),
                                      r=[("hs", h2), ("hmv", h2), ("hr", h2)], w=[("hn", h2)])
                                    A('pool', lambda e: e.tensor_tensor(out=hn[h2][:], in0=hn[h2][:], in1=ng_bc[:], op=ALU.mult),
                                      r=[("hn", h2)], w=[("hn", h2)])
                                    A('pool', lambda e, c_=c_: e.tensor_tensor(out=yct[h2][:], in0=hn[h2][:], in1=co_sb[:, c_, :], op=ALU.mult),
                                      r=[("hn", h2)], w=[("yct", h2)])
                                    A('pe', lambda e: e.transpose(pst[0:64, 0:128], yct[h2][:], ident_b), r=[("yct", h2)], w=["pst"])
                                    A('dve', lambda e: e.tensor_copy(out=ycT[h2][:], in_=pst[0:64, 0:128]), r=["pst"], w=[("ycT", h2)])
                                    A('sp', lambda e, c_=c_: e.dma_start(out=snd_y.ap()[(c_ // 16) * 256 + 128:(c_ // 16) * 256 + 192, (c_ % 16) * 128:(c_ % 16 + 1) * 128], in_=ycT[h2][:]),
                                      r=[("ycT", h2)], w=[("sndy", 2, c_)], dma=("yco", h2))
                        sc.barrier()
            if debug is not None and debug[0] == "y":
                A('pool', lambda e: e.dma_start(out=dbg, in_=snd_y.ap()), dma="dbg")
                sc.barrier()
                return nc

            for i in range(4):
                A('pool', lambda e, i=i: e.collective_compute("AllGather", ALU.bypass, replica_groups=G4,
                                                              ins=[snd_y.ap()[i * 256:(i + 1) * 256, :].opt()],
                                                              outs=[rcv_y.ap()[i * 1024:(i + 1) * 1024, :].opt()]),
                  dma="ccy", inc=1, cc=True)
            sc.barrier()
            if debug is not None and debug[0] == "c1":
                sc.barrier()
                A('sp', lambda e: e.dma_start(out=dbg, in_=cstb[:, 0, :]), dma="dbg")
                sc.barrier()
                return nc

            with ExitStack() as pd:
                mgh = sb(pd, "mgh", [128, 8, T], BF16)
                with ExitStack() as p1:
                    yall = sb(p1, "yall", [128, 8, T], BF16)
                    xbo = sb(p1, "xbo", [128, 8, T], BF16)
                    wbr_sb = sb(p1, "wbr_sb", [128, 8, D], BF16)
                    idx_sb = sb(p1, "idx_sb", [128, 8], I32)
                    bg_sb = sb(p1, "bg_sb", [128, 4, 8], F32)
                    wgs = [sb(p1, "wgs%d" % i, [128, 8, 4, 128], BF16) for i in range(2)]
                    gs = [sb(p1, "gs%d" % i, [128, 512], F32) for i in range(2)]
                    macc = [sb(p1, "macc%d" % i, [128, 512], F32) for i in range(2)]
                    tmpm = [sb(p1, "tmpm%d" % i, [128, 512], F32) for i in range(2)]
                    A('sp', lambda e: e.dma_start(out=idx_sb[:], in_=idxy), dma="setup")
                    with nc.allow_non_contiguous_dma(reason="tiny bias load"):
                        for n_ in range(4):
                            A('sp', lambda e, n_=n_: e.dma_start(out=bg_sb[:, n_, :], in_=bg[l][n_ * D:(n_ + 1) * D].rearrange("(dc p) -> p dc", p=128)), dma="setup")
                    A('sp', lambda e: e.dma_start(out=xbo[:], in_=snd_x.ap().rearrange("(c p) t -> p c t", p=128)), dma="setup")
                    for q in range(8):
                        src_ = wrows(l, q // 2, 1280 + (q % 2) * 128)
                        A('pool', lambda e, q=q, src_=src_: e.dma_start(out=wbr_sb[:, q, :], in_=src_), dma="setup")
                    sc.barrier()
                    if debug is not None and debug[0] == "c2":
                        sc.barrier()
                        A('sp', lambda e: e.dma_start(out=dbg, in_=cstb[:, 0, :]), dma="dbg")
                        sc.barrier()
                        return nc
                    for q in range(8):
                        A('pool', lambda e, q=q: e.indirect_dma_start(out=yall[:, q, :], out_offset=None, in_=rcv_y.ap(),
                                                                      in_offset=bass.IndirectOffsetOnAxis(ap=idx_sb[:, q:q + 1], axis=0)),
                          dma=("yall", q))
                    sc.barrier()
                    if debug is not None and debug[0] == "c3":
                        sc.barrier()
                        A('sp', lambda e: e.dma_start(out=dbg, in_=cstb[:, 0, :]), dma="dbg")
                        sc.barrier()
                        return nc
                    cnt = [0]
                    for dc in range(8):
                        ws = dc % 2
                        wkeys = [("wgs", ws, kc) for kc in range(8)]
                        for kc in range(8):
                            src_ = wrows(l, kc // 2, (kc % 2) * 512, 512)
                            A('pool', lambda e, kc=kc, src_=src_: e.dma_start(
                                out=wgs[ws][:, kc, :, :],
                                in_=src_.rearrange("(p a) b -> p a b", a=4)[:, :, dc * 128:(dc + 1) * 128]),
                              w=[("wgs", ws, kc)], dma=("wgs", ws))
                        for tg in range(4):
                            ms = (dc * 4 + tg) % 2
                            for n_ in range(4):
                                c2 = cnt[0] % 2
                                cnt[0] += 1
                                psG, pkG = nextps()
                                for kc in range(8):
                                    A('pe', lambda e, kc=kc, n_=n_: e.matmul(psG[:, 0:512], wgs[ws][:, kc, n_, :], xbo[:, kc, tg * 512:(tg + 1) * 512],
                                                                              start=(kc == 0), stop=(kc == 7)), r=wkeys, w=[pkG])
                                A('act', lambda e, n_=n_: e.activation(out=gs[c2][:], in_=psG[:, 0:512], func=AF.Sigmoid, bias=bg_sb[:, n_, dc:dc + 1]),
                                  r=[pkG], w=[("gs", c2)])
                                psP, pkP = nextps()
                                pb_ = (n_ % 2) * 64
                                for r_ in range(4):
                                    q = 2 * r_ + n_ // 2
                                    A('pe', lambda e, q=q, r_=r_, pb_=pb_: e.matmul(psP[:, 0:512], wbr_sb[pb_:pb_ + 64, q, dc * 128:(dc + 1) * 128],
                                                                                     yall[pb_:pb_ + 64, q, tg * 512:(tg + 1) * 512],
                                                                                     start=(r_ == 0), stop=(r_ == 3)), w=[pkP])
                                if n_ == 0:
                                    A('dve', lambda e: e.tensor_tensor(out=macc[ms][:], in0=gs[c2][:], in1=psP[:, 0:512], op=ALU.mult),
                                      r=[("gs", c2), pkP], w=[("macc", ms)])
                                else:
                                    A('dve', lambda e: e.tensor_tensor(out=tmpm[c2][:], in0=gs[c2][:], in1=psP[:, 0:512], op=ALU.mult),
                                      r=[("gs", c2), pkP], w=[("tmpm", c2)])
                                    if n_ < 3:
                                        A('pool', lambda e: e.tensor_tensor(out=macc[ms][:], in0=macc[ms][:], in1=tmpm[c2][:], op=ALU.add),
                                          r=[("tmpm", c2), ("macc", ms)], w=[("macc", ms)])
                                    else:
                                        A('pool', lambda e: e.tensor_tensor(out=mgh[:, dc, tg * 512:(tg + 1) * 512], in0=macc[ms][:], in1=tmpm[c2][:], op=ALU.add),
                                          r=[("tmpm", c2), ("macc", ms)], w=[("mg", dc, tg)])
                    sc.barrier()
                if debug is not None and debug[0] == "mg":
                    for c in range(8):
                        A('pool', lambda e, c=c: e.dma_start(out=dbg[c * 128:(c + 1) * 128, :], in_=mgh[:, c, :]), dma="dbg")
                    sc.barrier()
                    return nc

                z = sb(pd, "z", [128, 8, T], F32)
                x1b = sb(pd, "x1b", [128, 8, T], BF16)
                lnp = sb(pd, "lnp", [128, 4, 8], F32)
                gm = sb(pd, "gm", [16, T], F32)
                with nc.allow_non_contiguous_dma(reason="tiny ln param loads"):
                    for n_, src in enumerate((ln1g, ln1b, ln2g, ln2b)):
                        A('sp', lambda e, n_=n_, src=src: e.dma_start(out=lnp[:, n_, :], in_=src[l].rearrange("(c p) -> p c", p=128)), dma="setup")

                def layer_norm_fm(stack, gi, tag, also_bf16):
                    sq = [sb(stack, "sq%s%d" % (tag, i), [128, 512], F32) for i in range(2)]
                    mean_ = sb(stack, "mean" + tag, [128, 512], F32)
                    msq = sb(stack, "msq" + tag, [128, 512], F32)
                    rs_ = sb(stack, "rs" + tag, [128, 512], F32)
                    for tg in range(4):
                        tsl = slice(tg * 512, (tg + 1) * 512)
                        ps1, pk1 = nextps()
                        ps2, pk2 = nextps()
                        for c in range(8):
                            A('pe', lambda e, c=c: e.matmul(ps1[:, 0:512], ones_f, z[:, c, tsl], start=(c == 0), stop=(c == 7)),
                              r=[("z", c, tg)], w=[pk1])
                        for c in range(8):
                            A('act', lambda e, c=c: e.activation(out=sq[c % 2][:], in_=z[:, c, tsl], func=AF.Square),
                              r=[("z", c, tg)], w=[("sq", c % 2)])
                            A('pe', lambda e, c=c: e.matmul(ps2[:, 0:512], ones_f, sq[c % 2][:], start=(c == 0), stop=(c == 7)),
                              r=[("sq", c % 2)], w=[pk2])
                        A('dve', lambda e: e.tensor_scalar_mul(out=mean_[:], in0=ps1[:, 0:512], scalar1=1.0 / D), r=[pk1], w=["mean"])
                        A('dve', lambda e: e.tensor_tensor(out=msq[:], in0=mean_[:], in1=mean_[:], op=ALU.mult), r=["mean"], w=["msq"])
                        A('dve', lambda e: e.scalar_tensor_tensor(out=rs_[:], in0=ps2[:, 0:512], scalar=1.0 / D, in1=msq[:],
                                                                  op0=ALU.mult, op1=ALU.subtract), r=[pk2, "msq"], w=["rs"])
                        rstd_op(rs_[:], rs_[:], ["rs"], ["rs"])
                        for c in range(8):
                            zk = ("z", c, tg)
                            A('dve', lambda e, c=c: e.tensor_tensor(out=z[:, c, tsl], in0=z[:, c, tsl], in1=mean_[:], op=ALU.subtract),
                              r=[zk, "mean"], w=[zk])
                            A('pool', lambda e, c=c: e.tensor_tensor(out=z[:, c, tsl], in0=z[:, c, tsl], in1=rs_[:], op=ALU.mult),
                              r=[zk, "rs"], w=[zk])
                            A('act', lambda e, c=c: e.activation(out=z[:, c, tsl], in_=z[:, c, tsl], func=AF.Identity,
                                                                 bias=lnp[:, gi + 1, c:c + 1], scale=lnp[:, gi, c:c + 1]), r=[zk], w=[zk])
                            if also_bf16:
                                A('pool', lambda e, c=c: e.tensor_copy(out=x1b[:, c, tsl], in_=z[:, c, tsl]), r=[zk], w=[("x1b", c, tg)])

                with ExitStack() as p2:
                    wo_sb = sb(p2, "wo_sb", [128, 8, D], BF16)
                    xr = [sb(p2, "xr%d" % i, [128, 512], F32) for i in range(3)]
                    for kc in range(8):
                        src_ = wrows(l, kc // 2, 1024 + (kc % 2) * 128)
                        A('pool', lambda e, kc=kc, src_=src_: e.dma_start(out=wo_sb[:, kc, :], in_=src_), dma="setup")
                    sc.barrier()
                    n3 = [0]
                    for ec in range(8):
                        for tg in range(4):
                            s3 = n3[0] % 3
                            n3[0] += 1
                            A('sp', lambda e: e.dma_start(out=xr[s3][:], in_=xsrc[ec * 128:(ec + 1) * 128, tg * 512:(tg + 1) * 512]),
                              w=[("xr", s3)], dma=("xr", s3))
                            psM, pkM = nextps()
                            for kc in range(8):
                                A('pe', lambda e, kc=kc: e.matmul(psM[:, 0:512], wo_sb[:, kc, ec * 128:(ec + 1) * 128], mgh[:, kc, tg * 512:(tg + 1) * 512],
                                                                    start=(kc == 0), stop=(kc == 7)), w=[pkM])
                            A('dve', lambda e: e.scalar_tensor_tensor(out=z[:, ec, tg * 512:(tg + 1) * 512], in0=xr[s3][:], scalar=ALPHA,
                                                                      in1=psM[:, 0:512], op0=ALU.mult, op1=ALU.add),
                              r=[("xr", s3), pkM], w=[("z", ec, tg)])
                    sc.barrier()
                    layer_norm_fm(p2, 0, "a", True)
                    sc.barrier()
                if debug is not None and debug[0] == "x1":
                    for c in range(8):
                        A('sp', lambda e, c=c: e.dma_start(out=dbg[c * 128:(c + 1) * 128, :], in_=z[:, c, :]), dma="dbg")
                    sc.barrier()
                    return nc

                with ExitStack() as p3:
                    wr_sb = sb(p3, "wr_sb", [128, 8, 16], F32)
                    ex = sb(p3, "ex", [16, T], F32)
                    affo = sb(p3, "affo", [16, T], F32)
                    Aall = sb(p3, "Aall", [128, 1024], F32)
                    junk = sb(p3, "junk", [128, 1024], F32)
                    bs_ = sb(p3, "bs_", [128, 8], F32)
                    A('sp', lambda e: e.dma_start(out=wr_sb[:], in_=wr[l].rearrange("(c p) n -> p c n", p=128)), dma="setup")
                    sc.barrier()
                    for tg in range(4):
                        tsl = slice(tg * 512, (tg + 1) * 512)
                        psR, pkR = nextps()
                        for kc in range(8):
                            A('pe', lambda e, kc=kc: e.matmul(psR[0:16, 0:512], wr_sb[:, kc, :], z[:, kc, tsl], start=(kc == 0), stop=(kc == 7)), w=[pkR])
                        A('act', lambda e: e.activation(out=ex[:, tsl], in_=psR[0:16, 0:512], func=AF.Exp), r=[pkR], w=[("ex", tg)])
                        psE, pkE = nextps()
                        A('pe', lambda e: e.matmul(psE[0:16, 0:512], ones_f[0:16, 0:16], ex[:, tsl], start=True, stop=True), r=[("ex", tg)], w=[pkE])
                        A('dve', lambda e: e.reciprocal(out=affo[:, tsl], in_=psE[0:16, 0:512]), r=[pkE], w=[("affo", tg)])
                        A('dve', lambda e: e.tensor_tensor(out=affo[:, tsl], in0=affo[:, tsl], in1=ex[:, tsl], op=ALU.mult),
                          r=[("affo", tg), ("ex", tg)], w=[("affo", tg)])
                    sc.barrier()
                    A('sp', lambda e: e.dma_start(out=snd_a.ap(), in_=affo[:]), dma="setup")
                    sc.barrier()
                    A('pool', lambda e: e.collective_compute("AllGather", ALU.bypass, replica_groups=G4,
                                                             ins=[snd_a.ap().opt()], outs=[rcv_a.ap().opt()]), dma="cca", inc=1, cc=True)
                    sc.barrier()
                    for r_ in range(4):
                        for h_ in range(2):
                            g_ = 2 * r_ + h_
                            A('sp', lambda e, r_=r_, h_=h_, g_=g_: e.dma_start(out=Aall[g_ * 16:(g_ + 1) * 16, :],
                                                                               in_=rcv_a.ap()[r_ * 16:(r_ + 1) * 16, h_ * 1024:(h_ + 1) * 1024]), dma="setup")
                    A('dve', lambda e: e.memset(bs_[:, 0:1], 0.0), w=["bs"])
                    A('dve', lambda e: e.memset(bs_[:, 1:2], 1.5), r=["bs"], w=["bs"])
                    sc.barrier()
                    lo, hi, mid, cn, ge, d1, d2 = [bs_[:, i:i + 1] for i in range(7)]
                    for it in range(30):
                        A('dve', lambda e: e.tensor_scalar(out=mid, in0=lo, scalar1=hi, scalar2=0.5, op0=ALU.add, op1=ALU.mult), r=["bs"], w=["bs"])
                        A('dve', lambda e: e.tensor_scalar(out=junk[:], in0=Aall[:], scalar1=mid, scalar2=0.0, op0=ALU.is_ge, op1=ALU.add,
                                                           accum_out=cn), r=["bs"], w=["bs", "junk"])
                        psT, pkT = nextps()
                        A('pe', lambda e: e.matmul(psT[:, 0:1], blk, cn, start=True, stop=True), r=["bs"], w=[pkT])
                        A('dve', lambda e: e.tensor_single_scalar(out=ge, in_=psT[:, 0:1], scalar=1023.5, op=ALU.is_ge), r=[pkT, "bs"], w=["bs"])
                        A('dve', lambda e: e.tensor_tensor(out=d1, in0=mid, in1=lo, op=ALU.subtract), r=["bs"], w=["bs"])
                        A('dve', lambda e: e.tensor_tensor(out=d2, in0=hi, in1=mid, op=ALU.subtract), r=["bs"], w=["bs"])
                        A('dve', lambda e: e.scalar_tensor_tensor(out=lo, in0=d1, scalar=ge, in1=lo, op0=ALU.mult, op1=ALU.add), r=["bs"], w=["bs"])
                        A('dve', lambda e: e.scalar_tensor_tensor(out=hi, in0=d2, scalar=ge, in1=mid, op0=ALU.mult, op1=ALU.add), r=["bs"], w=["bs"])
                    A('dve', lambda e: e.scalar_tensor_tensor(out=gm[:], in0=affo[:], scalar=bs_[0:16, 0:1], in1=affo[:], op0=ALU.is_ge, op1=ALU.mult),
                      r=["bs"], w=["gm"])
                    for c in range(8):
                        A('pool', lambda e, c=c: e.tensor_scalar_mul(out=z[:, c, :], in0=z[:, c, :], scalar1=ALPHA), w=[("zz", c)])
                    sc.barrier()
                if debug is not None and debug[0] == "gm":
                    A('sp', lambda e: e.dma_start(out=dbg, in_=gm[:]), dma="dbg")
                    sc.barrier()
                    return nc

                with ExitStack() as p4:
                    wE = [sb(p4, "wE%d" % i, [128, 8, D], BF16) for i in range(3)]
                    G_sb = [sb(p4, "G_sb%d" % i, [128, 512], F32) for i in range(4)]
                    stmp = [sb(p4, "stmp%d" % i, [128, 512], F32) for i in range(2)]
                    stm2 = [sb(p4, "stm2%d" % i, [128, 512], F32) for i in range(2)]
                    hn_ = [0]

                    def load_expert_w(ex_, which):
                        erk, ebase = ex_ // 4, 1536 + (ex_ % 4) * 3072
                        for wh in which:
                            for kc in range(8):
                                src_ = wrows(l, erk, ebase + wh * 1024 + kc * 128)
                                A('pool', lambda e, wh=wh, kc=kc, src_=src_: e.dma_start(out=wE[wh][:, kc, :], in_=src_),
                                  w=[("wE", wh, kc)], dma=("wE", wh))

                    load_expert_w(0, (0, 1, 2))
                    wk13 = [("wE", wh, kc) for wh in range(2) for kc in range(8)]
                    w2k = [("wE", 2, kc) for kc in range(8)]
                    for ex_ in range(16):
                        for tg in range(4):
                            psQ, pkQ = nextps()
                            A('pe', lambda e, tg=tg: e.matmul(psQ[:, 0:512], ident_f[0:16, ex_:ex_ + 1].to_broadcast([16, 128]), gm[:, tg * 512:(tg + 1) * 512],
                                                               start=True, stop=True), w=[pkQ])
                            A('act', lambda e, tg=tg: e.copy(out=G_sb[tg][:], in_=psQ[:, 0:512]), r=[pkQ], w=[("G", tg)])
                        for ft in range(8):
                            for tg in range(4):
                                tsl = slice(tg * 512, (tg + 1) * 512)
                                h2 = hn_[0] % 2
                                hn_[0] += 1
                                ps1, pk1 = nextps()
                                ps3, pk3 = nextps()
                                for kc in range(8):
                                    A('pe', lambda e, kc=kc: e.matmul(ps1[:, 0:512], wE[0][:, kc, ft * 128:(ft + 1) * 128], x1b[:, kc, tsl], start=(kc == 0), stop=(kc == 7)),
                                      r=wk13, w=[pk1])
                                for kc in range(8):
                                    A('pe', lambda e, kc=kc: e.matmul(ps3[:, 0:512], wE[1][:, kc, ft * 128:(ft + 1) * 128], x1b[:, kc, tsl], start=(kc == 0), stop=(kc == 7)),
                                      r=wk13, w=[pk3])
                                A('act', lambda e: e.activation(out=stmp[h2][:], in_=ps1[:, 0:512], func=AF.Silu), r=[pk1], w=[("stmp", h2)])
                                A('dve', lambda e: e.tensor_tensor(out=stm2[h2][:], in0=stmp[h2][:], in1=ps3[:, 0:512], op=ALU.mult),
                                  r=[("stmp", h2), pk3], w=[("stm2", h2)])
                                A('dve', lambda e, tg=tg: e.tensor_tensor(out=mgh[:, ft, tsl], in0=stm2[h2][:], in1=G_sb[tg][:], op=ALU.mult),
                                  r=[("stm2", h2), ("G", tg)], w=[("hid", ft, tg)])
                        if ex_ + 1 < 16:
                            load_expert_w(ex_ + 1, (0, 1))
                        for dc in range(8):
                            for tg in range(4):
                                tsl = slice(tg * 512, (tg + 1) * 512)
                                psY, pkY = nextps()
                                for fc in range(8):
                                    A('pe', lambda e, fc=fc: e.matmul(psY[:, 0:512], wE[2][:, fc, dc * 128:(dc + 1) * 128], mgh[:, fc, tsl],
                                                                        start=(fc == 0), stop=(fc == 7)), r=w2k + [("hid", fc, tg)], w=[pkY])
                                A('dve', lambda e: e.tensor_tensor(out=z[:, dc, tsl], in0=z[:, dc, tsl], in1=psY[:, 0:512], op=ALU.add),
                                  r=[pkY], w=[("zz", dc, tg)])
                        if ex_ + 1 < 16:
                            load_expert_w(ex_ + 1, (2,))
                    sc.barrier()
                with ExitStack() as p4b:
                    layer_norm_fm(p4b, 2, "b", False)
                    sc.barrier()

                if not last:
                    for c in range(8):
                        A('sp', lambda e, c=c: e.dma_start(out=xres.ap()[c * 128:(c + 1) * 128, :], in_=z[:, c, :]), dma="setup")
                        A('pool', lambda e, c=c: e.dma_start(out=snd_x.ap()[c * 128:(c + 1) * 128, :], in_=z[:, c, :]), dma="setup")
                    sc.barrier()
                else:
                    with ExitStack() as p5:
                        xo = [sb(p5, "xo%d" % i, [128, D], F32) for i in range(2)]
                        for tt in range(16):
                            o2 = tt % 2
                            for hh in range(2):
                                psX, pkX = nextps()
                                for c4 in range(4):
                                    c = hh * 4 + c4
                                    A('pe', lambda e, c=c, c4=c4: e.transpose(psX[:, c4 * 128:(c4 + 1) * 128], z[:, c, tt * 128:(tt + 1) * 128], ident_f), w=[pkX])
                                A('act' if hh == 0 else 'dve',
                                  (lambda e, hh=hh: e.copy(out=xo[o2][:, hh * 512:(hh + 1) * 512], in_=psX[:, 0:512])) if hh == 0 else
                                  (lambda e, hh=hh: e.tensor_copy(out=xo[o2][:, hh * 512:(hh + 1) * 512], in_=psX[:, 0:512])),
                                  r=[pkX], w=[("xo", o2, hh)])
                            A('sp', lambda e: e.dma_start(out=out[tt * 128:(tt + 1) * 128, :], in_=xo[o2][:]),
                              r=[("xo", o2, 0), ("xo", o2, 1)], w=[("out", tt)], dma=("xo", o2))
                        sc.barrier()
        sc.barrier()
    return nc


def _t5_bucket(rel):
    rel = np.asarray(rel, dtype=np.int64)
    half, max_exact = 16, 8
    ret = np.where(rel > 0, half, 0)
    n = np.abs(rel)
    nf = np.maximum(n, 1).astype(np.float32)
    large = max_exact + (np.log(nf / np.float32(max_exact)) / np.float32(math.log(1024 / max_exact))
                         * np.float32(half - max_exact)).astype(np.int32)
    large = np.minimum(large, half - 1)
    return ret + np.where(n < max_exact, n, large)


def _static_tables():
    p = np.arange(128)
    ident = np.eye(128, dtype=np.float32)
    tri_f = (p[:, None] <= p[None, :]).astype(np.float32)
    tri_b = (p[:, None] >= p[None, :]).astype(np.float32)
    mneg_f = np.where(p[:, None] <= p[None, :], 0.0, NEG).astype(np.float32)
    mneg_b = np.where(p[:, None] >= p[None, :], 0.0, NEG).astype(np.float32)
    blk = ((p[:, None] % 16) == (p[None, :] % 16)).astype(np.float32)
    ones = np.ones((128, 128), np.float32)
    kbias = np.zeros((128, 128), np.float32)
    kbias[:64, 1] = NEG
    kbias[64:, 2] = NEG
    return np.ascontiguousarray(np.stack([ident, tri_f, tri_b, mneg_f, mneg_b, blk, ones, kbias]))


def _band_tables(win):
    def wmat(qbase, pbase):
        Q = qbase + np.arange(128)[:, None]
        P = pbase + np.arange(128)[None, :]
        lo = np.clip(P - win // 2, 0, S)
        hi = np.clip(P + win // 2, 0, S)
        cntw = (hi - lo).astype(np.float32)
        m = ((Q >= lo) & (Q < hi)).astype(np.float32) / cntw
        return (m - (Q == P).astype(np.float32)).astype(np.float32)
    mid = 4096
    return np.ascontiguousarray(np.stack([wmat(mid - 128, mid), wmat(mid, mid), wmat(mid + 128, mid),
                                          wmat(0, 0), wmat(S - 128, S - 128)]))


def _prep_inputs(inp, depth):
    L = depth
    f = lambda a: np.ascontiguousarray(np.asarray(a, dtype=np.float32))
    x = np.asarray(inp['x'], np.float32)
    w_in = np.asarray(inp['w_in'], np.float32)[:L]
    b_in = np.asarray(inp['b_in'], np.float32)[:L]
    consts = _static_tables()
    maps = []
    kq = np.arange(128)
    for c in range(NCORE):
        b, m = c // 4, c % 4
        d = {}
        d["xT"] = f(x[b, T * m:T * (m + 1), :].T)
        sl = lambda o: np.arange(o + 64 * m, o + 64 * m + 64)
        cols_f = np.concatenate([sl(0), sl(512), sl(1280), sl(768), sl(1536), sl(1024)])
        gv = 256 + np.concatenate([np.arange(64 * m, 64 * m + 64)] + [np.arange(64 * g, 64 * g + 64) for g in range(4) if g != m])
        gcols = np.array([2304 + m, 2312 + m, 2304 + 4 + m, 2312 + 4 + m])
        cols_t = np.concatenate([gv, sl(1792), sl(2048), gcols, sl(2320)])
        d["whf"] = f(w_in[:, :, cols_f])
        d["bhf"] = f(b_in[:, cols_f])
        d["wht"] = f(w_in[:, :, cols_t])
        d["bht"] = f(b_in[:, cols_t])
        d["fbias"] = f(np.asarray(inp['ml_fbias'])[:L, :, m])
        d["wsT"] = f(np.transpose(np.asarray(inp['gm_ws'])[:L, m], (0, 2, 1)))
        d["bsv"] = f(np.asarray(inp['gm_bs'])[:L, m, :])
        d["lng"] = f(np.asarray(inp['gm_ln_g'])[:L, 64 * m:64 * m + 64])
        rb = np.asarray(inp['rel_bias'], np.float32)
        bm = np.zeros((6, 128, 128), np.float32)
        for p_, dil in enumerate(DILS):
            for h_ in range(2):
                j = (-64 + 128 * h_ + kq[:, None]) - kq[None, :]
                valid = np.abs(j) <= 64
                vals = rb[_t5_bucket(dil * j), m]
                bm[p_ * 2 + h_] = np.where(valid, vals, np.float32(NEG))
        d["bmat"] = f(bm)
        mc = np.asarray(inp['ml_conv'], np.float32)[:L]
        d["cwq"] = f(np.transpose(mc[:, :, 64 * m:64 * m + 64], (0, 2, 1)))
        d["cwk"] = f(np.transpose(mc[:, :, 256 + 64 * m:256 + 64 * m + 64], (0, 2, 1)))
        d["ngv"] = f(np.asarray(inp['ml_norm_g'])[:L, 64 * m:64 * m + 64])
        d["bands"] = _band_tables((2, 4, 8, 16)[m])
        d["pw"] = f(np.asarray(inp['pool_w'])[:L, m])
        d["psc"] = f(np.asarray(inp['pool_scale'])[:L, 64 * m:64 * m + 64])
        pp, qq = np.meshgrid(np.arange(128), np.arange(8), indexing="ij")
        d["idxy"] = np.ascontiguousarray((m * 1024 + (qq // 2) * 256 + (qq % 2) * 128 + pp).astype(np.int32))
        wp = np.empty((L, WROWS, D), np.float32)
        w_br = np.asarray(inp['w_branch'], np.float32)
        pidx = np.arange(128)
        for l in range(L):
            for jj in range(2):
                kc = 2 * m + jj
                wp[l, jj * 512:(jj + 1) * 512] = w_in[l, kc * 128:(kc + 1) * 128, 2576:].reshape(512, D)
                wp[l, 1024 + jj * 128:1024 + (jj + 1) * 128] = inp['w_out'][l][kc * 128:(kc + 1) * 128]
                q = kc
                wp[l, 1280 + jj * 128:1280 + (jj + 1) * 128] = w_br[l, 2 * (q % 2) + pidx // 64, (q // 2) * 64 + pidx % 64, :]
            for k in range(4):
                e = 4 * m + k
                o = 1536 + k * 3072
                wp[l, o:o + 1024] = inp['w_e1'][l][e]
                wp[l, o + 1024:o + 2048] = inp['w_e3'][l][e]
                wp[l, o + 2048:o + 3072] = inp['w_e2'][l][e]
        d["wpack"] = wp
        d["bg"] = f(b_in[:, 2576:])
        d["ln1g"] = f(np.asarray(inp['ln1_g'])[:L])
        d["ln1b"] = f(np.asarray(inp['ln1_b'])[:L])
        d["wr"] = f(np.asarray(inp['w_router'])[:L])
        d["ln2g"] = f(np.asarray(inp['ln2_g'])[:L])
        d["ln2b"] = f(np.asarray(inp['ln2_b'])[:L])
        d["consts"] = consts
        maps.append(d)
    return maps


_NC_CACHE = {}


def kernel(**inputs):
    maps = _prep_inputs(inputs, DEPTH)
    if "nc" not in _NC_CACHE:
        _NC_CACHE["nc"] = build_program(DEPTH)
    res = run_bass_kernel_spmd(_NC_CACHE["nc"], maps, core_ids=list(range(NCORE)))
    outp = np.empty((2, S, D), np.float32)
    for c in range(NCORE):
        b, m = c // 4, c % 4
        outp[b, T * m:T * (m + 1), :] = np.asarray(res.results[c]["out"], np.float32)
    return outp
```

```python
import math
import os
from contextlib import ExitStack

import numpy as np
import ml_dtypes

import concourse.bass as bass
import concourse.mybir as mybir
from concourse.bass_utils import run_bass_kernel_spmd

F32 = mybir.dt.float32
BF16 = mybir.dt.bfloat16
I32 = mybir.dt.int32
AF = mybir.ActivationFunctionType
ALU = mybir.AluOpType

D = 1024
S = 8192
T = 2048
DEPTH = 4
NCORE = 8
ALPHA = (2 * DEPTH) ** 0.25
EPS = 1e-5
NEG = -30000.0
WROWS = 13824
HALO = 1024
CW = S + 2 * HALO
DILS = (1, 4, 16)


class Sched:
    def __init__(self, nc, stack):
        self.nc = nc
        self.stack = stack
        self.eng = {'sp': nc.sync, 'act': nc.scalar, 'pe': nc.tensor, 'dve': nc.vector, 'pool': nc.gpsimd}
        self.esem = {}
        self.eseq = {}
        for e in self.eng:
            self.esem[e] = stack.enter_context(nc.semaphore("es_" + e))
            self.eseq[e] = 0
        self.dsem = {}
        self.lastw = {}
        self.readers = {}
        self.waited = {e: {} for e in self.eng}
        self.sig = []
        self.nsem = 0
        self.bgkeys = set()
        self.marker = stack.enter_context(nc.sbuf_tensor("cc_marker", [1, 4], F32))

    def _dsem(self, key):
        if key not in self.dsem:
            self.nsem += 1
            self.dsem[key] = [self.stack.enter_context(self.nc.semaphore("ds%d" % self.nsem)), 0]
        return self.dsem[key]

    def _wait(self, eng, sem, val):
        w = self.waited[eng]
        k = id(sem)
        if w.get(k, (None, 0))[1] >= val:
            return
        w[k] = (sem, val)
        self.eng[eng].wait_ge(sem, val)

    def wait_bg(self, eng, key):
        for sw in (False, True):
            if (key, sw) in self.dsem:
                sem, cnt = self.dsem[(key, sw)]
                self._wait(eng, sem, cnt)

    def add(self, eng, fn, r=(), w=(), dma=None, inc=16, bg=False, cc=False):
        if dma is not None:
            dma = (dma, eng == 'pool')
        if bg:
            self.bgkeys.add(dma)
        deps = set()
        for k in r:
            if k in self.lastw:
                deps.add(self.lastw[k])
        for k in w:
            if k in self.lastw:
                deps.add(self.lastw[k])
            rd = self.readers.get(k)
            if rd:
                deps.update(rd[0].values())
                deps.update(rd[1])
        need = {}
        for d in deps:
            sem, val, deng, isdma = self.sig[d][:4]
            if (not isdma) and deng == eng and eng == 'pe':
                continue
            if isdma and len(self.sig[d]) > 4:
                val = self.sig[d][4][1]
            k = id(sem)
            if k not in need or need[k][1] < val:
                need[k] = (sem, val)
        for sem, val in need.values():
            self._wait(eng, sem, val)
        ins = fn(self.eng[eng])
        idx = len(self.sig)
        if cc:
            ds = self._dsem(dma)
            self.bgkeys.add(dma)
            ds[1] += 1
            ins.then_inc(ds[0])
            if not bg:
                self._wait('pool', ds[0], ds[1])
                mk = self.eng['pool'].memset(self.marker[0:1, 0:1], 0.0)
                self.eseq['pool'] += 1
                mk.then_inc(self.esem['pool'], 1)
                self.sig.append((self.esem['pool'], self.eseq['pool'], 'pool', False))
            else:
                self.sig.append((ds[0], ds[1], eng, True))
        elif dma is None:
            self.eseq[eng] += 1
            sem = self.esem[eng]
            ins.then_inc(sem, 1)
            self.sig.append((sem, self.eseq[eng], eng, False))
        else:
            ds = self._dsem(dma)
            ds[1] += inc
            if inc == 1:
                ins.then_inc(ds[0])
            else:
                ins.then_inc(ds[0], inc)
            self.sig.append((ds[0], ds[1], eng, True, ds))
        for k in w:
            self.lastw[k] = idx
            self.readers[k] = [{}, []]
        for k in r:
            rd = self.readers.setdefault(k, [{}, []])
            if dma is None:
                rd[0][eng] = idx
            else:
                rd[1].append(idx)
        return idx

    def barrier(self):
        for e in self.eng:
            for e2 in self.eng:
                if self.eseq[e2] > 0:
                    self._wait(e, self.esem[e2], self.eseq[e2])
            for k, (sem, cnt) in self.dsem.items():
                if cnt > 0 and k not in self.bgkeys:
                    self._wait(e, sem, cnt)
        self.lastw.clear()
        self.readers.clear()


def build_program(depth=DEPTH, debug=None):
    nc = bass.Bass("TRN2", target_bir_lowering=False)
    L = depth

    def din(name, shape, dt=F32):
        return nc.dram_tensor(name, list(shape), dt, kind="ExternalInput").ap()

    xT_in = din("xT", [D, T])
    whf = din("whf", [L, D, 384])
    bhf = din("bhf", [L, 384])
    wht = din("wht", [L, D, 452])
    bht = din("bht", [L, 452])
    fbias = din("fbias", [L, 2])
    wsT = din("wsT", [L, 128, 128])
    bsv = din("bsv", [L, 128])
    lng = din("lng", [L, 64])
    bmat = din("bmat", [6, 128, 128])
    cwq = din("cwq", [L, 64, 3])
    cwk = din("cwk", [L, 64, 3])
    ngv = din("ngv", [L, 64])
    bands = din("bands", [5, 128, 128])
    pw = din("pw", [L, 64, 64])
    psc = din("psc", [L, 64])
    idxy = din("idxy", [128, 8], I32)
    wpack = din("wpack", [L, WROWS, D])
    bg = din("bg", [L, 4096])
    ln1g = din("ln1g", [L, D])
    ln1b = din("ln1b", [L, D])
    wr = din("wr", [L, D, 16])
    ln2g = din("ln2g", [L, D])
    ln2b = din("ln2b", [L, D])
    consts = din("consts", [8, 128, 128])
    out = nc.dram_tensor("out", [T, D], F32, kind="ExternalOutput").ap()
    dbg = None
    if debug is not None:
        dbg = nc.dram_tensor("dbg", list(debug[1]), debug[2], kind="ExternalOutput").ap()

    snd_x = nc.dram_tensor("snd_x", [D, T], BF16)
    rcv_x = [nc.dram_tensor("rcv_x%d" % i, [4 * 256, T], BF16) for i in range(4)]
    snd_y = nc.dram_tensor("snd_y", [4 * 256, T], BF16)
    rcv_y = nc.dram_tensor("rcv_y", [4 * 1024, T], BF16)
    snd_a = nc.dram_tensor("snd_a", [16, T], F32)
    rcv_a = nc.dram_tensor("rcv_a", [64, T], F32)
    xres = nc.dram_tensor("xres", [D, T], F32)
    CHR = 512
    NCH = WROWS // CHR
    snd_w = [nc.dram_tensor("snd_w%d" % l, [WROWS, D], BF16) for l in range(L)]
    rcv_w = [[nc.dram_tensor("rcv_w%d_%d" % (l, i), [4 * CHR, D], BF16) for i in range(NCH)] for l in range(L)]
    G4 = [[0, 1, 2, 3], [4, 5, 6, 7]]
    G8 = [list(range(8))]

    with ExitStack() as top:
        sc = Sched(nc, top)
        A = sc.add

        def rstd_op(out_ap, in_ap, rk, wk):
            A('act', lambda e: e.activation(out=out_ap, in_=in_ap, func=AF.Sqrt, bias=EPS), r=rk, w=wk)
            A('dve', lambda e: e.reciprocal(out=out_ap, in_=out_ap), r=wk, w=wk)

        uniq = [0]

        def sb(stack, name, shape, dt):
            uniq[0] += 1
            return stack.enter_context(nc.sbuf_tensor("%s_%d" % (name, uniq[0]), list(shape), dt))

        psb = [top.enter_context(nc.psum_tensor("psb%d" % i, [128, 512], F32)) for i in range(7)]
        pst = top.enter_context(nc.psum_tensor("pst", [128, 1024], BF16))
        pscount = [0]

        def nextps():
            i = pscount[0] % 7
            pscount[0] += 1
            return psb[i], ("ps", i)

        cst = sb(top, "cst", [128, 8, 128], F32)
        A('sp', lambda e: e.dma_start(out=cst[:], in_=consts.rearrange("k p f -> p k f")), w=["cst"], dma="cst")
        ident_f = cst[:, 0, :]
        tri = [cst[:, 1, :], cst[:, 2, :]]
        mneg = [cst[:, 3, :], cst[:, 4, :]]
        blk = cst[:, 5, :]
        ones_f = cst[:, 6, :]
        kb = cst[:, 7, 0:3]
        cstb = sb(top, "cstb", [128, 2, 128], BF16)
        A('dve', lambda e: e.tensor_copy(out=cstb[:, 0, :], in_=cst[:, 0, :]), r=["cst"], w=["cstb0"])
        A('dve', lambda e: e.tensor_copy(out=cstb[:, 1, :], in_=cst[:, 6, :]), r=["cst"], w=["cstb1"])
        ident_b = cstb[:, 0, :]
        ones_b = cstb[:, 1, :]
        CK = ["cst", "cstb0", "cstb1"]

        if debug is not None and debug[0] == "tm1":
            sc.barrier()
            A('sp', lambda e: e.dma_start(out=dbg, in_=cstb[:, 0, :]), dma="dbg")
            sc.barrier()
            return nc
        def bounce_weights(l):
            for i in range(8):
                A('pool', lambda e, i=i: e.dma_start(out=snd_w[l].ap()[i * 1728:(i + 1) * 1728, :], in_=wpack[l][i * 1728:(i + 1) * 1728, :]),
                  dma=("sndw", l), bg=True)

        def gather_weights(l):
            sc.wait_bg('pool', ("sndw", l))
            for i in range(NCH):
                A('pool', lambda e, i=i: e.collective_compute("AllGather", ALU.bypass, replica_groups=G4,
                                                              ins=[snd_w[l].ap()[i * CHR:(i + 1) * CHR, :].opt()],
                                                              outs=[rcv_w[l][i].ap().opt()]),
                  dma=("ccw", l), inc=1, bg=True, cc=True)

        def wrows(l, rank, row0, n=128):
            i = row0 // CHR
            assert (row0 % CHR) + n <= CHR
            sem, _ = sc.dsem[((("ccw", l)), True)]
            sc._wait('pool', sem, i + 1)
            o = rank * CHR + row0 % CHR
            return rcv_w[l][i].ap()[o:o + n, :]

        A('pool', lambda e: e.dma_start(out=snd_x.ap(), in_=xT_in), w=["snd_x"], dma="snd_x")

        if debug is not None and debug[0] == "tm2":
            sc.barrier()
            A('pool', lambda e: e.dma_start(out=dbg, in_=snd_x.ap()[0:128, 0:128]), dma="dbg")
            sc.barrier()
            return nc
        for l in range(L):
            xsrc = xT_in if l == 0 else xres.ap()
            last = (l == L - 1)
            for i in range(4):
                A('pool', lambda e, i=i: e.collective_compute("AllGather", ALU.bypass, replica_groups=G4,
                                                              ins=[snd_x.ap()[i * 256:(i + 1) * 256, :].opt()], outs=[rcv_x[i].ap().opt()]),
                  r=["snd_x"], w=[("rcv_x", i)], dma="ccx", inc=1, cc=True)

            if debug is not None and debug[0] == "t0":
                sc.barrier()
                A('pool', lambda e: e.dma_start(out=dbg, in_=rcv_x[0].ap()[0:128, 0:128]), dma="dbg")
                sc.barrier()
                return nc
            with ExitStack() as pb:
                whf_sb = sb(pb, "whf_sb", [128, 8, 384], BF16)
                wht_sb = sb(pb, "wht_sb", [128, 8, 452], BF16)
                bhf_sb = sb(pb, "bhf_sb", [128, 3], F32)
                bht_bc = sb(pb, "bht_bc", [128, 452], F32)
                fb_bc = sb(pb, "fb_bc", [128, 2], F32)
                wsT_sb = sb(pb, "wsT_sb", [128, 128], BF16)
                bs_bc = sb(pb, "bs_bc", [64, 128], F32)
                lng_sb = sb(pb, "lng_sb", [64, 1], F32)
                bands_sb = sb(pb, "bands_sb", [128, 5, 128], BF16)
                pw_sb = sb(pb, "pw_sb", [64, 64], BF16)
                psc_sb = sb(pb, "psc_sb", [64, 1], F32)
                B_sb = sb(pb, "B_sb", [128, 6, 128], BF16)
                cw_sb = sb(pb, "cw_sb", [64, 2, 3], F32)
                ng_bc = sb(pb, "ng_bc", [128, 64], F32)
                C0 = sb(pb, "C0", [128, S], BF16)
                C1 = sb(pb, "C1", [128, CW], BF16)
                C2 = sb(pb, "C2", [128, CW], BF16)
                vaug = sb(pb, "vaug", [128, 64, 65], BF16)
                co_sb = sb(pb, "co_sb", [128, 64, 64], BF16)
                co_f = sb(pb, "co_f", [128, 64, 64], F32)
                gates = sb(pb, "gates", [128, 64, 4], F32)
                dx_sb = sb(pb, "dx_sb", [128, 64, 64], BF16)

                A('pool', lambda e: e.dma_start(out=whf_sb[:], in_=whf[l].rearrange("(c p) n -> p c n", p=128)), w=["whf"], dma="setup")
                A('pool', lambda e: e.dma_start(out=wht_sb[:], in_=wht[l].rearrange("(c p) n -> p c n", p=128)), w=["wht"], dma="setup")
                with nc.allow_non_contiguous_dma(reason="tiny bias loads"):
                    A('sp', lambda e: e.dma_start(out=bhf_sb[:], in_=bhf[l].rearrange("(c p) -> p c", p=128)), w=["bhf"], dma="setup")
                    A('sp', lambda e: e.dma_start(out=lng_sb[:], in_=lng[l].rearrange("(p o) -> p o", o=1)), w=["lng"], dma="setup")
                    A('sp', lambda e: e.dma_start(out=psc_sb[:], in_=psc[l].rearrange("(p o) -> p o", o=1)), w=["psc"], dma="setup")
                A('sp', lambda e: e.dma_start(out=bht_bc[:], in_=bht[l].partition_broadcast(128)), w=["bht"], dma="setup")
                A('sp', lambda e: e.dma_start(out=fb_bc[:], in_=fbias[l].partition_broadcast(128)), w=["fb"], dma="setup")
                A('sp', lambda e: e.dma_start(out=bs_bc[:], in_=bsv[l].partition_broadcast(64)), w=["bs"], dma="setup")
                A('sp', lambda e: e.dma_start(out=ng_bc[:], in_=ngv[l].partition_broadcast(128)), w=["ng"], dma="setup")
                A('pool', lambda e: e.dma_start(out=wsT_sb[:], in_=wsT[l]), w=["wsT"], dma="setup")
                A('pool', lambda e: e.dma_start(out=bands_sb[:], in_=bands.rearrange("k p f -> p k f")), w=["bands"], dma="setup")
                A('pool', lambda e: e.dma_start(out=pw_sb[:], in_=pw[l]), w=["pw"], dma="setup")
                A('pool', lambda e: e.dma_start(out=B_sb[:], in_=bmat.rearrange("k p f -> p k f")), w=["B"], dma="setup")
                A('sp', lambda e: e.dma_start(out=cw_sb[:, 0, :], in_=cwq[l]), w=["cwq"], dma="setup")
                A('sp', lambda e: e.dma_start(out=cw_sb[:, 1, :], in_=cwk[l]), w=["cwk"], dma="setup")
                if debug is not None and debug[0] == "t1":
                    sc.barrier()
                    A('pool', lambda e: e.dma_start(out=dbg, in_=rcv_x[0].ap()[0:128, 0:128]), dma="dbg")
                    sc.barrier()
                    return nc
                A('pool', lambda e: e.memset(C1[:], 0.0), w=["C1"])
                A('pool', lambda e: e.memset(C2[:], 0.0), w=["C2"])
                A('pool', lambda e: e.memset(vaug[:], 1.0), w=["vaug"])
                if debug is not None and debug[0] == "s0":
                    sc.barrier()
                    A('pool', lambda e: e.dma_start(out=dbg, in_=rcv_x[0].ap()[0:128, 0:128]), dma="dbg")
                    sc.barrier()
                    return nc
                if l == 0:
                    bounce_weights(0)
                    gather_weights(0)
                if l + 1 < L:
                    bounce_weights(l + 1)
                    gather_weights(l + 1)
                sc.barrier()
                A('dve', lambda e: e.tensor_tensor(out=bht_bc[:, 385:388:2], in0=bht_bc[:, 385:388:2], in1=fb_bc[:], op=ALU.add),
                  r=["fb", "bht"], w=["bht"])
                sc.barrier()

                SKIPB = bool(os.environ.get("K_SKIPB"))
                if SKIPB:
                    for i4 in range(4):
                        A('pool', lambda e, i4=i4: e.dma_start(out=snd_y.ap()[i4 * 256:(i4 + 1) * 256, :], in_=xT_in[0:256, :]), dma="setup")
                    sc.barrier()
                if not SKIPB:
                    with ExitStack() as pj:
                        xblk = [sb(pj, "xblk%d" % i, [128, 8, 512], BF16) for i in range(2)]
                        ublk = [sb(pj, "ublk%d" % i, [64, 512], F32) for i in range(2)]
                        tm = [sb(pj, "tm%d" % i, [128, 452], F32) for i in range(8)]
                        vg = [sb(pj, "vg%d" % i, [128, 256], F32) for i in range(8)]
                        st6 = [sb(pj, "st6%d" % i, [128, 6], F32) for i in range(8)]
                        mv = [sb(pj, "mv%d" % i, [128, 2], F32) for i in range(8)]
                        rstd = [sb(pj, "rstd%d" % i, [128, 1], F32) for i in range(8)]
                        vhat = [sb(pj, "vhat%d" % i, [128, 64], BF16) for i in range(8)]
                        t1 = [sb(pj, "t1%d" % i, [64, 128], F32) for i in range(8)]
                        ya = [sb(pj, "ya%d" % i, [64, 128], BF16) for i in range(8)]
                        pooled = [sb(pj, "pooled%d" % i, [64, 128], BF16) for i in range(4)]
                        yd = [sb(pj, "yd%d" % i, [64, 128], BF16) for i in range(4)]

                        pool_ps = {}

                        def pool_a(i):
                            ps, pk = nextps()
                            pool_ps[i] = (ps, pk)
                            srcs = []
                            if i > 0:
                                srcs.append((i - 1, 0))
                            srcs.append((i, 3 if i == 0 else (4 if i == 63 else 1)))
                            if i < 63:
                                srcs.append((i + 1, 2))
                            for n_, (j_, kind) in enumerate(srcs):
                                A('pe', lambda e, j_=j_, kind=kind, n_=n_: e.matmul(ps[0:64, 0:128], dx_sb[:, j_, :], bands_sb[:, kind, :],
                                                                                      start=(n_ == 0), stop=(n_ == len(srcs) - 1)),
                                  r=[("dx", j_)], w=[pk])

                        def pool_b(i):
                            s2 = i % 4
                            ps, pk = pool_ps[i]
                            A('dve', lambda e: e.tensor_copy(out=pooled[s2][:], in_=ps[0:64, 0:128]), r=[pk], w=[("pooled", s2)])
                            ps2, pk2 = nextps()
                            pool_ps[i] = (ps2, pk2)
                            A('pe', lambda e: e.matmul(ps2[0:64, 0:128], pw_sb[:], pooled[s2][:], start=True, stop=True),
                              r=[("pooled", s2)], w=[pk2])

                        def pool_c(i):
                            s2 = i % 4
                            ps2, pk2 = pool_ps.pop(i)
                            A('dve', lambda e: e.tensor_scalar_mul(out=yd[s2][:], in0=ps2[0:64, 0:128], scalar1=psc_sb[:, 0:1]),
                              r=[pk2], w=[("yd", s2)])
                            A('sp', lambda e: e.dma_start(out=snd_y.ap()[(i // 16) * 256 + 192:(i // 16) * 256 + 256, (i % 16) * 128:(i % 16 + 1) * 128], in_=yd[s2][:]),
                              r=[("yd", s2)], w=[("sndy", 3, i)], dma=("ydo", s2 % 2))

                        def stage1(i, s_, tt, xbk):
                            s2 = i % 8
                            ps, pk = nextps()
                            for k in range(8):
                                A('pe', lambda e, k=k: e.matmul(ps[:, 0:452], xblk[s_][:, k, tt * 128:(tt + 1) * 128], wht_sb[:, k, :],
                                                                  start=(k == 0), stop=(k == 7)),
                                  r=xbk, w=[pk])
                            A('dve', lambda e: e.tensor_tensor(out=tm[s2][:], in0=ps[:, 0:452], in1=bht_bc[:], op=ALU.add),
                              r=[pk], w=[("tm", s2)])
                            A('act', lambda e: e.activation(out=vg[s2][:], in_=tm[s2][:, 0:256], func=AF.Gelu), r=[("tm", s2)], w=[("vg", s2)])
                            A('pool', lambda e: e.tensor_copy(out=vaug[:, i, 0:64], in_=tm[s2][:, 256:320]), r=[("tm", s2)], w=[("vaug", i)])
                            A('pool', lambda e: e.tensor_copy(out=co_f[:, i, :], in_=tm[s2][:, 320:384]), r=[("tm", s2)], w=[("co", i)])
                            A('pool', lambda e: e.tensor_copy(out=gates[:, i, :], in_=tm[s2][:, 384:388]), r=[("tm", s2)], w=[("gates", i)])
                            A('pool', lambda e: e.tensor_copy(out=dx_sb[:, i, :], in_=tm[s2][:, 388:452]), r=[("tm", s2)], w=[("dx", i)])

                        def stage2a(i):
                            s2 = i % 8
                            A('dve', lambda e: e.bn_stats(out=st6[s2][:], in_=vg[s2][:]), r=[("vg", s2)], w=[("st6", s2)])
                            A('dve', lambda e: e.bn_aggr(out=mv[s2][:], in_=st6[s2][:]), r=[("st6", s2)], w=[("mv", s2)])
                            A('act', lambda e: e.activation(out=rstd[s2][:], in_=mv[s2][:, 1:2], func=AF.Sqrt, bias=EPS), r=[("mv", s2)], w=[("rstd", s2)])

                        def stage2b(i):
                            s2 = i % 8
                            A('dve', lambda e: e.reciprocal(out=rstd[s2][:], in_=rstd[s2][:]), r=[("rstd", s2)], w=[("rstd", s2)])
                            A('dve', lambda e: e.tensor_scalar(out=vhat[s2][:], in0=vg[s2][:, 0:64], scalar1=mv[s2][:, 0:1],
                                                               scalar2=rstd[s2][:, 0:1], op0=ALU.subtract, op1=ALU.mult),
                              r=[("vg", s2), ("mv", s2), ("rstd", s2)], w=[("vhat", s2)])

                        gm_ps = {}

                        def stage3a(i):
                            s2 = i % 8
                            psg, pkg = nextps()
                            gm_ps[i] = (psg, pkg)
                            A('pe', lambda e: e.matmul(psg[0:64, 0:128], vhat[s2][:], wsT_sb[:], start=True, stop=True),
                              r=[("vhat", s2)], w=[pkg])

                        def stage3b(i):
                            s2 = i % 8
                            sb_ = (i // 4) % 2
                            tt = i % 4
                            psg, pkg = gm_ps.pop(i)
                            A('dve', lambda e: e.scalar_tensor_tensor(out=t1[s2][:], in0=psg[0:64, 0:128], scalar=lng_sb[:, 0:1],
                                                                      in1=bs_bc[:], op0=ALU.mult, op1=ALU.add), r=[pkg], w=[("t1", s2)])
                            A('dve', lambda e: e.tensor_tensor(out=ya[s2][:], in0=t1[s2][:], in1=ublk[sb_][:, tt * 128:(tt + 1) * 128], op=ALU.mult),
                              r=[("t1", s2), ("ublk", sb_)], w=[("ya", s2)])
                            A('sp', lambda e: e.dma_start(out=snd_y.ap()[(i // 16) * 256:(i // 16) * 256 + 64, (i % 16) * 128:(i % 16 + 1) * 128], in_=ya[s2][:]),
                              r=[("ya", s2)], w=[("sndy", 0, i)], dma=("yao", s2 % 2))

                        def later_stages(step):
                            for lag, fn in ((4, stage3b), (3, pool_c), (3, stage3a), (2, pool_b), (2, stage2b), (1, stage2a)):
                                idx = step - lag
                                if 0 <= idx < 64:
                                    fn(idx)
                            if 1 <= step <= 64:
                                pool_a(step - 1)

                        def load_xblk(j):
                            s_ = j % 2
                            rk, off = j // 4, (j % 4) * 512
                            for i4 in range(4):
                                A('sp', lambda e, i4=i4: e.dma_start(out=xblk[s_][:, 2 * i4:2 * i4 + 2, :],
                                                                     in_=rcv_x[i4].ap()[rk * 256:(rk + 1) * 256, off:off + 512].rearrange("(c p) t -> p c t", p=128)),
                                  w=[("xblk", s_, i4)], dma=("xblk", s_))

                        load_xblk(0)
                        for j in range(16):
                            s_ = j % 2
                            xbk = [("xblk", s_, i4) for i4 in range(4)]
                            for ci in range(3):
                                ps, pk = nextps()
                                for k in range(8):
                                    A('pe', lambda e, k=k: e.matmul(ps[:, 0:512], whf_sb[:, k, ci * 128:(ci + 1) * 128], xblk[s_][:, k, :],
                                                                      start=(k == 0), stop=(k == 7)),
                                      r=xbk, w=[pk])
                                if ci == 0:
                                    A('act', lambda e: e.activation(out=ublk[s_][:], in_=ps[0:64, 0:512], func=AF.Gelu, bias=bhf_sb[0:64, 0:1]),
                                      r=[pk], w=[("ublk", s_)])
                                    A('dve', lambda e: e.tensor_scalar(out=C0[64:128, j * 512:(j + 1) * 512], in0=ps[64:128, 0:512],
                                                                       scalar1=bhf_sb[64:128, 0:1], scalar2=0.125, op0=ALU.add, op1=ALU.mult),
                                      r=[pk], w=[("C0", j)])
                                else:
                                    Cx = C1 if ci == 1 else C2
                                    A('dve', lambda e, Cx=Cx: e.tensor_scalar(out=Cx[:, HALO + j * 512:HALO + (j + 1) * 512], in0=ps[:, 0:512],
                                                                               scalar1=bhf_sb[:, ci:ci + 1], scalar2=None, op0=ALU.add),
                                      r=[pk], w=[("C%d" % ci, j)])
                            if j + 1 < 16:
                                load_xblk(j + 1)
                            for tt in range(4):
                                i = 4 * j + tt
                                stage1(i, s_, tt, xbk)
                                later_stages(i)
                        for step in range(64, 69):
                            later_stages(step)
                        gz = sb(pj, "gz", [128, 64, 2], F32)
                        ge_ = sb(pj, "ge_", [128, 64, 2], F32)
                        sc.barrier()
                        A('act', lambda e: e.activation(out=co_sb[:], in_=co_f[:], func=AF.Sigmoid), w=["co_all"])
                        A('act', lambda e: e.activation(out=gz[:], in_=gates[:, :, 1:4:2], func=AF.Abs), w=["gz"])
                        A('act', lambda e: e.activation(out=ge_[:], in_=gz[:], func=AF.Exp, scale=-1.0), r=["gz"], w=["ge"])
                        A('act', lambda e: e.activation(out=ge_[:], in_=ge_[:], func=AF.Ln, bias=1.0), r=["ge"], w=["ge"])
                        A('dve', lambda e: e.tensor_scalar_min(out=gz[:], in0=gates[:, :, 1:4:2], scalar1=0.0), r=["gz"], w=["gz"])
                        A('dve', lambda e: e.tensor_tensor(out=gates[:, :, 1:4:2], in0=gz[:], in1=ge_[:], op=ALU.subtract), r=["gz", "ge"], w=["gates"])
                        sc.barrier()

                    if debug is not None and debug[0] == "s1":
                        A('pool', lambda e: e.dma_start(out=dbg, in_=snd_y.ap()), dma="dbg")
                        sc.barrier()
                        return nc
                    with ExitStack() as pa:
                        vtok = sb(pa, "vtok", [128, 69, 64], BF16)
                        acc = sb(pa, "acc", [64, 2, T], F32)
                        pT = [sb(pa, "pT%d" % i, [128, 256], BF16) for i in range(3)]
                        rec = sb(pa, "rec", [64, T], F32)
                        yb = sb(pa, "yb", [64, T], BF16)
                        ptc = [0]
                        for sbk in range(4):
                            base = T * sbk
                            tiles = []
                            for p_, dil in enumerate(DILS):
                                nq = T // dil // 128
                                for r_ in range(dil):
                                    for m_ in range(nq + 1):
                                        tiles.append((p_, r_, m_, HALO + base + r_ + dil * (128 * m_ - 64), dil))
                            tix = {}
                            for g0 in range(0, len(tiles), 16):
                                grp = tiles[g0:g0 + 16]
                                for n_, (p_, r_, m_, c0, dil) in enumerate(grp):
                                    tix[(p_, r_, m_)] = g0 + n_
                                    A('pe', lambda e, n_=n_, c0=c0, dil=dil: e.transpose(pst[:, n_ * 64:(n_ + 1) * 64],
                                                                                           C2[64:128, c0:c0 + 127 * dil + 1:dil], ident_b[64:128, 64:128]),
                                      w=["pst"])
                                A('dve', lambda e, g0=g0, grp=grp: e.tensor_copy(out=vtok[:, g0:g0 + len(grp), :],
                                                                                 in_=pst[:, 0:len(grp) * 64].rearrange("p (a b) -> p a b", b=64)),
                                  r=["pst"], w=[("vtok", g0)])
                            for p_, dil in enumerate(DILS):
                                nq = T // dil // 128
                                for r_ in range(dil):
                                    for a_ in range(nq):
                                        q0 = base + r_ + dil * 128 * a_
                                        ps, pk = nextps()
                                        pt = pT[ptc[0] % 3]
                                        ptk = ("pT", ptc[0] % 3)
                                        ptc[0] += 1
                                        for h_ in range(2):
                                            kp0 = base + r_ + dil * (128 * a_ - 64 + 128 * h_)
                                            kc0 = HALO + kp0
                                            A('pe', lambda e, h_=h_, kc0=kc0: e.matmul(ps[:, h_ * 128:(h_ + 1) * 128],
                                                                                        C1[64:128, kc0:kc0 + 127 * dil + 1:dil],
                                                                                        C0[64:128, q0:q0 + 127 * dil + 1:dil], start=True, stop=False),
                                              w=[pk])
                                            A('pe', lambda e, h_=h_: e.matmul(ps[:, h_ * 128:(h_ + 1) * 128], ident_b, B_sb[:, p_ * 2 + h_, :],
                                                                              start=False, stop=True), w=[pk])
                                            kbc = 1 if kp0 < 0 else (2 if kp0 + 127 * dil >= S else 0)
                                            A('act', lambda e, h_=h_, kbc=kbc: e.activation(out=pt[:, h_ * 128:(h_ + 1) * 128],
                                                                                             in_=ps[:, h_ * 128:(h_ + 1) * 128], func=AF.Exp,
                                                                                             bias=kb[:, kbc:kbc + 1]),
                                              r=[pk], w=[ptk])
                                        pn, pnk = nextps()
                                        for h_ in range(2):
                                            ti = tix[(p_, r_, a_ + h_)]
                                            A('pe', lambda e, h_=h_, ti=ti: e.matmul(pn[0:64, 0:128], vtok[:, ti, :], pt[:, h_ * 128:(h_ + 1) * 128],
                                                                                      start=(h_ == 0), stop=(h_ == 1)),
                                              r=[ptk, ("vtok", (ti // 16) * 16)], w=[pnk])
                                        for h_ in range(2):
                                            A('pe', lambda e, h_=h_: e.matmul(pn[0:64, 128:256], ones_b[:, 0:64], pt[:, h_ * 128:(h_ + 1) * 128],
                                                                              start=(h_ == 0), stop=(h_ == 1)), r=[ptk], w=[pnk])
                                        lo = r_ + dil * 128 * a_
                                        accv = acc[:, :, lo:lo + 127 * dil + 1:dil]
                                        pnv = pn[0:64, 0:256].rearrange("p (a b) -> p a b", a=2)
                                        if p_ == 0:
                                            A('dve', lambda e, accv=accv, pnv=pnv: e.tensor_copy(out=accv, in_=pnv), r=[pnk], w=["acc"])
                                        else:
                                            A('dve', lambda e, accv=accv, pnv=pnv: e.tensor_tensor(out=accv, in0=accv, in1=pnv, op=ALU.add),
                                              r=[pnk, "acc"], w=["acc"])
                            A('dve', lambda e: e.reciprocal(out=rec[:], in_=acc[:, 1, :]), r=["acc"], w=["rec"])
                            A('dve', lambda e: e.tensor_tensor(out=yb[:], in0=acc[:, 0, :], in1=rec[:], op=ALU.mult), r=["acc", "rec"], w=["yb"])
                            A('sp', lambda e: e.dma_start(out=snd_y.ap()[sbk * 256 + 64:sbk * 256 + 128, :], in_=yb[:]), r=["yb"], w=[("sndy", 1, sbk)], dma="ybo")
                        sc.barrier()

                    if debug is not None and debug[0] == "s2":
                        A('pool', lambda e: e.dma_start(out=dbg, in_=snd_y.ap()), dma="dbg")
                        sc.barrier()
                        return nc
                    with ExitStack() as pm:
                        cq = sb(pm, "cq", [64, S], BF16)
                        ck = sb(pm, "ck", [64, S], BF16)
                        ktok = sb(pm, "ktok", [128, 64, 64], BF16)
                        hf = sb(pm, "hf", [128, 64, 64], F32)
                        cvt = [sb(pm, "cvt%d" % i, [64, T], F32) for i in range(2)]
                        for pc in range(4):
                            for wi, (Cx, dst) in enumerate(((C1, cq), (C2, ck))):
                                c0 = HALO + pc * T
                                tv = cvt[wi]
                                tk = ("cvt", wi)
                                A('dve', lambda e, Cx=Cx, tv=tv: e.tensor_scalar_mul(out=tv[:], in0=Cx[0:64, c0:c0 + T], scalar1=cw_sb[:, wi, 1:2]),
                                  w=[tk])
                                A('dve', lambda e, Cx=Cx, tv=tv: e.scalar_tensor_tensor(out=tv[:], in0=Cx[0:64, c0 - 1:c0 - 1 + T], scalar=cw_sb[:, wi, 0:1],
                                                                                        in1=tv[:], op0=ALU.mult, op1=ALU.add), r=[tk], w=[tk])
                                A('dve', lambda e, Cx=Cx, tv=tv: e.scalar_tensor_tensor(out=tv[:], in0=Cx[0:64, c0 + 1:c0 + 1 + T], scalar=cw_sb[:, wi, 2:3],
                                                                                        in1=tv[:], op0=ALU.mult, op1=ALU.add), r=[tk], w=[tk])
                                if wi == 0:
                                    A('act', lambda e, tv=tv, dst=dst: e.activation(out=dst[:, pc * T:(pc + 1) * T], in_=tv[:], func=AF.Silu),
                                      r=[tk], w=[("cq", pc)])
                                else:
                                    A('act', lambda e, tv=tv: e.activation(out=tv[:], in_=tv[:], func=AF.Silu), r=[tk], w=[tk])
                                    A('pool', lambda e, tv=tv, dst=dst: e.tensor_scalar_mul(out=dst[:, pc * T:(pc + 1) * T], in0=tv[:], scalar1=0.125),
                                      r=[tk], w=[("ck", pc)])
                        sc.barrier()
                        for g0 in range(0, 64, 16):
                            for n_ in range(16):
                                c_ = g0 + n_
                                A('pe', lambda e, n_=n_, c_=c_: e.transpose(pst[:, n_ * 64:(n_ + 1) * 64], ck[0:64, c_ * 128:(c_ + 1) * 128],
                                                                            ident_b[0:64, 0:64]), w=["pst"])
                            A('dve', lambda e, g0=g0: e.tensor_copy(out=ktok[:, g0:g0 + 16, :], in_=pst[:, 0:1024].rearrange("p (a b) -> p a b", b=64)),
                              r=["pst"], w=["ktok"])
                        sc.barrier()

                        NS = 3
                        av_ = [sb(pm, "av%d" % i, [128, 4], F32) for i in range(NS)]
                        eG = [sb(pm, "eG%d" % i, [128, 1], F32) for i in range(NS)]
                        eD = [sb(pm, "eD%d" % i, [128, 128], F32) for i in range(NS)]
                        ST = [sb(pm, "ST%d" % i, [128, 128], BF16) for i in range(NS)]
                        kw = [sb(pm, "kw%d" % i, [128, 64], BF16) for i in range(NS)]
                        tI = [sb(pm, "tI%d" % i, [128, 65], F32) for i in range(NS)]
                        tot = [sb(pm, "tot%d" % i, [128, 65], F32) for i in range(NS)]
                        dd = [sb(pm, "dd%d" % i, [128, 2], F32) for i in range(NS)]
                        Cf = [sb(pm, "Cf%d" % i, [64, 65], F32) for i in range(2)]
                        Cb = [sb(pm, "Cb%d" % i, [64, 65], BF16) for i in range(2)]
                        hs = [sb(pm, "hs%d" % i, [128, 64], F32) for i in range(2)]
                        hst = [sb(pm, "hst%d" % i, [128, 6], F32) for i in range(2)]
                        hmv = [sb(pm, "hmv%d" % i, [128, 2], F32) for i in range(2)]
                        hr = [sb(pm, "hr%d" % i, [128, 1], F32) for i in range(2)]
                        hn = [sb(pm, "hn%d" % i, [128, 64], F32) for i in range(2)]
                        yct = [sb(pm, "yct%d" % i, [128, 64], BF16) for i in range(2)]
                        ycT = [sb(pm, "ycT%d" % i, [64, 128], BF16) for i in range(2)]
                        for d_ in range(2):
                            A('dve', lambda e, d_=d_: e.memset(Cf[d_][:], 0.0), w=[("Cf", d_)])
                            A('dve', lambda e, d_=d_: e.memset(Cb[d_][:], 0.0), w=[("Cb", d_)])
                        stp = [0]
                        for it in range(64):
                            for d_ in range(2):
                                c_ = it if d_ == 0 else 63 - it
                                s3 = stp[0] % NS
                                stp[0] += 1
                                lf_ = gates[:, c_, 2 * d_ + 1:2 * d_ + 2]
                                li_ = gates[:, c_, 2 * d_:2 * d_ + 1]
                                psA, pkA = nextps()
                                A('pe', lambda e, d_=d_, lf_=lf_: e.matmul(psA[:, 0:1], tri[d_], lf_, start=True, stop=True), w=[pkA])
                                A('pe', lambda e, lf_=lf_: e.matmul(psA[:, 1:2], ones_f, lf_, start=True, stop=True), w=[pkA])
                                psB, pkB = nextps()
                                A('pe', lambda e, d_=d_, lf_=lf_: e.matmul(psB[:, 0:128], lf_.to_broadcast([128, 128]), tri[d_], start=True, stop=False),
                                  w=[pkB])
                                A('pe', lambda e, d_=d_: e.matmul(psB[:, 0:128], ident_f, mneg[d_], start=False, stop=True), w=[pkB])
                                avs = av_[s3]
                                A('dve', lambda e, avs=avs, li_=li_: e.tensor_tensor(out=avs[:, 0:1], in0=li_, in1=psA[:, 0:1], op=ALU.subtract),
                                  r=[pkA], w=[("av", s3)])
                                A('dve', lambda e, avs=avs: e.tensor_tensor(out=avs[:, 1:2], in0=psA[:, 1:2], in1=avs[:, 0:1], op=ALU.add),
                                  r=[pkA, ("av", s3)], w=[("av", s3)])
                                A('act', lambda e, avs=avs: e.activation(out=eD[s3][:], in_=psB[:, 0:128], func=AF.Exp, bias=avs[:, 0:1]),
                                  r=[pkB, ("av", s3)], w=[("eD", s3)])
                                A('act', lambda e, avs=avs: e.activation(out=avs[:, 2:3], in_=psA[:, 0:1], func=AF.Exp), r=[pkA, ("av", s3)], w=[("av", s3)])
                                A('act', lambda e, avs=avs: e.activation(out=avs[:, 3:4], in_=avs[:, 1:2], func=AF.Exp), r=[("av", s3)], w=[("av", s3)])
                                A('act', lambda e: e.activation(out=eG[s3][:], in_=psA[:, 1:2], func=AF.Exp), r=[pkA], w=[("eG", s3)])
                                psS, pkS = nextps()
                                A('pe', lambda e, c_=c_: e.matmul(psS[:, 0:128], ck[:, c_ * 128:(c_ + 1) * 128], cq[:, c_ * 128:(c_ + 1) * 128],
                                                                   start=True, stop=True), w=[pkS])
                                A('dve', lambda e: e.tensor_tensor(out=ST[s3][:], in0=psS[:, 0:128], in1=eD[s3][:], op=ALU.mult),
                                  r=[pkS, ("eD", s3)], w=[("ST", s3)])
                                A('dve', lambda e, c_=c_, avs=avs: e.tensor_scalar_mul(out=kw[s3][:], in0=ktok[:, c_, :], scalar1=avs[:, 3:4]),
                                  r=[("av", s3)], w=[("kw", s3)])
                                psC, pkC = nextps()
                                A('pe', lambda e, c_=c_: e.matmul(psC[0:64, 0:65], kw[s3][:], vaug[:, c_, :], start=True, stop=True),
                                  r=[("kw", s3)], w=[pkC])
                                psO, pkO = nextps()
                                A('pe', lambda e, c_=c_: e.matmul(psO[:, 0:65], ST[s3][:], vaug[:, c_, :], start=True, stop=True),
                                  r=[("ST", s3)], w=[pkO])
                                A('pe', lambda e, c_=c_, d_=d_: e.matmul(psO[:, 65:130], cq[:, c_ * 128:(c_ + 1) * 128], Cb[d_][:], start=True, stop=True),
                                  r=[("Cb", d_)], w=[pkO])
                                A('dve', lambda e, avs=avs: e.tensor_scalar_mul(out=tI[s3][:], in0=psO[:, 65:130], scalar1=avs[:, 2:3]),
                                  r=[pkO, ("av", s3)], w=[("tI", s3)])
                                A('dve', lambda e: e.tensor_tensor(out=tot[s3][:], in0=psO[:, 0:65], in1=tI[s3][:], op=ALU.add),
                                  r=[pkO, ("tI", s3)], w=[("tot", s3)])
                                A('dve', lambda e, d_=d_: e.scalar_tensor_tensor(out=Cf[d_][:], in0=Cf[d_][:], scalar=eG[s3][0:64, 0:1], in1=psC[0:64, 0:65],
                                                                                 op0=ALU.mult, op1=ALU.add), r=[pkC, ("eG", s3), ("Cf", d_)], w=[("Cf", d_)])
                                A('pool', lambda e, d_=d_: e.tensor_copy(out=Cb[d_][:], in_=Cf[d_][:]), r=[("Cf", d_)], w=[("Cb", d_)])
                                A('dve', lambda e: e.scalar_tensor_tensor(out=dd[s3][:, 0:1], in0=tot[s3][:, 64:65], scalar=-1.0, in1=tot[s3][:, 64:65],
                                                                          op0=ALU.mult, op1=ALU.max), r=[("tot", s3)], w=[("dd", s3)])
                                A('dve', lambda e: e.tensor_scalar_max(out=dd[s3][:, 0:1], in0=dd[s3][:, 0:1], scalar1=1.0),
                                  r=[("dd", s3)], w=[("dd", s3)])
                                A('dve', lambda e: e.reciprocal(out=dd[s3][:, 1:2], in_=dd[s3][:, 0:1]), r=[("dd", s3)], w=[("dd", s3)])
                                first = (c_ < 32) == (d_ == 0)
                                if first:
                                    A('dve', lambda e, c_=c_: e.tensor_scalar_mul(out=hf[:, c_, :], in0=tot[s3][:, 0:64], scalar1=dd[s3][:, 1:2]),
                                      r=[("tot", s3), ("dd", s3)], w=[("hf", c_)])
                                else:
                                    h2 = c_ % 2
                                    A('dve', lambda e, c_=c_: e.scalar_tensor_tensor(out=hs[h2][:], in0=tot[s3][:, 0:64], scalar=dd[s3][:, 1:2],
                                                                                      in1=hf[:, c_, :], op0=ALU.mult, op1=ALU.add),
                                      r=[("tot", s3), ("dd", s3), ("hf", c_)], w=[("hs", h2)])
                                    A('dve', lambda e: e.bn_stats(out=hst[h2][:], in_=hs[h2][:]), r=[("hs", h2)], w=[("hst", h2)])
                                    A('dve', lambda e: e.bn_aggr(out=hmv[h2][:], in_=hst[h2][:]), r=[("hst", h2)], w=[("hmv", h2)])
                                    A('act', lambda e: e.activation(out=hr[h2][:], in_=hmv[h2][:, 1:2], func=AF.Ln, bias=EPS), r=[("hmv", h2)], w=[("hr", h2)])
                                    A('act', lambda e: e.activation(out=hr[h2][:], in_=hr[h2][:], func=AF.Exp, scale=-0.5), r=[("hr", h2)], w=[("hr", h2)])
                                    A('dve', lambda e: e.tensor_scalar(out=hn[h2][:], in0=hs[h2][:], scalar1=hmv[h2][:, 0:1], scalar2=hr[h2][:, 0:1],
                                                                       op0=ALU.subtract, op1=ALU.mult),
                                      r=[("hs", h2), ("hmv", h2), ("hr", h2)], w=[("hn", h2)])
                                    A('pool', lambda e: e.tensor_tensor(out=hn[h2][:], in0=hn[h2][:], in1=ng_bc[:], op=ALU.mult),
                                      r=[("hn", h2)], w=[("hn", h2)])
                                    A('pool', lambda e, c_=c_: e.tensor_tensor(out=yct[h2][:], in0=hn[h2][:], in1=co_sb[:, c_, :], op=ALU.mult),
                                      r=[("hn", h2)], w=[("yct", h2)])
                                    A('pe', lambda e: e.transpose(pst[0:64, 0:128], yct[h2][:], ident_b), r=[("yct", h2)], w=["pst"])
                                    A('dve', lambda e: e.tensor_copy(out=ycT[h2][:], in_=pst[0:64, 0:128]), r=["pst"], w=[("ycT", h2)])
                                    A('sp', lambda e, c_=c_: e.dma_start(out=snd_y.ap()[(c_ // 16) * 256 + 128:(c_ // 16) * 256 + 192, (c_ % 16) * 128:(c_ % 16 + 1) * 128], in_=ycT[h2][:]),
                                      r=[("ycT", h2)], w=[("sndy", 2, c_)], dma=("yco", h2))
                        sc.barrier()
            if debug is not None and debug[0] == "y":
                A('pool', lambda e: e.dma_start(out=dbg, in_=snd_y.ap()), dma="dbg")
                sc.barrier()
                return nc

            for i in range(4):
                A('pool', lambda e, i=i: e.collective_compute("AllGather", ALU.bypass, replica_groups=G4,
                                                              ins=[snd_y.ap()[i * 256:(i + 1) * 256, :].opt()],
                                                              outs=[rcv_y.ap()[i * 1024:(i + 1) * 1024, :].opt()]),
                  dma="ccy", inc=1, cc=True)
            sc.barrier()
            if debug is not None and debug[0] == "c1":
                sc.barrier()
                A('sp', lambda e: e.dma_start(out=dbg, in_=cstb[:, 0, :]), dma="dbg")
                sc.barrier()
                return nc

            with ExitStack() as pd:
                mgh = sb(pd, "mgh", [128, 8, T], BF16)
                with ExitStack() as p1:
                    yall = sb(p1, "yall", [128, 8, T], BF16)
                    xbo = sb(p1, "xbo", [128, 8, T], BF16)
                    wbr_sb = sb(p1, "wbr_sb", [128, 8, D], BF16)
                    idx_sb = sb(p1, "idx_sb", [128, 8], I32)
                    bg_sb = sb(p1, "bg_sb", [128, 4, 8], F32)
                    wgs = [sb(p1, "wgs%d" % i, [128, 8, 4, 128], BF16) for i in range(2)]
                    gs = [sb(p1, "gs%d" % i, [128, 512], F32) for i in range(2)]
                    macc = [sb(p1, "macc%d" % i, [128, 512], F32) for i in range(2)]
                    tmpm = [sb(p1, "tmpm%d" % i, [128, 512], F32) for i in range(2)]
                    A('sp', lambda e: e.dma_start(out=idx_sb[:], in_=idxy), dma="setup")
                    with nc.allow_non_contiguous_dma(reason="tiny bias load"):
                        for n_ in range(4):
                            A('sp', lambda e, n_=n_: e.dma_start(out=bg_sb[:, n_, :], in_=bg[l][n_ * D:(n_ + 1) * D].rearrange("(dc p) -> p dc", p=128)), dma="setup")
                    A('sp', lambda e: e.dma_start(out=xbo[:], in_=snd_x.ap().rearrange("(c p) t -> p c t", p=128)), dma="setup")
                    for q in range(8):
                        src_ = wrows(l, q // 2, 1280 + (q % 2) * 128)
                        A('pool', lambda e, q=q, src_=src_: e.dma_start(out=wbr_sb[:, q, :], in_=src_), dma="setup")
                    sc.barrier()
                    if debug is not None and debug[0] == "c2":
                        sc.barrier()
                        A('sp', lambda e: e.dma_start(out=dbg, in_=cstb[:, 0, :]), dma="dbg")
                        sc.barrier()
                        return nc
                    for q in range(8):
                        A('pool', lambda e, q=q: e.indirect_dma_start(out=yall[:, q, :], out_offset=None, in_=rcv_y.ap(),
                                                                      in_offset=bass.IndirectOffsetOnAxis(ap=idx_sb[:, q:q + 1], axis=0)),
                          dma=("yall", q))
                    sc.barrier()
                    if debug is not None and debug[0] == "c3":
                        sc.barrier()
                        A('sp', lambda e: e.dma_start(out=dbg, in_=cstb[:, 0, :]), dma="dbg")
                        sc.barrier()
                        return nc
                    cnt = [0]
                    for dc in range(8):
                        ws = dc % 2
                        wkeys = [("wgs", ws, kc) for kc in range(8)]
                        for kc in range(8):
                            src_ = wrows(l, kc // 2, (kc % 2) * 512, 512)
                            A('pool', lambda e, kc=kc, src_=src_: e.dma_start(
                                out=wgs[ws][:, kc, :, :],
                                in_=src_.rearrange("(p a) b -> p a b", a=4)[:, :, dc * 128:(dc + 1) * 128]),
                              w=[("wgs", ws, kc)], dma=("wgs", ws))
                        for tg in range(4):
                            ms = (dc * 4 + tg) % 2
                            for n_ in range(4):
                                c2 = cnt[0] % 2
                                cnt[0] += 1
                                psG, pkG = nextps()
                                for kc in range(8):
                                    A('pe', lambda e, kc=kc, n_=n_: e.matmul(psG[:, 0:512], wgs[ws][:, kc, n_, :], xbo[:, kc, tg * 512:(tg + 1) * 512],
                                                                              start=(kc == 0), stop=(kc == 7)), r=wkeys, w=[pkG])
                                A('act', lambda e, n_=n_: e.activation(out=gs[c2][:], in_=psG[:, 0:512], func=AF.Sigmoid, bias=bg_sb[:, n_, dc:dc + 1]),
                                  r=[pkG], w=[("gs", c2)])
                                psP, pkP = nextps()
                                pb_ = (n_ % 2) * 64
                                for r_ in range(4):
                                    q = 2 * r_ + n_ // 2
                                    A('pe', lambda e, q=q, r_=r_, pb_=pb_: e.matmul(psP[:, 0:512], wbr_sb[pb_:pb_ + 64, q, dc * 128:(dc + 1) * 128],
                                                                                     yall[pb_:pb_ + 64, q, tg * 512:(tg + 1) * 512],
                                                                                     start=(r_ == 0), stop=(r_ == 3)), w=[pkP])
                                if n_ == 0:
                                    A('dve', lambda e: e.tensor_tensor(out=macc[ms][:], in0=gs[c2][:], in1=psP[:, 0:512], op=ALU.mult),
                                      r=[("gs", c2), pkP], w=[("macc", ms)])
                                else:
                                    A('dve', lambda e: e.tensor_tensor(out=tmpm[c2][:], in0=gs[c2][:], in1=psP[:, 0:512], op=ALU.mult),
                                      r=[("gs", c2), pkP], w=[("tmpm", c2)])
                                    if n_ < 3:
                                        A('pool', lambda e: e.tensor_tensor(out=macc[ms][:], in0=macc[ms][:], in1=tmpm[c2][:], op=ALU.add),
                                          r=[("tmpm", c2), ("macc", ms)], w=[("macc", ms)])
                                    else:
                                        A('pool', lambda e: e.tensor_tensor(out=mgh[:, dc, tg * 512:(tg + 1) * 512], in0=macc[ms][:], in1=tmpm[c2][:], op=ALU.add),
                                          r=[("tmpm", c2), ("macc", ms)], w=[("mg", dc, tg)])
                    sc.barrier()
                if debug is not None and debug[0] == "mg":
                    for c in range(8):
                        A('pool', lambda e, c=c: e.dma_start(out=dbg[c * 128:(c + 1) * 128, :], in_=mgh[:, c, :]), dma="dbg")
                    sc.barrier()
                    return nc

                z = sb(pd, "z", [128, 8, T], F32)
                x1b = sb(pd, "x1b", [128, 8, T], BF16)
                lnp = sb(pd, "lnp", [128, 4, 8], F32)
                gm = sb(pd, "gm", [16, T], F32)
                with nc.allow_non_contiguous_dma(reason="tiny ln param loads"):
                    for n_, src in enumerate((ln1g, ln1b, ln2g, ln2b)):
                        A('sp', lambda e, n_=n_, src=src: e.dma_start(out=lnp[:, n_, :], in_=src[l].rearrange("(c p) -> p c", p=128)), dma="setup")

                def layer_norm_fm(stack, gi, tag, also_bf16):
                    sq = [sb(stack, "sq%s%d" % (tag, i), [128, 512], F32) for i in range(2)]
                    mean_ = sb(stack, "mean" + tag, [128, 512], F32)
                    msq = sb(stack, "msq" + tag, [128, 512], F32)
                    rs_ = sb(stack, "rs" + tag, [128, 512], F32)
                    for tg in range(4):
                        tsl = slice(tg * 512, (tg + 1) * 512)
                        ps1, pk1 = nextps()
                        ps2, pk2 = nextps()
                        for c in range(8):
                            A('pe', lambda e, c=c: e.matmul(ps1[:, 0:512], ones_f, z[:, c, tsl], start=(c == 0), stop=(c == 7)),
                              r=[("z", c, tg)], w=[pk1])
                        for c in range(8):
                            A('act', lambda e, c=c: e.activation(out=sq[c % 2][:], in_=z[:, c, tsl], func=AF.Square),
                              r=[("z", c, tg)], w=[("sq", c % 2)])
                            A('pe', lambda e, c=c: e.matmul(ps2[:, 0:512], ones_f, sq[c % 2][:], start=(c == 0), stop=(c == 7)),
                              r=[("sq", c % 2)], w=[pk2])
                        A('dve', lambda e: e.tensor_scalar_mul(out=mean_[:], in0=ps1[:, 0:512], scalar1=1.0 / D), r=[pk1], w=["mean"])
                        A('dve', lambda e: e.tensor_tensor(out=msq[:], in0=mean_[:], in1=mean_[:], op=ALU.mult), r=["mean"], w=["msq"])
                        A('dve', lambda e: e.scalar_tensor_tensor(out=rs_[:], in0=ps2[:, 0:512], scalar=1.0 / D, in1=msq[:],
                                                                  op0=ALU.mult, op1=ALU.subtract), r=[pk2, "msq"], w=["rs"])
                        rstd_op(rs_[:], rs_[:], ["rs"], ["rs"])
                        for c in range(8):
                            zk = ("z", c, tg)
                            A('dve', lambda e, c=c: e.tensor_tensor(out=z[:, c, tsl], in0=z[:, c, tsl], in1=mean_[:], op=ALU.subtract),
                              r=[zk, "mean"], w=[zk])
                            A('pool', lambda e, c=c: e.tensor_tensor(out=z[:, c, tsl], in0=z[:, c, tsl], in1=rs_[:], op=ALU.mult),
                              r=[zk, "rs"], w=[zk])
                            A('act', lambda e, c=c: e.activation(out=z[:, c, tsl], in_=z[:, c, tsl], func=AF.Identity,
                                                                 bias=lnp[:, gi + 1, c:c + 1], scale=lnp[:, gi, c:c + 1]), r=[zk], w=[zk])
                            if also_bf16:
                                A('pool', lambda e, c=c: e.tensor_copy(out=x1b[:, c, tsl], in_=z[:, c, tsl]), r=[zk], w=[("x1b", c, tg)])

                with ExitStack() as p2:
                    wo_sb = sb(p2, "wo_sb", [128, 8, D], BF16)
                    xr = [sb(p2, "xr%d" % i, [128, 512], F32) for i in range(3)]
                    for kc in range(8):
                        src_ = wrows(l, kc // 2, 1024 + (kc % 2) * 128)
                        A('pool', lambda e, kc=kc, src_=src_: e.dma_start(out=wo_sb[:, kc, :], in_=src_), dma="setup")
                    sc.barrier()
                    n3 = [0]
                    for ec in range(8):
                        for tg in range(4):
                            s3 = n3[0] % 3
                            n3[0] += 1
                            A('sp', lambda e: e.dma_start(out=xr[s3][:], in_=xsrc[ec * 128:(ec + 1) * 128, tg * 512:(tg + 1) * 512]),
                              w=[("xr", s3)], dma=("xr", s3))
                            psM, pkM = nextps()
                            for kc in range(8):
                                A('pe', lambda e, kc=kc: e.matmul(psM[:, 0:512], wo_sb[:, kc, ec * 128:(ec + 1) * 128], mgh[:, kc, tg * 512:(tg + 1) * 512],
                                                                    start=(kc == 0), stop=(kc == 7)), w=[pkM])
                            A('dve', lambda e: e.scalar_tensor_tensor(out=z[:, ec, tg * 512:(tg + 1) * 512], in0=xr[s3][:], scalar=ALPHA,
                                                                      in1=psM[:, 0:512], op0=ALU.mult, op1=ALU.add),
                              r=[("xr", s3), pkM], w=[("z", ec, tg)])
                    sc.barrier()
                    layer_norm_fm(p2, 0, "a", True)
                    sc.barrier()
                if debug is not None and debug[0] == "x1":
                    for c in range(8):
                        A('sp', lambda e, c=c: e.dma_start(out=dbg[c * 128:(c + 1) * 128, :], in_=z[:, c, :]), dma="dbg")
                    sc.barrier()
                    return nc

                with ExitStack() as p3:
                    wr_sb = sb(p3, "wr_sb", [128, 8, 16], F32)
                    ex = sb(p3, "ex", [16, T], F32)
                    affo = sb(p3, "affo", [16, T], F32)
                    Aall = sb(p3, "Aall", [128, 1024], F32)
                    junk = sb(p3, "junk", [128, 1024], F32)
                    bs_ = sb(p3, "bs_", [128, 8], F32)
                    A('sp', lambda e: e.dma_start(out=wr_sb[:], in_=wr[l].rearrange("(c p) n -> p c n", p=128)), dma="setup")
                    sc.barrier()
                    for tg in range(4):
                        tsl = slice(tg * 512, (tg + 1) * 512)
                        psR, pkR = nextps()
                        for kc in range(8):
                            A('pe', lambda e, kc=kc: e.matmul(psR[0:16, 0:512], wr_sb[:, kc, :], z[:, kc, tsl], start=(kc == 0), stop=(kc == 7)), w=[pkR])
                        A('act', lambda e: e.activation(out=ex[:, tsl], in_=psR[0:16, 0:512], func=AF.Exp), r=[pkR], w=[("ex", tg)])
                        psE, pkE = nextps()
                        A('pe', lambda e: e.matmul(psE[0:16, 0:512], ones_f[0:16, 0:16], ex[:, tsl], start=True, stop=True), r=[("ex", tg)], w=[pkE])
                        A('dve', lambda e: e.reciprocal(out=affo[:, tsl], in_=psE[0:16, 0:512]), r=[pkE], w=[("affo", tg)])
                        A('dve', lambda e: e.tensor_tensor(out=affo[:, tsl], in0=affo[:, tsl], in1=ex[:, tsl], op=ALU.mult),
                          r=[("affo", tg), ("ex", tg)], w=[("affo", tg)])
                    sc.barrier()
                    A('sp', lambda e: e.dma_start(out=snd_a.ap(), in_=affo[:]), dma="setup")
                    sc.barrier()
                    A('pool', lambda e: e.collective_compute("AllGather", ALU.bypass, replica_groups=G4,
                                                             ins=[snd_a.ap().opt()], outs=[rcv_a.ap().opt()]), dma="cca", inc=1, cc=True)
                    sc.barrier()
                    for r_ in range(4):
                        for h_ in range(2):
                            g_ = 2 * r_ + h_
                            A('sp', lambda e, r_=r_, h_=h_, g_=g_: e.dma_start(out=Aall[g_ * 16:(g_ + 1) * 16, :],
                                                                               in_=rcv_a.ap()[r_ * 16:(r_ + 1) * 16, h_ * 1024:(h_ + 1) * 1024]), dma="setup")
                    A('dve', lambda e: e.memset(bs_[:, 0:1], 0.0), w=["bs"])
                    A('dve', lambda e: e.memset(bs_[:, 1:2], 1.5), r=["bs"], w=["bs"])
                    sc.barrier()
                    lo, hi, mid, cn, ge, d1, d2 = [bs_[:, i:i + 1] for i in range(7)]
                    for it in range(30):
                        A('dve', lambda e: e.tensor_scalar(out=mid, in0=lo, scalar1=hi, scalar2=0.5, op0=ALU.add, op1=ALU.mult), r=["bs"], w=["bs"])
                        A('dve', lambda e: e.tensor_scalar(out=junk[:], in0=Aall[:], scalar1=mid, scalar2=0.0, op0=ALU.is_ge, op1=ALU.add,
                                                           accum_out=cn), r=["bs"], w=["bs", "junk"])
                        psT, pkT = nextps()
                        A('pe', lambda e: e.matmul(psT[:, 0:1], blk, cn, start=True, stop=True), r=["bs"], w=[pkT])
                        A('dve', lambda e: e.tensor_single_scalar(out=ge, in_=psT[:, 0:1], scalar=1023.5, op=ALU.is_ge), r=[pkT, "bs"], w=["bs"])
                        A('dve', lambda e: e.tensor_tensor(out=d1, in0=mid, in1=lo, op=ALU.subtract), r=["bs"], w=["bs"])
                        A('dve', lambda e: e.tensor_tensor(out=d2, in0=hi, in1=mid, op=ALU.subtract), r=["bs"], w=["bs"])
                        A('dve', lambda e: e.scalar_tensor_tensor(out=lo, in0=d1, scalar=ge, in1=lo, op0=ALU.mult, op1=ALU.add), r=["bs"], w=["bs"])
                        A('dve', lambda e: e.scalar_tensor_tensor(out=hi, in0=d2, scalar=ge, in1=mid, op0=ALU.mult, op1=ALU.add), r=["bs"], w=["bs"])
                    A('dve', lambda e: e.scalar_tensor_tensor(out=gm[:], in0=affo[:], scalar=bs_[0:16, 0:1], in1=affo[:], op0=ALU.is_ge, op1=ALU.mult),
                      r=["bs"], w=["gm"])
                    for c in range(8):
                        A('pool', lambda e, c=c: e.tensor_scalar_mul(out=z[:, c, :], in0=z[:, c, :], scalar1=ALPHA), w=[("zz", c)])
                    sc.barrier()
                if debug is not None and debug[0] == "gm":
                    A('sp', lambda e: e.dma_start(out=dbg, in_=gm[:]), dma="dbg")
                    sc.barrier()
                    return nc

                with ExitStack() as p4:
                    wE = [sb(p4, "wE%d" % i, [128, 8, D], BF16) for i in range(3)]
                    G_sb = [sb(p4, "G_sb%d" % i, [128, 512], F32) for i in range(4)]
                    stmp = [sb(p4, "stmp%d" % i, [128, 512], F32) for i in range(2)]
                    stm2 = [sb(p4, "stm2%d" % i, [128, 512], F32) for i in range(2)]
                    hn_ = [0]

                    def load_expert_w(ex_, which):
                        erk, ebase = ex_ // 4, 1536 + (ex_ % 4) * 3072
                        for wh in which:
                            for kc in range(8):
                                src_ = wrows(l, erk, ebase + wh * 1024 + kc * 128)
                                A('pool', lambda e, wh=wh, kc=kc, src_=src_: e.dma_start(out=wE[wh][:, kc, :], in_=src_),
                                  w=[("wE", wh, kc)], dma=("wE", wh))

                    load_expert_w(0, (0, 1, 2))
                    wk13 = [("wE", wh, kc) for wh in range(2) for kc in range(8)]
                    w2k = [("wE", 2, kc) for kc in range(8)]
                    for ex_ in range(16):
                        for tg in range(4):
                            psQ, pkQ = nextps()
                            A('pe', lambda e, tg=tg: e.matmul(psQ[:, 0:512], ident_f[0:16, ex_:ex_ + 1].to_broadcast([16, 128]), gm[:, tg * 512:(tg + 1) * 512],
                                                               start=True, stop=True), w=[pkQ])
                            A('act', lambda e, tg=tg: e.copy(out=G_sb[tg][:], in_=psQ[:, 0:512]), r=[pkQ], w=[("G", tg)])
                        for ft in range(8):
                            for tg in range(4):
                                tsl = slice(tg * 512, (tg + 1) * 512)
                                h2 = hn_[0] % 2
                                hn_[0] += 1
                                ps1, pk1 = nextps()
                                ps3, pk3 = nextps()
                                for kc in range(8):
                                    A('pe', lambda e, kc=kc: e.matmul(ps1[:, 0:512], wE[0][:, kc, ft * 128:(ft + 1) * 128], x1b[:, kc, tsl], start=(kc == 0), stop=(kc == 7)),
                                      r=wk13, w=[pk1])
                                for kc in range(8):
                                    A('pe', lambda e, kc=kc: e.matmul(ps3[:, 0:512], wE[1][:, kc, ft * 128:(ft + 1) * 128], x1b[:, kc, tsl], start=(kc == 0), stop=(kc == 7)),
                                      r=wk13, w=[pk3])
                                A('act', lambda e: e.activation(out=stmp[h2][:], in_=ps1[:, 0:512], func=AF.Silu), r=[pk1], w=[("stmp", h2)])
                                A('dve', lambda e: e.tensor_tensor(out=stm2[h2][:], in0=stmp[h2][:], in1=ps3[:, 0:512], op=ALU.mult),
                                  r=[("stmp", h2), pk3], w=[("stm2", h2)])
                                A('dve', lambda e, tg=tg: e.tensor_tensor(out=mgh[:, ft, tsl], in0=stm2[h2][:], in1=G_sb[tg][:], op=ALU.mult),
                                  r=[("stm2", h2), ("G", tg)], w=[("hid", ft, tg)])
                        if ex_ + 1 < 16:
                            load_expert_w(ex_ + 1, (0, 1))
                        for dc in range(8):
                            for tg in range(4):
                                tsl = slice(tg * 512, (tg + 1) * 512)
                                psY, pkY = nextps()
                                for fc in range(8):
                                    A('pe', lambda e, fc=fc: e.matmul(psY[:, 0:512], wE[2][:, fc, dc * 128:(dc + 1) * 128], mgh[:, fc, tsl],
                                                                        start=(fc == 0), stop=(fc == 7)), r=w2k + [("hid", fc, tg)], w=[pkY])
                                A('dve', lambda e: e.tensor_tensor(out=z[:, dc, tsl], in0=z[:, dc, tsl], in1=psY[:, 0:512], op=ALU.add),
                                  r=[pkY], w=[("zz", dc, tg)])
                        if ex_ + 1 < 16:
                            load_expert_w(ex_ + 1, (2,))
                    sc.barrier()
                with ExitStack() as p4b:
                    layer_norm_fm(p4b, 2, "b", False)
                    sc.barrier()

                if not last:
                    for c in range(8):
                        A('sp', lambda e, c=c: e.dma_start(out=xres.ap()[c * 128:(c + 1) * 128, :], in_=z[:, c, :]), dma="setup")
                        A('pool', lambda e, c=c: e.dma_start(out=snd_x.ap()[c * 128:(c + 1) * 128, :], in_=z[:, c, :]), dma="setup")
                    sc.barrier()
                else:
                    with ExitStack() as p5:
                        xo = [sb(p5, "xo%d" % i, [128, D], F32) for i in range(2)]
                        for tt in range(16):
                            o2 = tt % 2
                            for hh in range(2):
                                psX, pkX = nextps()
                                for c4 in range(4):
                                    c = hh * 4 + c4
                                    A('pe', lambda e, c=c, c4=c4: e.transpose(psX[:, c4 * 128:(c4 + 1) * 128], z[:, c, tt * 128:(tt + 1) * 128], ident_f), w=[pkX])
                                A('act' if hh == 0 else 'dve',
                                  (lambda e, hh=hh: e.copy(out=xo[o2][:, hh * 512:(hh + 1) * 512], in_=psX[:, 0:512])) if hh == 0 else
                                  (lambda e, hh=hh: e.tensor_copy(out=xo[o2][:, hh * 512:(hh + 1) * 512], in_=psX[:, 0:512])),
                                  r=[pkX], w=[("xo", o2, hh)])
                            A('sp', lambda e: e.dma_start(out=out[tt * 128:(tt + 1) * 128, :], in_=xo[o2][:]),
                              r=[("xo", o2, 0), ("xo", o2, 1)], w=[("out", tt)], dma=("xo", o2))
                        sc.barrier()
        sc.barrier()
    return nc


def _t5_bucket(rel):
    rel = np.asarray(rel, dtype=np.int64)
    half, max_exact = 16, 8
    ret = np.where(rel > 0, half, 0)
    n = np.abs(rel)
    nf = np.maximum(n, 1).astype(np.float32)
    large = max_exact + (np.log(nf / np.float32(max_exact)) / np.float32(math.log(1024 / max_exact))
                         * np.float32(half - max_exact)).astype(np.int32)
    large = np.minimum(large, half - 1)
    return ret + np.where(n < max_exact, n, large)


def _static_tables():
    p = np.arange(128)
    ident = np.eye(128, dtype=np.float32)
    tri_f = (p[:, None] <= p[None, :]).astype(np.float32)
    tri_b = (p[:, None] >= p[None, :]).astype(np.float32)
    mneg_f = np.where(p[:, None] <= p[None, :], 0.0, NEG).astype(np.float32)
    mneg_b = np.where(p[:, None] >= p[None, :], 0.0, NEG).astype(np.float32)
    blk = ((p[:, None] % 16) == (p[None, :] % 16)).astype(np.float32)
    ones = np.ones((128, 128), np.float32)
    kbias = np.zeros((128, 128), np.float32)
    kbias[:64, 1] = NEG
    kbias[64:, 2] = NEG
    return np.ascontiguousarray(np.stack([ident, tri_f, tri_b, mneg_f, mneg_b, blk, ones, kbias]))


def _band_tables(win):
    def wmat(qbase, pbase):
        Q = qbase + np.arange(128)[:, None]
        P = pbase + np.arange(128)[None, :]
        lo = np.clip(P - win // 2, 0, S)
        hi = np.clip(P + win // 2, 0, S)
        cntw = (hi - lo).astype(np.float32)
        m = ((Q >= lo) & (Q < hi)).astype(np.float32) / cntw
        return (m - (Q == P).astype(np.float32)).astype(np.float32)
    mid = 4096
    return np.ascontiguousarray(np.stack([wmat(mid - 128, mid), wmat(mid, mid), wmat(mid + 128, mid),
                                          wmat(0, 0), wmat(S - 128, S - 128)]))


def _prep_inputs(inp, depth):
    L = depth
    f = lambda a: np.ascontiguousarray(np.asarray(a, dtype=np.float32))
    x = np.asarray(inp['x'], np.float32)
    w_in = np.asarray(inp['w_in'], np.float32)[:L]
    b_in = np.asarray(inp['b_in'], np.float32)[:L]
    consts = _static_tables()
    maps = []
    kq = np.arange(128)
    for c in range(NCORE):
        b, m = c // 4, c % 4
        d = {}
        d["xT"] = f(x[b, T * m:T * (m + 1), :].T)
        sl = lambda o: np.arange(o + 64 * m, o + 64 * m + 64)
        cols_f = np.concatenate([sl(0), sl(512), sl(1280), sl(768), sl(1536), sl(1024)])
        gv = 256 + np.concatenate([np.arange(64 * m, 64 * m + 64)] + [np.arange(64 * g, 64 * g + 64) for g in range(4) if g != m])
        gcols = np.array([2304 + m, 2312 + m, 2304 + 4 + m, 2312 + 4 + m])
        cols_t = np.concatenate([gv, sl(1792), sl(2048), gcols, sl(2320)])
        d["whf"] = f(w_in[:, :, cols_f])
        d["bhf"] = f(b_in[:, cols_f])
        d["wht"] = f(w_in[:, :, cols_t])
        d["bht"] = f(b_in[:, cols_t])
        d["fbias"] = f(np.asarray(inp['ml_fbias'])[:L, :, m])
        d["wsT"] = f(np.transpose(np.asarray(inp['gm_ws'])[:L, m], (0, 2, 1)))
        d["bsv"] = f(np.asarray(inp['gm_bs'])[:L, m, :])
        d["lng"] = f(np.asarray(inp['gm_ln_g'])[:L, 64 * m:64 * m + 64])
        rb = np.asarray(inp['rel_bias'], np.float32)
        bm = np.zeros((6, 128, 128), np.float32)
        for p_, dil in enumerate(DILS):
            for h_ in range(2):
                j = (-64 + 128 * h_ + kq[:, None]) - kq[None, :]
                valid = np.abs(j) <= 64
                vals = rb[_t5_bucket(dil * j), m]
                bm[p_ * 2 + h_] = np.where(valid, vals, np.float32(NEG))
        d["bmat"] = f(bm)
        mc = np.asarray(inp['ml_conv'], np.float32)[:L]
        d["cwq"] = f(np.transpose(mc[:, :, 64 * m:64 * m + 64], (0, 2, 1)))
        d["cwk"] = f(np.transpose(mc[:, :, 256 + 64 * m:256 + 64 * m + 64], (0, 2, 1)))
        d["ngv"] = f(np.asarray(inp['ml_norm_g'])[:L, 64 * m:64 * m + 64])
        d["bands"] = _band_tables((2, 4, 8, 16)[m])
        d["pw"] = f(np.asarray(inp['pool_w'])[:L, m])
        d["psc"] = f(np.asarray(inp['pool_scale'])[:L, 64 * m:64 * m + 64])
        pp, qq = np.meshgrid(np.arange(128), np.arange(8), indexing="ij")
        d["idxy"] = np.ascontiguousarray((m * 1024 + (qq // 2) * 256 + (qq % 2) * 128 + pp).astype(np.int32))
        wp = np.empty((L, WROWS, D), np.float32)
        w_br = np.asarray(inp['w_branch'], np.float32)
        pidx = np.arange(128)
        for l in range(L):
            for jj in range(2):
                kc = 2 * m + jj
                wp[l, jj * 512:(jj + 1) * 512] = w_in[l, kc * 128:(kc + 1) * 128, 2576:].reshape(512, D)
                wp[l, 1024 + jj * 128:1024 + (jj + 1) * 128] = inp['w_out'][l][kc * 128:(kc + 1) * 128]
                q = kc
                wp[l, 1280 + jj * 128:1280 + (jj + 1) * 128] = w_br[l, 2 * (q % 2) + pidx // 64, (q // 2) * 64 + pidx % 64, :]
            for k in range(4):
                e = 4 * m + k
                o = 1536 + k * 3072
                wp[l, o:o + 1024] = inp['w_e1'][l][e]
                wp[l, o + 1024:o + 2048] = inp['w_e3'][l][e]
                wp[l, o + 2048:o + 3072] = inp['w_e2'][l][e]
        d["wpack"] = wp
        d["bg"] = f(b_in[:, 2576:])
        d["ln1g"] = f(np.asarray(inp['ln1_g'])[:L])
        d["ln1b"] = f(np.asarray(inp['ln1_b'])[:L])
        d["wr"] = f(np.asarray(inp['w_router'])[:L])
        d["ln2g"] = f(np.asarray(inp['ln2_g'])[:L])
        d["ln2b"] = f(np.asarray(inp['ln2_b'])[:L])
        d["consts"] = consts
        maps.append(d)
    return maps


_NC_CACHE = {}


def kernel(**inputs):
    maps = _prep_inputs(inputs, DEPTH)
    if "nc" not in _NC_CACHE:
        _NC_CACHE["nc"] = build_program(DEPTH)
    res = run_bass_kernel_spmd(_NC_CACHE["nc"], maps, core_ids=list(range(NCORE)))
    outp = np.empty((2, S, D), np.float32)
    for c in range(NCORE):
        b, m = c // 4, c % 4
        outp[b, T * m:T * (m + 1), :] = np.asarray(res.results[c]["out"], np.float32)
    return outp
```
